# Optimizing a Trainium2 kernel written in Bass

```python
import math
import jax
import jax.numpy as jnp
from jax import lax
import numpy as np

D_MODEL = 1024
BATCH = 32
SEQ = 256
DEPTH = 2
DEC_BATCH = 4
DEC_SEQ = 1024
PAST_LEN = 256

GRID_W = 64
N_MIXERS = 2
N_ATTN_LAYERS = (DEPTH + 1) // 2
N_REC_LAYERS = DEPTH // 2
N_HEADS = 16
N_KV_HEADS = 4
HEAD_DIM = D_MODEL // N_HEADS
AXIS_DIM = HEAD_DIM // 2
ATTN_WIDTH = N_HEADS * HEAD_DIM
KV_WIDTH = N_KV_HEADS * HEAD_DIM
Q_BLOCK = 128
ROPE_THETA = 10000.0
REC_EXPAND = 128
N_REC_HEADS = D_MODEL // REC_EXPAND
REC_DK = REC_EXPAND
REC_DV = D_MODEL // N_REC_HEADS
REC_KEY_WIDTH = N_REC_HEADS * REC_DK
REC_VAL_WIDTH = N_REC_HEADS * REC_DV
CHUNK = 16
NORM_EPS = 1e-6
LN_EPS = 1e-5
DEEPNORM_ALPHA = (2.0 * DEPTH) ** 0.25
DEEPNORM_BETA = (8.0 * DEPTH) ** -0.25

kernel_name = "hybrid_gqa_hgrn2_diffusion_step"


def _rms_norm(x, g):
    xf = x.astype(jnp.float32)
    y = xf * lax.rsqrt(jnp.mean(xf * xf, axis=-1, keepdims=True) + NORM_EPS)
    return (y * g.astype(jnp.float32)).astype(x.dtype)


def _layer_norm(x, g, b):
    xf = x.astype(jnp.float32)
    mu = jnp.mean(xf, axis=-1, keepdims=True)
    var = jnp.mean(jnp.square(xf - mu), axis=-1, keepdims=True)
    y = (xf - mu) * lax.rsqrt(var + LN_EPS) * g.astype(jnp.float32) + b.astype(jnp.float32)
    return y.astype(x.dtype)


def _adaln(cond, w, b):
    mod = jax.nn.silu(cond.astype(jnp.float32)) @ w.astype(jnp.float32) + b.astype(jnp.float32)
    mod = mod.astype(cond.dtype)
    return jnp.split(mod, 3, axis=-1)


def _axial_rope(n_tokens):
    n_rows = n_tokens // GRID_W
    rows = jnp.repeat(jnp.arange(n_rows, dtype=jnp.float32), GRID_W)
    cols = jnp.tile(jnp.arange(GRID_W, dtype=jnp.float32), n_rows)
    inv_freq = 1.0 / (ROPE_THETA ** (jnp.arange(0, AXIS_DIM, 2, dtype=jnp.float32) / AXIS_DIM))
    ang_r = rows[:, None] * inv_freq[None, :]
    ang_c = cols[:, None] * inv_freq[None, :]
    ang = jnp.concatenate([ang_r, ang_r, ang_c, ang_c], axis=-1)
    return jnp.cos(ang)[:, None, :], jnp.sin(ang)[:, None, :]


def _apply_rope(x, cos, sin):
    xf = x.astype(jnp.float32)

    def rot_half(u):
        u1, u2 = jnp.split(u, 2, axis=-1)
        return jnp.concatenate([-u2, u1], axis=-1)

    xr, xc = jnp.split(xf, 2, axis=-1)
    rotated = jnp.concatenate([rot_half(xr), rot_half(xc)], axis=-1)
    return (xf * cos + rotated * sin).astype(x.dtype)


def _block_attention(q, k, v):
    B, Lq, H, d = q.shape
    G = H // N_KV_HEADS
    nb = Lq // Q_BLOCK
    qb = q.reshape(B, nb, Q_BLOCK, N_KV_HEADS, G, d).transpose(1, 0, 2, 3, 4, 5)
    kf = k.astype(jnp.float32)
    vf = v.astype(jnp.float32)
    scale = 1.0 / math.sqrt(d)

    def one_block(q_blk):
        s = jnp.einsum('bqkgd,bskd->bkgqs', q_blk.astype(jnp.float32), kf) * scale
        p = jax.nn.softmax(s, axis=-1)
        return jnp.einsum('bkgqs,bskd->bqkgd', p, vf).astype(q.dtype)

    out = lax.map(one_block, qb)
    return out.transpose(1, 0, 2, 3, 4, 5).reshape(B, Lq, H, d)


def _attn_project(h, w_in, q_gain, k_gain):
    B, L, _ = h.shape
    proj = h @ w_in
    q, k, v, g = jnp.split(proj, [ATTN_WIDTH, ATTN_WIDTH + KV_WIDTH, ATTN_WIDTH + 2 * KV_WIDTH], axis=-1)
    q = _rms_norm(q.reshape(B, L, N_HEADS, HEAD_DIM), q_gain)
    k = _rms_norm(k.reshape(B, L, N_KV_HEADS, HEAD_DIM), k_gain)
    v = v.reshape(B, L, N_KV_HEADS, HEAD_DIM)
    return q, k, v, g


def _attn_out(o, g, w_out):
    B, L = o.shape[0], o.shape[1]
    gated = o.reshape(B, L, ATTN_WIDTH).astype(jnp.float32) * jax.nn.silu(g.astype(jnp.float32))
    return gated.astype(o.dtype) @ w_out


def _attn_context(h, w_in, q_gain, k_gain, w_out):
    q, k, v, g = _attn_project(h, w_in, q_gain, k_gain)
    o = _block_attention(q, k, v)
    return _attn_out(o, g, w_out), k, v


def _attn_latent(h, ctx_k, ctx_v, w_in, q_gain, k_gain, w_out):
    q, k, v, g = _attn_project(h, w_in, q_gain, k_gain)
    cos, sin = _axial_rope(h.shape[1])
    q = _apply_rope(q, cos, sin)
    k = _apply_rope(k, cos, sin)
    keys = jnp.concatenate([ctx_k.astype(k.dtype), k], axis=1)
    vals = jnp.concatenate([ctx_v.astype(v.dtype), v], axis=1)
    o = _block_attention(q, keys, vals)
    return _attn_out(o, g, w_out)


def _gla_chunked(q, k, v, log_f, s0):
    B, H, L, dk = q.shape
    dv = v.shape[-1]
    n = L // CHUNK

    def chunks(u):
        return jnp.moveaxis(u.reshape(B, H, n, CHUNK, u.shape[-1]), 2, 0)

    b = jnp.cumsum(chunks(log_f), axis=3)
    mask = jnp.tril(jnp.ones((CHUNK, CHUNK), dtype=bool))[:, :, None]

    def step(S, xs):
        q_i, k_i, v_i, b_i = xs
        diff = b_i[:, :, :, None, :] - b_i[:, :, None, :, :]
        decay = jnp.where(mask, jnp.exp(jnp.minimum(diff, 0.0)), 0.0)
        scores = jnp.einsum('bhtk,bhsk,bhtsk->bhts', q_i, k_i, decay)
        b_last = b_i[:, :, -1, :]
        o_i = (jnp.einsum('bhts,bhsv->bhtv', scores, v_i)
               + jnp.einsum('bhtk,bhkv->bhtv', q_i * jnp.exp(b_i), S))
        S = (jnp.exp(b_last)[..., None] * S
             + jnp.einsum('bhsk,bhsv->bhkv', k_i * jnp.exp(b_last[:, :, None, :] - b_i), v_i))
        return S, o_i

    s_fin, o = lax.scan(step, s0.astype(jnp.float32), (chunks(q), chunks(k), chunks(v), b))
    o = jnp.moveaxis(o, 0, 2).reshape(B, H, L, dv)
    return o, s_fin


def _rec_mix(h, s0_fw, s0_bw, w_in, lb, norm_gain, w_out):
    B, L, _ = h.shape
    proj = h @ w_in
    q, f_fw, f_bw, i_in, g = jnp.split(
        proj, [REC_KEY_WIDTH, 2 * REC_KEY_WIDTH, 3 * REC_KEY_WIDTH, 3 * REC_KEY_WIDTH + REC_VAL_WIDTH], axis=-1)

    def heads(u, dh):
        return u.astype(jnp.float32).reshape(B, L, N_REC_HEADS, dh).transpose(0, 2, 1, 3)

    qh = jax.nn.silu(heads(q, REC_DK))
    vh = heads(i_in, REC_DV)
    lbf = lb.astype(jnp.float32)

    def gates(f_raw, lb_d):
        z = heads(f_raw, REC_DK)
        lb_h = lb_d.reshape(N_REC_HEADS, 1, REC_DK)
        forget = lb_h + (1.0 - lb_h) * jax.nn.sigmoid(z)
        inp = (1.0 - lb_h) * jax.nn.sigmoid(-z)
        return jnp.log(forget), inp

    lf_fw, k_fw = gates(f_fw, lbf[0])
    lf_bw, k_bw = gates(f_bw, lbf[1])
    o_fw, s_fw = _gla_chunked(qh, k_fw, vh, lf_fw, s0_fw)

    def flip(u):
        return jnp.flip(u, axis=2)

    o_bw, s_bw = _gla_chunked(flip(qh), flip(k_bw), flip(vh), flip(lf_bw), s0_bw)
    o = (o_fw + flip(o_bw)).transpose(0, 2, 1, 3)
    o = _rms_norm(o, norm_gain).reshape(B, L, REC_VAL_WIDTH)
    gated = (o * jax.nn.silu(g.astype(jnp.float32))).astype(h.dtype)
    return gated @ w_out, s_fw, s_bw


def setup_inputs(seed: int = 0) -> dict:
    key = jax.random.key(seed)
    ks = jax.random.split(key, 20)
    f32 = jnp.float32
    attn_in_w = 2 * ATTN_WIDTH + 2 * KV_WIDTH
    rec_in_w = 3 * REC_KEY_WIDTH + 2 * REC_VAL_WIDTH
    return {
        'x_prompt': jax.random.normal(ks[0], (BATCH, SEQ, D_MODEL), f32),
        'x_sample': jax.random.normal(ks[1], (DEC_BATCH, DEC_SEQ, D_MODEL), f32),
        'cache_k': jax.random.normal(ks[2], (DEC_BATCH, N_ATTN_LAYERS, PAST_LEN, N_KV_HEADS, HEAD_DIM), f32),
        'cache_v': jax.random.normal(ks[3], (DEC_BATCH, N_ATTN_LAYERS, PAST_LEN, N_KV_HEADS, HEAD_DIM), f32),
        'state_rec': 0.5 * jax.random.normal(ks[4], (DEC_BATCH, N_REC_LAYERS, 2, N_REC_HEADS, REC_DK, REC_DV), f32),
        'c': jax.random.normal(ks[5], (DEC_BATCH, D_MODEL), f32),
        'c_ctx': jax.random.normal(ks[6], (D_MODEL,), f32),
        'ada_w': 0.3 * D_MODEL ** -0.5 * jax.random.normal(ks[7], (DEPTH, D_MODEL, 3 * D_MODEL), f32),
        'ada_b': 0.02 * jax.random.normal(ks[8], (DEPTH, 3 * D_MODEL), f32),
        'attn_w_in': D_MODEL ** -0.5 * jax.random.normal(ks[9], (N_ATTN_LAYERS, D_MODEL, attn_in_w), f32),
        'attn_q_gain': 1.0 + 0.05 * jax.random.normal(ks[10], (N_ATTN_LAYERS, HEAD_DIM), f32),
        'attn_k_gain': 1.0 + 0.05 * jax.random.normal(ks[11], (N_ATTN_LAYERS, HEAD_DIM), f32),
        'attn_w_out': DEEPNORM_BETA * ATTN_WIDTH ** -0.5 * jax.random.normal(ks[12], (N_ATTN_LAYERS, ATTN_WIDTH, D_MODEL), f32),
        'rec_w_in': D_MODEL ** -0.5 * jax.random.normal(ks[13], (N_REC_LAYERS, D_MODEL, rec_in_w), f32),
        'rec_lower_bounds': 0.1 * jax.random.normal(ks[14], (DEPTH, 2, REC_KEY_WIDTH), f32),
        'rec_norm_gain': 1.0 + 0.05 * jax.random.normal(ks[15], (N_REC_LAYERS, REC_DV), f32),
        'rec_w_out': DEEPNORM_BETA * REC_VAL_WIDTH ** -0.5 * jax.random.normal(ks[16], (N_REC_LAYERS, REC_VAL_WIDTH, D_MODEL), f32),
        'ln_gain': 1.0 + 0.05 * jax.random.normal(ks[17], (DEPTH, D_MODEL), f32),
        'ln_bias': 0.02 * jax.random.normal(ks[18], (DEPTH, D_MODEL), f32),
    }


def reference(x_prompt, x_sample, cache_k, cache_v, state_rec, c, c_ctx,
              ada_w, ada_b, attn_w_in, attn_q_gain, attn_k_gain, attn_w_out,
              rec_w_in, rec_lower_bounds, rec_norm_gain, rec_w_out, ln_gain, ln_bias):
    lb_soft = jax.nn.softmax(rec_lower_bounds.astype(jnp.float32), axis=0)
    lb_all = jnp.cumsum(lb_soft, axis=0) - lb_soft[0]

    xp = x_prompt
    xs = x_sample
    new_k, new_v, new_s = [], [], []
    for i in range(DEPTH):
        j = i // N_MIXERS
        sh_p, sc_p, g_p = _adaln(c_ctx, ada_w[i], ada_b[i])
        sh_s, sc_s, g_s = _adaln(c, ada_w[i], ada_b[i])
        sh_s, sc_s, g_s = sh_s[:, None, :], sc_s[:, None, :], g_s[:, None, :]
        hp = xp * (1.0 + sc_p) + sh_p
        hs = xs * (1.0 + sc_s) + sh_s
        if i % N_MIXERS == 0:
            out_p, kp, vp = _attn_context(hp, attn_w_in[j], attn_q_gain[j], attn_k_gain[j], attn_w_out[j])
            new_k.append(kp.astype(x_prompt.dtype))
            new_v.append(vp.astype(x_prompt.dtype))
            out_s = _attn_latent(hs, cache_k[:, j], cache_v[:, j], attn_w_in[j], attn_q_gain[j],
                                 attn_k_gain[j], attn_w_out[j])
        else:
            zeros = jnp.zeros((xp.shape[0], N_REC_HEADS, REC_DK, REC_DV), jnp.float32)
            out_p, sp_fw, sp_bw = _rec_mix(hp, zeros, zeros, rec_w_in[j], lb_all[i],
                                           rec_norm_gain[j], rec_w_out[j])
            new_s.append(jnp.stack([sp_fw, sp_bw], axis=1).astype(x_prompt.dtype))
            out_s, _, _ = _rec_mix(hs, state_rec[:, j, 0], state_rec[:, j, 1], rec_w_in[j], lb_all[i],
                                   rec_norm_gain[j], rec_w_out[j])
        xp = _layer_norm(DEEPNORM_ALPHA * xp + g_p * out_p, ln_gain[i], ln_bias[i])
        xs = _layer_norm(DEEPNORM_ALPHA * xs + g_s * out_s, ln_gain[i], ln_bias[i])

    new_cache_k = jnp.stack(new_k, axis=1)
    new_cache_v = jnp.stack(new_v, axis=1)
    new_state_rec = jnp.stack(new_s, axis=1)
    return (xp, xs, new_cache_k, new_cache_v, new_state_rec)
```

```python
import contextlib
import numpy as np
import concourse.bass as bass
import concourse.mybir as mybir
from concourse.bass_utils import run_bass_kernel_spmd

F32 = mybir.dt.float32
BF16 = mybir.dt.bfloat16
AF = mybir.ActivationFunctionType
ALU = mybir.AluOpType
AX = mybir.AxisListType

D = 1024
NT = 12
NTOK = NT * 128
ALPHA = 4.0 ** 0.25
NORM_EPS = 1e-6
LN_EPS = 1e-5
SEQS = [(0, 2, False), (2, 2, False), (4, 8, True)]
SI_S = 2
PB = (0, 4, False)
DEBUG = False
import os
SKIP = os.environ.get('KSKIP', '')


class Dep:
    __slots__ = ("w", "r", "name", "excl")

    def __init__(self, name="", excl=False):
        self.excl = excl
        self.w = {}
        self.r = {}
        self.name = name


class Sig:
    def __init__(self, name, sem, h=None):
        self.name = name
        self.sem = sem
        self.h = h
        self.n = 0
        self.seen = {}


class KB:
    def __init__(self, nc, es):
        self.nc = nc
        self.es = es
        self.root = es
        self.nsem = 0
        self.pe = self.eng("pe", nc.tensor)
        self.act = self.eng("act", nc.scalar)
        self.dve = self.eng("dve", nc.vector)
        self.pool = self.eng("pool", nc.gpsimd)
        self.sp = self.eng("sp", nc.sync)
        self.dsems = []
        self.uid = 0

    def newsem(self, name):
        self.nsem += 1
        return self.root.enter_context(self.nc.semaphore(f"{name}_{self.nsem}"))

    def eng(self, name, h):
        return Sig(name, self.newsem(name), h)

    def dsem(self, name):
        s = Sig(name, self.newsem(name))
        self.dsems.append(s)
        return s

    def sb(self, name, shape, dt):
        self.uid += 1
        return self.es.enter_context(self.nc.sbuf_tensor(f"{name}_{self.uid}", list(shape), dt))

    def _waits(self, e, reads, writes):
        need = {}
        for d in reads:
            for s, v in d.w.items():
                need[s] = max(need.get(s, 0), v)
        for d in writes:
            for s, v in d.w.items():
                need[s] = max(need.get(s, 0), v)
            for s, v in d.r.items():
                if s is e:
                    continue
                need[s] = max(need.get(s, 0), v)
        if e is self.pe:
            need.pop(e, None)
        for s, v in need.items():
            if e.seen.get(s, 0) >= v:
                continue
            e.h.wait_ge(s.sem, v)
            e.seen[s] = v

    def op(self, e, fn, reads=(), writes=(), signal=True):
        ex = [d for d in reads if d.excl]
        if ex:
            writes = list(writes) + ex
        self._waits(e, reads, writes)
        inst = fn()
        if signal:
            e.n += 1
            inst.then_inc(e.sem, 1)
            val = e.n
        else:
            val = e.n + 1
        for d in reads:
            d.r[e] = max(d.r.get(e, 0), val)
        for d in writes:
            d.w[e] = max(d.w.get(e, 0), val)
        return inst

    def dma(self, q, ds, out, in_, reads=(), writes=(), **kw):
        self._waits_dma(q, ds, reads, writes)
        inst = q.h.dma_start(out=out, in_=in_, **kw)
        ds.n += 16
        inst.then_inc(ds.sem, 16)
        for d in reads:
            d.r[ds] = ds.n
        for d in writes:
            d.w[ds] = ds.n
        return inst

    def _waits_dma(self, q, ds, reads, writes):
        need = {}
        for d in reads:
            for s, v in d.w.items():
                need[s] = max(need.get(s, 0), v)
        for d in writes:
            for s, v in d.w.items():
                if s is not ds:
                    need[s] = max(need.get(s, 0), v)
            for s, v in d.r.items():
                need[s] = max(need.get(s, 0), v)
        for s, v in need.items():
            if q.seen.get(s, 0) >= v:
                continue
            q.h.wait_ge(s.sem, v)
            q.seen[s] = v

    def barrier(self):
        sigs = [self.pe, self.act, self.dve, self.pool] + self.dsems
        for e in (self.pe, self.act, self.dve, self.pool, self.sp):
            for s_ in sigs:
                if s_ is e or s_.n == 0:
                    continue
                if e.seen.get(s_, 0) >= s_.n:
                    continue
                e.h.wait_ge(s_.sem, s_.n)
                e.seen[s_] = s_.n

    def finish(self):
        for s in self.dsems:
            if s.n > 0:
                self.sp.h.wait_ge(s.sem, s.n)
        for e in (self.pe, self.act, self.dve, self.pool):
            if e.n > 0:
                self.sp.h.wait_ge(e.sem, e.n)


def build_program(n_layers=2, stg=99):
    nc = bass.Bass("TRN2", target_bir_lowering=False, dynamic_dma_scratch_size=8192)

    def din(name, shape, dt=F32):
        return nc.dram_tensor(name, list(shape), dt, kind="ExternalInput").ap()

    def dout(name, shape, dt=F32):
        return nc.dram_tensor(name, list(shape), dt, kind="ExternalOutput").ap()

    x_all = din("x_all", [NTOK, D])
    cache_k = din("cache_k", [256, 4, 64])
    cache_v = din("cache_v", [256, 4, 64])
    state0 = din("state0", [2, 8, 128, 128])
    cond = din("condT", [128, 8, 2])
    ada_w = din("ada_w", [2, D, 3 * D])
    ada_b = din("ada_b", [2, 3 * D])
    adabT_in = din("adabT", [128, 2, 24])
    attn_w_in = din("attn_w_in", [D, 2560])
    q_gain = din("q_gain", [1, 64])
    k_gain = din("k_gain", [1, 64])
    attn_w_out = din("attn_w_out", [D, D])
    rec_w_in = din("rec_w_in", [D, 5120])
    rec_lb = din("lbr", [128, 2, 2, 8])
    rec_gain = din("rec_gain", [1, 128])
    rec_w_out = din("rec_w_out", [D, D])
    ln_gain = din("ln_gain", [2, D])
    ln_bias = din("ln_bias", [2, D])
    c_ident = din("c_ident", [128, 128])
    c_rope = din("c_rope", [128, 2, 8, 64])
    c_wc = din("c_wc", [128, 2, 130])
    c_mask = din("c_mask", [128, 2, 128])
    c_sel = din("c_sel", [2, 2, 128])
    c_bias = din("c_bias", [128, 8, 10])
    c_rf = din("c_rf", [128, 8, 2])

    y_out = dout("y_out", [NTOK, D])
    nk_out = dout("nk_out", [2, 256, 4, 64])
    nv_out = dout("nv_out", [2, 256, 4, 64])
    ns_out = dout("ns_out", [2, 2, 8, 128, 128])
    nk1_out = dout("nk1_out", [1024, 4, 64])
    nv1_out = dout("nv1_out", [1024, 4, 64])
    ns1_out = dout("ns1_out", [4, 2, 8, 128, 128])
    if DEBUG:
        x1_dram = dout("x1_dram", [NTOK, D])
    else:
        x1_dram = nc.dram_tensor("x1_dram", [NTOK, D], F32, kind="Internal").ap()

    es = contextlib.ExitStack()
    with es:
        kb = KB(nc, es)
        PE, ACT, DVE, POOL, SP = kb.pe, kb.act, kb.dve, kb.pool, kb.sp
        T, V, S, G = nc.tensor, nc.vector, nc.scalar, nc.gpsimd

        banks = []
        bdeps = []
        for i in range(8):
            banks.append(es.enter_context(nc.psum_tensor(f"bank{i}", [128, 512], F32)))
            bdeps.append(Dep(f"bank{i}", excl=True))

        hT = kb.sb("hT", [128, 8, NTOK], BF16)
        hT_d = [Dep(f"hT{i}") for i in range(NT)]
        gT = kb.sb("gT", [128, 8, NTOK], BF16)
        gT_d = [Dep(f"gT{i}") for i in range(NT)]
        Wsl = [kb.sb("W", [128, 8, 640], BF16) for _ in range(2)]
        Wsl_d = [Dep("W0"), Dep("W1")]
        Wsl_s = [kb.dsem("W0"), kb.dsem("W1")]
        wout_d = Dep("wout")
        wout_s = kb.dsem("wout")
        xt_d = [Dep(f"xt{i}") for i in range(4)]
        xt_s = [kb.dsem(f"xt{i}") for i in range(4)]
        vt_d = [Dep("vt0"), Dep("vt1")]
        yt_d = [Dep(f"yt{i}") for i in range(4)]
        yt_s = [kb.dsem(f"yt{i}") for i in range(4)]
        gate_rep_d = [Dep("gr0"), Dep("gr1")]
        lnrep_d = Dep("lnrep")
        lnrep_s = kb.dsem("lnrep")
        cs = kb.dsem("const")
        cd = Dep("const")
        ident_f = kb.sb("ident_f", [128, 128], F32)
        ident_b = kb.sb("ident_b", [128, 128], BF16)
        wc = kb.sb("wc", [128, 2, 130], F32)
        maskt = kb.sb("mask", [128, 2, 128], F32)
        gain5 = kb.sb("gain5", [128, 5, 64], F32)
        bias_t = kb.sb("bias_t", [128, 8, 10], F32)
        rf_t = kb.sb("rf_t", [128, 8, 2], F32)
        rgain = kb.sb("rgain", [128, 128], F32)
        condT = kb.sb("condT", [128, 8, 2], F32)
        scT = kb.sb("scT", [128, 8, 2], F32)
        adabT = kb.sb("adabT", [128, 2, 24], F32)
        modT = kb.sb("modT", [128, 2, 16, 2], F32)
        modT_d = Dep("modT")
        gate_dram = nc.dram_tensor("gate_dram", [2, 2, D], F32, kind="Internal").ap()
        gate_s = kb.dsem("gate")
        gate_rep_s = [kb.dsem("grep0"), kb.dsem("grep1")]
        gate_row_d = Dep("gate_row")
        lbr = kb.sb("lbr", [128, 2, 2, 8], F32)
        lbm1 = kb.sb("lbm1", [128, 2, 8], F32)
        oml = kb.sb("oml", [128, 2, 8], F32)
        lb_d = Dep("lb")

        def cload(dst, src, **kw):
            kb.dma(SP, cs, dst, src, writes=[cd], **kw)

        cload(ident_f[:], c_ident)
        cload(bias_t[:], c_bias)
        cload(rf_t[:], c_rf)
        cload(wc[:], c_wc)
        cload(maskt[:], c_mask)
        for h in range(4):
            cload(gain5[:, h, :], q_gain[0:1, :].to_broadcast([128, 64]))
        cload(gain5[:, 4, :], k_gain[0:1, :].to_broadcast([128, 64]))
        cload(rgain[:], rec_gain[0:1, :].to_broadcast([128, 128]))
        cload(condT[:], cond)
        cload(adabT[:], adabT_in)
        cload(lbr[:], rec_lb)

        cd2 = Dep("const2")
        kb.op(ACT, lambda: S.copy(out=ident_b[:], in_=ident_f[:]), reads=[cd], writes=[cd2])
        kb.op(ACT, lambda: S.mul(out=gain5[:], in_=gain5[:], mul=8.0), reads=[cd], writes=[cd2])
        kb.op(ACT, lambda: S.activation(out=scT[:], in_=condT[:], func=AF.Silu), reads=[cd], writes=[cd2])
        kb.op(DVE, lambda: V.tensor_tensor(out=lbm1[:], in0=lbr[:, 1, :, :], in1=lbr[:, 0, :, :], op=ALU.subtract),
              reads=[cd], writes=[lb_d])
        kb.op(ACT, lambda: S.activation(out=oml[:], in_=lbm1[:], func=AF.Sigmoid, scale=-1.0),
              reads=[lb_d], writes=[lb_d])
        kb.op(DVE, lambda: V.tensor_scalar(out=lbm1[:], in0=oml[:], scalar1=-1.0, scalar2=None, op0=ALU.mult),
              reads=[lb_d], writes=[lb_d])

        with contextlib.ExitStack() as es_p:
            kb.es = es_p
            stage = [kb.sb("adastage", [128, 8, 512], F32) for _ in range(2)]
            stage_d = [Dep("st0"), Dep("st1")]
            stage_s = [kb.dsem("st0"), kb.dsem("st1")]
            rowtmp = [kb.sb("rowtmp", [2, 512], F32) for _ in range(2)]
            adab_row = kb.sb("adab_row", [2, 2, D], F32)
            grow = kb.sb("grow", [2, 2, D], F32)
            grow_d = Dep("grow")
            adab_d = Dep("adab_row")
            adab_s = kb.dsem("adab_row")
            rowtmp_d = [Dep("rowtmp0"), Dep("rowtmp1")]
            kb.es = es
            for r in range(2):
                kb.dma(SP, adab_s, adab_row[r:r + 1, :, :], ada_b[:, 2 * D:3 * D].rearrange("(o l) n -> o l n", o=1),
                       writes=[adab_d])
            it = 0
            for l in range(2):
                aw = ada_w[l].rearrange("(k p) n -> p k n", p=128)
                for j in range(6):
                    sl = it % 2
                    it += 1
                    kb.dma(SP, stage_s[sl], stage[sl][:], aw[:, :, j * 512:(j + 1) * 512], writes=[stage_d[sl]])
                    if j < 4:
                        bk = 0
                        rsl = it % 2
                        for k in range(8):
                            kb.op(PE, lambda k=k, sl=sl: T.matmul(
                                banks[2][0:2, :], lhsT=scT[:, k, :], rhs=stage[sl][:, k, :],
                                start=(k == 0), stop=(k == 7)),
                                reads=[stage_d[sl], cd2], writes=[bdeps[2]], signal=(k == 7))
                        kb.op(ACT, lambda rsl=rsl: S.copy(out=rowtmp[rsl][:, :], in_=banks[2][0:2, :]),
                              reads=[bdeps[2]], writes=[rowtmp_d[rsl]])
                        for sub in range(4):
                            cch = j * 4 + sub
                            kb.op(PE, lambda sub=sub, cch=cch, rsl=rsl: T.transpose(
                                out=banks[bk][:, cch * 2:cch * 2 + 2], in_=rowtmp[rsl][:, sub * 128:(sub + 1) * 128],
                                identity=ident_f[0:2, 0:2]), reads=[rowtmp_d[rsl], cd], writes=[bdeps[bk]],
                                signal=(sub == 3))
                        if j == 3:
                            kb.op(DVE, lambda l=l: V.tensor_tensor(
                                out=modT[:, l, :, :], in0=banks[0][:, 0:32].rearrange("p (c o) -> p c o", o=2),
                                in1=adabT[:, l, 0:16].unsqueeze(2).to_broadcast([128, 16, 2]), op=ALU.add),
                                reads=[bdeps[0], cd], writes=[modT_d])
                            kb.op(DVE, lambda l=l: V.tensor_scalar(
                                out=modT[:, l, 8:16, :], in0=modT[:, l, 8:16, :], scalar1=1.0, scalar2=None,
                                op0=ALU.add), reads=[modT_d], writes=[modT_d])
                    else:
                        bk = 1
                        nb = j - 4
                        for k in range(8):
                            kb.op(PE, lambda k=k, sl=sl: T.matmul(
                                banks[bk][0:2, :], lhsT=scT[:, k, :], rhs=stage[sl][:, k, :],
                                start=(k == 0), stop=(k == 7)),
                                reads=[stage_d[sl], cd2], writes=[bdeps[bk]], signal=(k == 7))
                        kb.op(DVE, lambda l=l, nb=nb: V.tensor_tensor(
                            out=grow[:, l, nb * 512:(nb + 1) * 512], in0=banks[1][0:2, :],
                            in1=adab_row[:, l, nb * 512:(nb + 1) * 512], op=ALU.add),
                            reads=[bdeps[1], adab_d], writes=[grow_d])
                        if l == 1 and nb == 1:
                            kb.dma(SP, gate_s, gate_dram, grow[:], reads=[grow_d], writes=[gate_row_d])
            kb.barrier()

        ev_rr = [0]

        def make_hT(src_ap, src_dep, i, c, l, pb, all_act=False):
            for half in range(2):
                bk = pb[half]
                for q in range(4):
                    k = half * 4 + q
                    kb.op(PE, lambda k=k, q=q, bk=bk: T.transpose(
                        out=banks[bk][:, q * 128:(q + 1) * 128], in_=src_ap[:, k * 128:(k + 1) * 128],
                        identity=ident_f[:]), reads=[src_dep, cd], writes=[bdeps[bk]], signal=(q == 3))
                for q in range(4):
                    k = half * 4 + q
                    if half == 0 or all_act:
                        kb.op(ACT, lambda k=k, q=q, bk=bk: S.activation(
                            out=hT[:, k, i * 128:(i + 1) * 128], in_=banks[bk][:, q * 128:(q + 1) * 128],
                            func=AF.Identity, scale=modT[:, l, 8 + k, c:c + 1], bias=modT[:, l, k, c:c + 1]),
                            reads=[bdeps[bk], modT_d], writes=[hT_d[i]])
                    else:
                        kb.op(DVE, lambda k=k, q=q, bk=bk: V.tensor_scalar(
                            out=hT[:, k, i * 128:(i + 1) * 128], in0=banks[bk][:, q * 128:(q + 1) * 128],
                            scalar1=modT[:, l, 8 + k, c:c + 1], scalar2=modT[:, l, k, c:c + 1],
                            op0=ALU.mult, op1=ALU.add),
                            reads=[bdeps[bk], modT_d], writes=[hT_d[i]])

        def load_w(slot, pieces):
            for (c0, ncol, src) in pieces:
                kb.dma(POOL, Wsl_s[slot], Wsl[slot][:, :, c0:c0 + ncol],
                       src.rearrange("(k p) n -> p k n", p=128), writes=[Wsl_d[slot]])

        def epilogue(l, w_out_ap, res_src, last):
            with contextlib.ExitStack() as es_e:
                kb.es = es_e
                xt = [kb.sb("xt", [128, D], F32) for _ in range(4)]
                wout = kb.sb("wout", [128, 8, D], BF16)
                vt = [kb.sb("vt", [128, D], F32) for _ in range(2)]
                yt = [kb.sb("yt", [128, D], F32) for _ in range(4)]
                gate_rep = [kb.sb("gate_rep", [128, D], F32) for _ in range(2)]
                lnrep = kb.sb("lnrep", [128, 2, D], F32)
                _epilogue(l, w_out_ap, res_src, last, xt, vt, yt, gate_rep, lnrep, wout)
                kb.barrier()
            kb.es = es

        def _epilogue(l, w_out_ap, res_src, last, xt, vt, yt, gate_rep, lnrep, wout):
            kb.dma(POOL, wout_s, wout[:], w_out_ap.rearrange("(k p) n -> p k n", p=128), writes=[wout_d])
            kb.dma(SP, lnrep_s, lnrep[:, 0, :], ln_gain[l:l + 1, :].to_broadcast([128, D]), writes=[lnrep_d])
            kb.dma(SP, lnrep_s, lnrep[:, 1, :], ln_bias[l:l + 1, :].to_broadcast([128, D]), writes=[lnrep_d])
            for c in range(2):
                kb.dma(SP, gate_rep_s[c], gate_rep[c][:], gate_dram[c:c + 1, l, :].to_broadcast([128, D]),
                       reads=[gate_row_d], writes=[gate_rep_d[c]])
            stats = [kb.sb("stats", [128, 2, 6], F32) for _ in range(2)]
            mv = [kb.sb("mv", [128, 2], F32) for _ in range(2)]
            rstd = [kb.sb("rstd", [128, 1], F32) for _ in range(2)]
            nmr = [kb.sb("nmr", [128, 1], F32) for _ in range(2)]
            st_d = [Dep("stats0"), Dep("stats1")]
            def xload(i):
                if i < NT:
                    kb.dma(SP, xt_s[i % 4], xt[i % 4][:], res_src[i * 128:(i + 1) * 128, :], writes=[xt_d[i % 4]])

            def stage1(i):
                c = 0 if i < 4 else 1
                xs = i % 4
                p = i % 2
                xload(i + 3)
                for k in range(8):
                    for nb in range(2):
                        bk = (i % 2) * 2 + nb
                        kb.op(PE, lambda: T.matmul(
                            banks[bk][:, :], lhsT=gT[:, k, i * 128:(i + 1) * 128],
                            rhs=wout[:, k, nb * 512:(nb + 1) * 512], start=(k == 0), stop=(k == 7)),
                            reads=[gT_d[i], wout_d], writes=[bdeps[bk]], signal=(k == 7))
                for nb in range(2):
                    bk = (i % 2) * 2 + nb
                    kb.op(DVE, lambda: V.tensor_tensor(
                        out=vt[p][:, nb * 512:(nb + 1) * 512], in0=banks[bk][:, :],
                        in1=gate_rep[c][:, nb * 512:(nb + 1) * 512], op=ALU.mult),
                        reads=[bdeps[bk], gate_rep_d[c]], writes=[vt_d[p]])
                kb.op(DVE, lambda: V.scalar_tensor_tensor(
                    out=vt[p][:], in0=xt[xs][:], scalar=ALPHA, in1=vt[p][:], op0=ALU.mult, op1=ALU.add),
                    reads=[xt_d[xs], vt_d[p]], writes=[vt_d[p]])
                kb.op(DVE, lambda: V.bn_stats(out=stats[p][:, 0, :], in_=vt[p][:, 0:512]), reads=[vt_d[p]], writes=[st_d[p]])
                kb.op(DVE, lambda: V.bn_stats(out=stats[p][:, 1, :], in_=vt[p][:, 512:1024]), reads=[vt_d[p]], writes=[st_d[p]])
                kb.op(DVE, lambda: V.bn_aggr(out=mv[p][:], in_=stats[p][:]), reads=[st_d[p]], writes=[st_d[p]])
                kb.op(DVE, lambda: V.tensor_scalar(out=rstd[p][:], in0=mv[p][:, 1:2], scalar1=LN_EPS, scalar2=None,
                                                   op0=ALU.add), reads=[st_d[p]], writes=[st_d[p]])
                kb.op(ACT, lambda: S.sqrt(out=rstd[p][:], in_=rstd[p][:]), reads=[st_d[p]], writes=[st_d[p]])

            def stage2(i):
                c = 0 if i < 4 else 1
                ys = i % 4
                p = i % 2
                kb.op(DVE, lambda: V.reciprocal(out=rstd[p][:], in_=rstd[p][:]), reads=[st_d[p]], writes=[st_d[p]])
                kb.op(DVE, lambda: V.scalar_tensor_tensor(out=nmr[p][:], in0=mv[p][:, 0:1], scalar=-1.0, in1=rstd[p][:],
                                                          op0=ALU.mult, op1=ALU.mult), reads=[st_d[p]], writes=[st_d[p]])
                kb.op(ACT, lambda: S.activation(out=yt[ys][:], in_=vt[p][:], func=AF.Identity,
                                                scale=rstd[p][:, 0:1], bias=nmr[p][:, 0:1]),
                      reads=[vt_d[p], st_d[p]], writes=[yt_d[ys]])
                kb.op(POOL, lambda: G.tensor_tensor(out=yt[ys][:], in0=yt[ys][:], in1=lnrep[:, 0, :], op=ALU.mult),
                      reads=[yt_d[ys], lnrep_d], writes=[yt_d[ys]])
                kb.op(POOL, lambda: G.tensor_tensor(out=yt[ys][:], in0=yt[ys][:], in1=lnrep[:, 1, :], op=ALU.add),
                      reads=[yt_d[ys], lnrep_d], writes=[yt_d[ys]])
                if last:
                    kb.dma(SP, yt_s[ys], y_out[i * 128:(i + 1) * 128, :], yt[ys][:], reads=[yt_d[ys]])
                else:
                    kb.dma(SP, yt_s[ys], x1_dram[i * 128:(i + 1) * 128, :], yt[ys][:], reads=[yt_d[ys]])

            def stage3(i):
                if not last and 0 <= i < NT:
                    make_hT(yt[i % 4], yt_d[i % 4], i, 0 if i < 4 else 1, l + 1, (6, 7) if i % 2 == 0 else (4, 5), all_act=True)

            xload(0)
            xload(1)
            xload(2)
            stage1(0)
            for i in range(NT):
                if i + 1 < NT:
                    stage1(i + 1)
                stage2(i)
                stage3(i - 1)
            stage3(NT - 1)

        def w1_pieces(h):
            return [(j * 128, 128, rec_w_in[:, j * 1024 + h * 128:j * 1024 + (h + 1) * 128]) for j in range(5)]

        def layer1():
            with contextlib.ExitStack() as es1:
                kb.es = es1
                sqT = [kb.sb("sqT", [128, 512], F32) for _ in range(2)]
                sqT_d = [Dep("sqT0"), Dep("sqT1")]
                snT = [kb.sb("snT", [128, 2, 512], F32) for _ in range(2)]
                snT_d = [Dep("snT0"), Dep("snT1")]
                lfT = [kb.sb("lfT", [128, 2, 512], F32) for _ in range(2)]
                lfT_d = [Dep("lfT0"), Dep("lfT1")]
                lf_tok = kb.sb("lf_tok", [128, 2, 4, 128], F32); lftok_d = [Dep("lftok0"), Dep("lftok1")]
                epos = kb.sb("epos", [128, 2, 512], F32); eneg = kb.sb("eneg", [128, 2, 512], F32)
                e_d = [Dep("e0"), Dep("e1")]
                ftmp = kb.sb("ftmp", [128, 2, 4], F32); ftmp_d = [Dep("ftmp0"), Dep("ftmp1")]
                sgT1 = [kb.sb("sgT1", [128, NTOK], BF16) for _ in range(2)]
                qtT = [kb.sb("qtT", [128, 2, NTOK], BF16) for _ in range(2)]
                ktT = [kb.sb("ktT", [128, 2, NTOK], BF16) for _ in range(2)]
                kt_tok = [kb.sb("kt_tok", [128, NT, 2, 128], BF16) for _ in range(2)]
                vtok = [kb.sb("vtok", [128, NT, 128], BF16) for _ in range(2)]
                esc = [kb.sb("esc", [128, NT, 2, 2], F32) for _ in range(2)]
                elast = [kb.sb("elast", [128, NT, 2], F32) for _ in range(2)]
                sgT1_d = [Dep("sgT1a"), Dep("sgT1b")]
                qtT_d = [Dep("qtTa"), Dep("qtTb")]
                ktT_d = [Dep("ktTa"), Dep("ktTb")]
                kttok_d = [[Dep("kttoka0"), Dep("kttoka1")], [Dep("kttokb0"), Dep("kttokb1")]]
                vtok_d = [[Dep("vtoka0"), Dep("vtoka1")], [Dep("vtokb0"), Dep("vtokb1")]]
                esc_d = [Dep("esca"), Dep("escb")]
                NCH = 10
                AT_bf = kb.sb("AT_bf", [128, NCH, 128], BF16); AT_d = [Dep(f"AT{i}") for i in range(NCH)]
                St = kb.sb("St", [128, NCH, 128], F32); St_d = [Dep(f"S{i}") for i in range(NCH)]
                St_s = [kb.dsem(f"S{i}") for i in range(NCH)]
                Sbf = kb.sb("Sbf", [128, NCH, 128], BF16); Sbf_d = [Dep(f"Sbf{i}") for i in range(NCH)]
                sstage = kb.sb("sstage", [128, 4, 128], F32); sstage_d = [Dep(f"sst{i}") for i in range(4)]
                sstage_s = [kb.dsem(f"sst{i}") for i in range(4)]
                o_acc = kb.sb("o_acc", [128, NT, 128], F32); oacc_d = [Dep(f"oacc{i}") for i in range(NT)]
                oss = kb.sb("oss", [128, NT], F32); oss_d = [Dep("oss0"), Dep("oss1")]
                kb.es = es

                pcount = [0]

                def stageA(h, c):
                    hp = h % 2
                    par = c % 2
                    W = Wsl[hp]
                    Wd = Wsl_d[hp]
                    tok0 = c * 512
                    hds = [hT_d[c * 4 + q] for q in range(4)]

                    def proj_fm(col0):
                        pcount[0] += 1
                        bk = pcount[0] % 2
                        for k in range(8):
                            kb.op(PE, lambda: T.matmul(
                                banks[bk][:, :], lhsT=W[:, k, col0:col0 + 128],
                                rhs=hT[:, k, tok0:tok0 + 512], start=(k == 0), stop=(k == 7)),
                                reads=hds + [Wd], writes=[bdeps[bk]], signal=(k == 7))
                        return bk
                    bk = proj_fm(0)
                    kb.op(ACT, lambda: S.activation(out=sqT[par][:], in_=banks[bk][:, :], func=AF.Silu),
                          reads=[bdeps[bk]], writes=[sqT_d[par]])
                    yield
                    bk = proj_fm(512)
                    kb.op(ACT, lambda: S.activation(out=sgT1[hp][:, tok0:tok0 + 512], in_=banks[bk][:, :], func=AF.Silu),
                          reads=[bdeps[bk]], writes=[sgT1_d[hp]])
                    yield
                    for d in range(2):
                        bk = proj_fm(128 + d * 128)
                        kb.op(ACT, lambda: S.activation(out=snT[par][:, d, :], in_=banks[bk][:, :],
                                                        func=AF.Sigmoid, scale=-1.0),
                              reads=[bdeps[bk]], writes=[snT_d[par]])
                        yield
                    for d in range(2):
                        kb.op(ACT, lambda: S.activation(out=lfT[par][:, d, :], in_=snT[par][:, d, :],
                                                        func=AF.Ln, scale=lbm1[:, d, h:h + 1], bias=1.0),
                              reads=[snT_d[par], lb_d], writes=[lfT_d[par]])
                    pcount[0] += 1
                    bk = pcount[0] % 2
                    for j in range(4):
                        for k in range(8):
                            kb.op(PE, lambda: T.matmul(
                                banks[bk][:, j * 128:(j + 1) * 128],
                                lhsT=hT[:, k, tok0 + j * 128:tok0 + (j + 1) * 128],
                                rhs=W[:, k, 384:512], start=(k == 0), stop=(k == 7)),
                                reads=hds + [Wd], writes=[bdeps[bk]], signal=(k == 7))
                    kb.op(DVE, lambda: V.tensor_copy(
                        out=vtok[hp][:, c * 4:c * 4 + 4, :], in_=banks[bk][:, :].rearrange("p (j d) -> p j d", d=128)),
                        reads=[bdeps[bk]], writes=[vtok_d[hp][min(c, 1)]])
                    yield

                def stageB1(h, c):
                    hp = h % 2
                    par = c % 2
                    tok0 = c * 512
                    for d in range(2):
                        for j in range(4):
                            kb.op(PE, lambda: T.transpose(
                                out=banks[3 + d][:, j * 128:(j + 1) * 128], in_=lfT[par][:, d, j * 128:(j + 1) * 128],
                                identity=ident_f[:]), reads=[lfT_d[par], cd], writes=[bdeps[3 + d]], signal=(j == 3))
                    for d in range(2):
                        kb.op(DVE, lambda: V.tensor_copy(
                            out=lf_tok[:, d, :, :], in_=banks[3 + d][:, :].rearrange("p (j d) -> p j d", d=128)),
                            reads=[bdeps[3 + d]], writes=[lftok_d[d]])
                    yield
                    for d in range(2):
                        for j in range(4):
                            kb.op(PE, lambda: T.matmul(
                                banks[3 + d][:, j * 128:(j + 1) * 128], lhsT=lf_tok[:, d, j, :],
                                rhs=wc[:, d, 0:128], start=True, stop=True),
                                reads=[lftok_d[d], cd], writes=[bdeps[3 + d]], signal=(j == 3))
                    for d in range(2):
                        kb.op(ACT, lambda: S.activation(out=epos[:, d, :], in_=banks[3 + d][:, :], func=AF.Exp),
                              reads=[bdeps[3 + d]], writes=[e_d[d]])
                        kb.op(ACT, lambda: S.activation(out=eneg[:, d, :], in_=banks[3 + d][:, :], func=AF.Exp,
                                                        scale=-1.0),
                              reads=[bdeps[3 + d]], writes=[e_d[d]])
                    yield
                    for d in range(2):
                        c0_, c1_ = (0, 127) if d == 0 else (127, 0)
                        snc = snT[par][:, d, :].rearrange("p (j t) -> p j t", t=128)
                        kb.op(DVE, lambda: V.tensor_scalar(out=ftmp[:, d, :], in0=snc[:, :, c0_],
                                                           scalar1=lbm1[:, d, h:h + 1],
                                                           scalar2=1.0, op0=ALU.mult, op1=ALU.add),
                              reads=[snT_d[par], lb_d], writes=[ftmp_d[d]])
                        env = eneg[:, d, :].rearrange("p (j t) -> p j t", t=128)
                        epv = epos[:, d, :].rearrange("p (j t) -> p j t", t=128)
                        kb.op(DVE, lambda: V.tensor_tensor(out=esc[hp][:, c * 4:c * 4 + 4, d, 0], in0=env[:, :, c0_],
                                                           in1=ftmp[:, d, :], op=ALU.mult),
                              reads=[e_d[d], ftmp_d[d]], writes=[esc_d[hp]])
                        kb.op(DVE, lambda: V.tensor_copy(out=esc[hp][:, c * 4:c * 4 + 4, d, 1], in_=epv[:, :, c1_]),
                              reads=[e_d[d]], writes=[esc_d[hp]])
                        if c >= 1:
                            kb.op(DVE, lambda: V.tensor_tensor(
                                out=esc[hp][:, c * 4:c * 4 + 4, d, 0], in0=esc[hp][:, c * 4:c * 4 + 4, d, 0],
                                in1=rf_t[:, (c - 1) * 4:(c - 1) * 4 + 4, d], op=ALU.mult),
                                reads=[esc_d[hp], cd], writes=[esc_d[hp]])
                        kb.op(DVE, lambda: V.tensor_tensor(
                            out=qtT[hp][:, d, tok0:tok0 + 512], in0=sqT[par][:], in1=epos[:, d, :], op=ALU.mult),
                            reads=[sqT_d[par], e_d[d]], writes=[qtT_d[hp]])
                        kb.op(DVE, lambda: V.scalar_tensor_tensor(
                            out=ktT[hp][:, d, tok0:tok0 + 512], in0=snT[par][:, d, :], scalar=oml[:, d, h:h + 1],
                            in1=eneg[:, d, :], op0=ALU.mult, op1=ALU.mult),
                            reads=[snT_d[par], e_d[d], lb_d], writes=[ktT_d[hp]])
                    if c == 2:
                        kb.op(DVE, lambda: V.tensor_tensor(out=elast[hp][:], in0=esc[hp][:, :, :, 0],
                                                           in1=esc[hp][:, :, :, 1], op=ALU.mult),
                              reads=[esc_d[hp]], writes=[esc_d[hp]])
                    yield

                def stageB2(h, c):
                    hp = h % 2
                    tok0 = c * 512
                    for d in range(2):
                        bb = banks[3 + d].bitcast(BF16)
                        for j in range(4):
                            kb.op(PE, lambda: T.transpose(
                                out=bb[:, j * 128:(j + 1) * 128],
                                in_=ktT[hp][:, d, tok0 + j * 128:tok0 + (j + 1) * 128], identity=ident_b[:]),
                                reads=[ktT_d[hp], cd2], writes=[bdeps[3 + d]], signal=(j == 3))
                    for d in range(2):
                        bb = banks[3 + d].bitcast(BF16)
                        kb.op(DVE, lambda: V.tensor_copy(
                            out=kt_tok[hp][:, c * 4:c * 4 + 4, d, :],
                            in_=bb[:, 0:512].rearrange("p (j d) -> p j d", d=128)),
                            reads=[bdeps[3 + d]], writes=[kttok_d[hp][min(c, 1)]])
                    yield

                import itertools

                def m1_stream(h):
                    return itertools.chain(stageA(h, 0), stageA(h, 1), stageB1(h, 0), stageA(h, 2), stageB2(h, 0),
                                           stageB1(h, 1), stageB2(h, 1), stageB1(h, 2), stageB2(h, 2))

                def pull(st, n):
                    for _ in range(n):
                        try:
                            next(st)
                        except StopIteration:
                            return

                def m2_setup(h):
                    chains = []
                    for si, (t0, nt, is_s) in enumerate(SEQS):
                        for d in range(2):
                            ci = si * 2 + d
                            if is_s:
                                kb.dma(SP, St_s[ci], St[:, ci, :], state0[d, h, :, :], writes=[St_d[ci]])
                            else:
                                kb.op(POOL, lambda: G.memset(St[:, ci, :], 0.0), writes=[St_d[ci]])
                            order = list(range(nt)) if d == 0 else list(range(nt - 1, -1, -1))
                            chains.append((ci, si, d, [t0 + x for x in order]))
                    return chains

                wcount = [0]
                sgcount = [0]

                def m2_round(h, chains, step, touched, mid_hook):
                    hp = h % 2
                    act = [ch for ch in chains if step < len(ch[3])]
                    waves = [act[i:i + 4] for i in range(0, len(act), 4)]
                    for wi, wave in enumerate(waves):
                        wcount[0] += 1
                        bP = 7 if wcount[0] % 2 == 0 else 2
                        for sl, (ci, si, d, tiles) in enumerate(wave):
                            tl = tiles[step]
                            ts_ = slice(tl * 128, (tl + 1) * 128)
                            cs_ = slice(sl * 128, (sl + 1) * 128)
                            kb.op(PE, lambda: T.matmul(banks[5][:, cs_], lhsT=ktT[hp][:, d, ts_], rhs=qtT[hp][:, d, ts_],
                                                       start=True, stop=True),
                                  reads=[ktT_d[hp], qtT_d[hp]], writes=[bdeps[5]], signal=(sl == len(wave) - 1))
                        for sl, (ci, si, d, tiles) in enumerate(wave):
                            tl = tiles[step]
                            cs_ = slice(sl * 128, (sl + 1) * 128)
                            kb.op(PE, lambda: T.matmul(banks[bP][:, cs_], lhsT=kt_tok[hp][:, tl, d, :],
                                                       rhs=vtok[hp][:, tl, :], start=True, stop=True),
                                  reads=[kttok_d[hp][0 if tl < 4 else 1], vtok_d[hp][0 if tl < 4 else 1]], writes=[bdeps[bP]],
                                  signal=(sl == len(wave) - 1))
                        for sl, (ci, si, d, tiles) in enumerate(wave):
                            tl = tiles[step]
                            cs_ = slice(sl * 128, (sl + 1) * 128)
                            kb.op(ACT, lambda: S.activation(out=Sbf[:, ci, :], in_=St[:, ci, :], func=AF.Copy,
                                                            scale=esc[hp][:, tl, d, 0:1]),
                                  reads=[St_d[ci], esc_d[hp]], writes=[Sbf_d[ci]])
                            kb.op(DVE, lambda: V.tensor_tensor(out=AT_bf[:, ci, :], in0=banks[5][:, cs_],
                                                               in1=maskt[:, d, :], op=ALU.mult),
                                  reads=[bdeps[5], cd], writes=[AT_d[ci]])
                        for sl, (ci, si, d, tiles) in enumerate(wave):
                            tl = tiles[step]
                            kb.op(DVE, lambda: V.tensor_scalar(out=St[:, ci, :], in0=St[:, ci, :],
                                                               scalar1=elast[hp][:, tl, d:d + 1], scalar2=None,
                                                               op0=ALU.mult),
                                  reads=[St_d[ci], esc_d[hp]], writes=[St_d[ci]])
                        if mid_hook is not None:
                            mid_hook(step, wi, False)
                        for sl, (ci, si, d, tiles) in enumerate(wave):
                            tl = tiles[step]
                            ts_ = slice(tl * 128, (tl + 1) * 128)
                            cs_ = slice(sl * 128, (sl + 1) * 128)
                            kb.op(PE, lambda: T.matmul(banks[6][:, cs_], lhsT=AT_bf[:, ci, :], rhs=vtok[hp][:, tl, :],
                                                       start=True, stop=False, skip_group_check=True),
                                  reads=[AT_d[ci], vtok_d[hp][0 if tl < 4 else 1]], writes=[bdeps[6]], signal=False)
                            kb.op(PE, lambda: T.matmul(banks[6][:, cs_], lhsT=qtT[hp][:, d, ts_], rhs=Sbf[:, ci, :],
                                                       start=False, stop=True, skip_group_check=True),
                                  reads=[qtT_d[hp], Sbf_d[ci]], writes=[bdeps[6]], signal=(sl == len(wave) - 1))
                        for sl, (ci, si, d, tiles) in enumerate(wave):
                            tl = tiles[step]
                            cs_ = slice(sl * 128, (sl + 1) * 128)
                            if not touched[tl]:
                                touched[tl] = True
                                kb.op(DVE, lambda: V.tensor_copy(out=o_acc[:, tl, :], in_=banks[6][:, cs_]),
                                      reads=[bdeps[6]], writes=[oacc_d[tl]])
                            else:
                                kb.op(DVE, lambda: V.tensor_tensor(out=o_acc[:, tl, :], in0=banks[6][:, cs_],
                                                                   in1=o_acc[:, tl, :], op=ALU.add),
                                      reads=[bdeps[6], oacc_d[tl]], writes=[oacc_d[tl]])
                            kb.op(DVE, lambda: V.scalar_tensor_tensor(
                                out=St[:, ci, :], in0=banks[bP][:, cs_], scalar=esc[hp][:, tl, d, 1:2],
                                in1=St[:, ci, :], op0=ALU.mult, op1=ALU.add),
                                reads=[bdeps[bP], St_d[ci], esc_d[hp]], writes=[St_d[ci]])
                            if SEQS[si][2]:
                                lt = tl - SEQS[si][0]
                                if (d == 0 and lt % 2 == 1) or (d == 1 and lt % 2 == 0):
                                    sgcount[0] += 1
                                    q_ = sgcount[0] % 4
                                    kb.op(ACT, lambda: S.copy(out=sstage[:, q_, :], in_=St[:, ci, :]),
                                          reads=[St_d[ci]], writes=[sstage_d[q_]])
                                    kb.dma(SP, sstage_s[q_], ns1_out[lt // 2, d, h, :, :], sstage[:, q_, :],
                                           reads=[sstage_d[q_]])
                        if mid_hook is not None:
                            mid_hook(step, wi, True)

                def m3(h, half):
                    hp = h % 2
                    lo, n_ = (0, 4) if half == 0 else (4, 8)
                    t8 = slice(lo, lo + n_)
                    osq = kt_tok[hp][:].rearrange("p t d k -> p (t d k)").bitcast(F32).rearrange(
                        "p (t k) -> p t k", k=128)[:, t8, :]
                    osq_d = kttok_d[hp][half]
                    on_bf = vtok[hp][:, t8, :]
                    on_d = vtok_d[hp][half]
                    oa = o_acc[:, t8, :]
                    oad = oacc_d[lo:lo + n_]
                    ossv = oss[:, t8]
                    kb.op(DVE, lambda: V.tensor_tensor(out=osq, in0=oa, in1=oa, op=ALU.mult),
                          reads=oad, writes=[osq_d])
                    kb.op(DVE, lambda: V.tensor_reduce(out=ossv, in_=osq, axis=AX.X, op=ALU.add),
                          reads=[osq_d], writes=[oss_d[half]])
                    kb.op(DVE, lambda: V.tensor_scalar(out=ossv, in0=ossv, scalar1=1.0 / 128.0,
                                                       scalar2=NORM_EPS, op0=ALU.mult, op1=ALU.add),
                          reads=[oss_d[half]], writes=[oss_d[half]])
                    kb.op(ACT, lambda: S.sqrt(out=ossv, in_=ossv), reads=[oss_d[half]], writes=[oss_d[half]])
                    kb.op(DVE, lambda: V.reciprocal(out=ossv, in_=ossv), reads=[oss_d[half]], writes=[oss_d[half]])
                    kb.op(DVE, lambda: V.tensor_tensor(
                        out=osq, in0=oa, in1=ossv.unsqueeze(2).to_broadcast([128, n_, 128]), op=ALU.mult),
                        reads=oad + [oss_d[half]], writes=[osq_d])
                    kb.op(DVE, lambda: V.tensor_tensor(
                        out=on_bf, in0=osq, in1=rgain[:, :].unsqueeze(1).to_broadcast([128, n_, 128]),
                        op=ALU.mult), reads=[osq_d, cd], writes=[on_d])

                def m3b(h, half):
                    hp = h % 2
                    lo, n_ = (0, 4) if half == 0 else (4, 8)
                    on_bf = vtok[hp][:, lo:lo + n_, :]
                    on_d = vtok_d[hp][half]
                    bkm = 5 if half == 0 else 6
                    bb = banks[bkm].bitcast(BF16)
                    for q in range(n_):
                        kb.op(PE, lambda: T.transpose(out=bb[:, q * 128:(q + 1) * 128], in_=on_bf[:, q, :],
                                                      identity=ident_b[:]),
                              reads=[on_d, cd2], writes=[bdeps[bkm]], signal=(q == n_ - 1))
                    kb.op(DVE, lambda: V.tensor_tensor(
                        out=gT[:, h, lo * 128:(lo + n_) * 128], in0=bb[:, 0:n_ * 128],
                        in1=sgT1[hp][:, lo * 128:(lo + n_) * 128], op=ALU.mult),
                        reads=[bdeps[bkm], sgT1_d[hp]], writes=[gT_d[lo + q] for q in range(n_)])

                load_w(0, w1_pieces(0))
                load_w(1, w1_pieces(1))
                pull(m1_stream(0), 1000)
                for h in range(8):
                    chains = m2_setup(h)
                    touched = [False] * NT
                    stream = m1_stream(h + 1) if h + 1 < 8 else iter(())
                    hp_count = [0]

                    def hook(step, wi, end_, h=h, stream=stream, hp_count=hp_count):
                        hp_count[0] += 1
                        pull(stream, 1 if end_ else 2)
                        if end_:
                            return
                        if step == 0 and wi == 0 and h >= 1:
                            m3b(h - 1, 1)
                        if step == 2 and wi == 0:
                            for (ci, si, d, tiles) in chains:
                                if not SEQS[si][2]:
                                    kb.dma(SP, St_s[ci], ns_out[si, d, h, :, :], St[:, ci, :], reads=[St_d[ci]])
                            m3(h, 0)
                        if step == 4 and wi == 0:
                            m3b(h, 0)
                    for r in range(8):
                        m2_round(h, chains, r, touched, hook)
                    pull(stream, 1000)
                    m3(h, 1)
                    if h + 2 < 8:
                        load_w(h % 2, w1_pieces(h + 2))
                m3b(7, 1)
                kb.barrier()
            kb.es = es

        def w0_pieces(g):
            return [(0, 256, attn_w_in[:, g * 256:(g + 1) * 256]),
                    (256, 64, attn_w_in[:, 1024 + g * 64:1024 + (g + 1) * 64]),
                    (320, 64, attn_w_in[:, 1280 + g * 64:1280 + (g + 1) * 64]),
                    (384, 256, attn_w_in[:, 1536 + g * 256:1536 + (g + 1) * 256])]

        with contextlib.ExitStack() as es0:
            kb.es = es0
            qkv_all = [kb.sb("qkv_all", [128, 8, 384], F32), None, kb.sb("qkv_allS", [128, 8, 384], F32)]
            qkvA_d = [[Dep(f"qkvA{p}{i}") for i in range(8)] for p in range(3)]
            qkvA_s = [kb.dsem("qkvA0"), kb.dsem("qkvA1"), kb.dsem("qkvA2")]
            sq = [kb.sb("sq", [128, 320], F32) for _ in range(2)]
            sq_d = [Dep("sq0"), Dep("sq1")]
            ss_all = [kb.sb("ss_all", [128, 8, 5], F32) for _ in range(3)]
            ss_d = [Dep("ss0"), Dep("ss1"), Dep("ss2")]
            kout = [kb.sb("kout", [128, 8, 64], F32), None, kb.sb("koutS", [128, 8, 64], F32)]
            kout_d = [Dep("kout0"), None, Dep("kout2")]
            kout_s = [kb.dsem("kout0"), None, kb.dsem("kout2")]
            rt1 = kb.sb("rt1", [128, 8, 320], F32)
            rt2 = kb.sb("rt2", [128, 8, 320], F32)
            rt_d = Dep("rt")
            q_bf_all = [kb.sb("q_bf_all", [128, 8, 384], BF16), None, kb.sb("q_bf_allS", [128, 8, 384], BF16)]
            rope = kb.sb("rope", [128, 2, 8, 64], F32)
            rope_d = Dep("rope")
            rope_s = kb.dsem("rope")
            qbf_d = [Dep("qbf0"), Dep("qbf1"), Dep("qbf2")]
            qT4 = kb.sb("qT4", [128, 8, 4, 128], BF16)
            qT4_d = Dep("qT4")
            kT = kb.sb("kT", [128, 1280], BF16)
            kT_d = Dep("kT")
            vaug = kb.sb("vaug", [128, 10, 65], BF16)
            vaug_d = Dep("vaug")
            sgT = kb.sb("sgT", [128, 2, 1024], BF16)
            sgT_d = Dep("sgT")
            PT = [kb.sb("PT", [128, 10, 512], BF16) for _ in range(2)]
            PT_d = [Dep("PT0"), Dep("PT1")]
            ckv = kb.sb("ckv", [128, 2, 2, 64], F32)
            ckv_d = Dep("ckv")
            ckv_s = kb.dsem("ckv")
            ck_bf = kb.sb("ck_bf", [128, 2, 128], BF16)
            ckbf_d = Dep("ckbf")
            rinv4 = kb.sb("rinv4", [128, 4], F32)
            rinv_d = Dep("rinv4")
            on_bf = kb.sb("on_bf", [128, 256], BF16)
            on_d = Dep("on")
            xt = [kb.sb("xt", [128, D], F32) for _ in range(2)]
            kb.es = es

            kb.dma(SP, rope_s, rope[:], c_rope, writes=[rope_d])
            kb.op(POOL, lambda: G.memset(vaug[:], 1.0), writes=[vaug_d])
            kb.op(POOL, lambda: G.memset(qT4[:], 0.0), writes=[qT4_d])
            load_w(0, w0_pieces(0))
            tcount = 0
            qtcount = 0
            tpcount = 0
            def p123(g, si, par):
                for ti in range((PB if si < 0 else SEQS[si])[1]):
                    p1_tile(g, si, par, ti)
                p23(g, si, par)

            def p1_tile(g, si, par, ti):
                nonlocal tcount
                t0, nt, is_s = PB if si < 0 else SEQS[si]
                W = Wsl[g % 2]
                Wd = Wsl_d[g % 2]
                if True:
                    tile = t0 + ti
                    tcount += 1
                    bk = tcount % 2
                    p_ = tcount % 2
                    for k in range(8):
                        kb.op(PE, lambda: T.matmul(
                            banks[bk][:, 0:384], lhsT=hT[:, k, tile * 128:(tile + 1) * 128],
                            rhs=W[:, k, 0:384], start=(k == 0), stop=(k == 7)),
                            reads=[hT_d[tile], Wd], writes=[bdeps[bk]], signal=(k == 7))
                    kb.op(DVE, lambda: V.tensor_copy(out=qkv_all[par][:, ti, :], in_=banks[bk][:, 0:384]),
                          reads=[bdeps[bk]], writes=[qkvA_d[par][ti]])
                    kb.op(POOL, lambda: G.tensor_tensor(out=sq[p_][:], in0=qkv_all[par][:, ti, 0:320],
                                                        in1=qkv_all[par][:, ti, 0:320], op=ALU.mult),
                          reads=[qkvA_d[par][ti]], writes=[sq_d[p_]])
                    kb.op(DVE, lambda: V.tensor_reduce(out=ss_all[par][:, ti, :],
                                                       in_=sq[p_][:].rearrange("p (h d) -> p h d", d=64),
                                                       axis=AX.X, op=ALU.add), reads=[sq_d[p_]], writes=[ss_d[par]])

            def p23(g, si, par):
                t0, nt, is_s = PB if si < 0 else SEQS[si]
                koff = 2 if is_s else 0
                qds = qkvA_d[par][0:nt]
                if is_s:
                    kb.dma(SP, qkvA_s[par], nv1_out.rearrange("(t p) g d -> p t g d", p=128)[:, :, g, :],
                           qkv_all[par][:, 0:nt, 320:384], reads=qds)
                if not is_s:
                    kb.dma(SP, qkvA_s[par], nv_out.rearrange("s (t p) g d -> p (s t) g d", p=128)[:, :, g, :],
                           qkv_all[par][:, 0:nt, 320:384], reads=qds)
                ssv = ss_all[par][:, 0:nt, :]
                kb.op(DVE, lambda: V.tensor_scalar(out=ssv, in0=ssv, scalar1=64.0 * NORM_EPS,
                                                   scalar2=None, op0=ALU.add), reads=[ss_d[par]], writes=[ss_d[par]])
                kb.op(ACT, lambda: S.sqrt(out=ssv, in_=ssv), reads=[ss_d[par]], writes=[ss_d[par]])
                kb.op(DVE, lambda: V.reciprocal(out=ssv, in_=ssv), reads=[ss_d[par]], writes=[ss_d[par]])
                qk4 = qkv_all[par][:, 0:nt, 0:320].rearrange("p t (h d) -> p t h d", d=64)
                kb.op(DVE, lambda: V.tensor_tensor(
                    out=qk4, in0=qk4, in1=ssv.unsqueeze(3).to_broadcast([128, nt, 5, 64]), op=ALU.mult),
                    reads=qds + [ss_d[par]], writes=qds)
                g5 = gain5[:].unsqueeze(1).to_broadcast([128, nt, 5, 64])
                qb4 = q_bf_all[par][:, 0:nt, 0:320].rearrange("p t (h d) -> p t h d", d=64)
                if not is_s:
                    kb.op(DVE, lambda: V.tensor_tensor(out=qb4, in0=qk4, in1=g5, op=ALU.mult),
                          reads=qds + [cd2], writes=[qbf_d[par]])
                    kb.op(POOL, lambda: G.tensor_tensor(
                        out=kout[par][:, 0:nt, :], in0=qkv_all[par][:, 0:nt, 256:320],
                        in1=gain5[:, 4, :].unsqueeze(1).to_broadcast([128, nt, 64]), op=ALU.mult),
                        reads=qds + [cd2], writes=[kout_d[par]])
                    kb.dma(SP, kout_s[par], nk_out.rearrange("s (t p) g d -> p (s t) g d", p=128)[:, :, g, :],
                           kout[par][:, 0:nt, :], reads=[kout_d[par]])
                else:
                    kb.op(DVE, lambda: V.tensor_tensor(out=qk4, in0=qk4, in1=g5, op=ALU.mult),
                          reads=qds + [cd2], writes=qds)
                    r14 = rt1[:, 0:nt, :].rearrange("p t (h d) -> p t h d", d=64)
                    kb.op(DVE, lambda: V.tensor_tensor(
                        out=r14, in0=qk4, in1=rope[:, 0, 0:nt, :].unsqueeze(2).to_broadcast([128, nt, 5, 64]),
                        op=ALU.mult), reads=qds + [rope_d], writes=[rt_d])
                    qv = qkv_all[par][:, 0:nt, 0:320].rearrange("p t (h a b c) -> p t h a b c", a=2, b=2, c=16)
                    r2v = rt2[:, 0:nt, :].rearrange("p t (h a b c) -> p t h a b c", a=2, b=2, c=16)
                    snv = rope[:, 1, 0:nt, :].rearrange("p t (a b c) -> p t a b c", a=2, b=2)
                    for a in range(2):
                        for b in range(2):
                            eng_, h_ = (POOL, G) if a == 0 else (DVE, V)
                            kb.op(eng_, lambda: h_.tensor_tensor(
                                out=r2v[:, :, :, a, b, :], in0=qv[:, :, :, a, 1 - b, :],
                                in1=snv[:, :, a, b, :].unsqueeze(2).to_broadcast([128, nt, 5, 16]), op=ALU.mult),
                                reads=qds + [rope_d], writes=[rt_d])
                    kb.op(DVE, lambda: V.tensor_tensor(out=q_bf_all[par][:, 0:nt, 0:320], in0=rt1[:, 0:nt, :],
                                                       in1=rt2[:, 0:nt, :], op=ALU.add),
                          reads=[rt_d], writes=[qbf_d[par]])
                    kb.op(POOL, lambda: G.tensor_tensor(out=kout[par][:, 0:nt, :], in0=rt1[:, 0:nt, 256:320],
                                                        in1=rt2[:, 0:nt, 256:320], op=ALU.add),
                          reads=[rt_d], writes=[kout_d[par]])
                    kb.dma(SP, kout_s[par], nk1_out.rearrange("(t p) g d -> p t g d", p=128)[:, :, g, :],
                           kout[par][:, 0:nt, :], reads=[kout_d[par]])
                kb.op(POOL, lambda: G.tensor_copy(out=q_bf_all[par][:, 0:nt, 320:384], in_=q_bf_all[par][:, 0:nt, 256:320]),
                      reads=[qbf_d[par]], writes=[qbf_d[par]])

            def rest(g, si, par, mid_hook=None, tile_hook=None):
                nonlocal qtcount, tpcount
                t0, nt, is_s = PB if si < 0 else SEQS[si]
                W = Wsl[g % 2]
                Wd = Wsl_d[g % 2]
                koff = 2 if is_s else 0
                nkb = nt + koff
                if is_s:
                    kb.dma(SP, ckv_s, ckv[:, 0, :, :], cache_k[:, g, :].rearrange("(b p) d -> p b d", p=128),
                           writes=[ckv_d])
                    kb.dma(SP, ckv_s, ckv[:, 1, :, :], cache_v[:, g, :].rearrange("(b p) d -> p b d", p=128),
                           writes=[ckv_d])
                    kb.op(POOL, lambda: G.tensor_copy(out=ck_bf[:, :, 0:64], in_=ckv[:, 0, :, :]),
                          reads=[ckv_d], writes=[ckbf_d])
                    kb.op(POOL, lambda: G.tensor_copy(out=ck_bf[:, :, 64:128], in_=ckv[:, 0, :, :]),
                          reads=[ckv_d], writes=[ckbf_d])
                    kb.op(POOL, lambda: G.tensor_copy(out=vaug[:, 0:2, 0:64], in_=ckv[:, 1, :, :]),
                          reads=[ckv_d], writes=[vaug_d])
                    bkb = banks[3].bitcast(BF16)
                    for b2 in range(2):
                        kb.op(PE, lambda b2=b2: T.transpose(out=bkb[:, b2 * 128:(b2 + 1) * 128],
                                                            in_=ck_bf[:, b2, :], identity=ident_b[:]),
                              reads=[ckbf_d, cd2], writes=[bdeps[3]], signal=(b2 == 1))
                    kb.op(DVE, lambda: V.tensor_copy(out=kT[:, 0:256], in_=bkb[:, 0:256]),
                          reads=[bdeps[3]], writes=[kT_d])
                kb.op(POOL, lambda: G.tensor_copy(out=vaug[:, koff:koff + nt, 0:64], in_=qkv_all[par][:, 0:nt, 320:384]),
                      reads=qkvA_d[par][0:nt], writes=[vaug_d])
                for tp in range(nt // 2):
                    tpcount += 1
                    bkn = 2
                    bkb = banks[bkn].bitcast(BF16)
                    for u in range(2):
                        for j in range(3):
                            kb.op(PE, lambda: T.transpose(
                                out=bkb[:, j * 256 + u * 128:j * 256 + (u + 1) * 128],
                                in_=q_bf_all[par][:, 2 * tp + u, j * 128:(j + 1) * 128], identity=ident_b[:]),
                                reads=[qbf_d[par], cd2], writes=[bdeps[bkn]], signal=(u == 1 and j == 2))
                    kb.op(DVE, lambda: V.tensor_copy(
                        out=qT4[0:64, 2 * tp:2 * tp + 2, 0:4:2, :],
                        in_=bkb[0:64, 0:512].rearrange("p (h u t) -> p u h t", h=2, u=2)),
                        reads=[bdeps[bkn]], writes=[qT4_d])
                    kb.op(DVE, lambda: V.tensor_copy(
                        out=qT4[64:128, 2 * tp:2 * tp + 2, 1:4:2, :],
                        in_=bkb[64:128, 0:512].rearrange("p (h u t) -> p u h t", h=2, u=2)),
                        reads=[bdeps[bkn]], writes=[qT4_d])
                    kb.op(DVE, lambda: V.tensor_copy(
                        out=kT[:, (koff + 2 * tp) * 128:(koff + 2 * tp + 2) * 128], in_=bkb[:, 512:768]),
                        reads=[bdeps[bkn]], writes=[kT_d])
                ntok = nt * 128
                for c0 in range(0, ntok, 512):
                    n = min(512, ntok - c0)
                    for j in range(2):
                        bk = 3
                        for k in range(8):
                            kb.op(PE, lambda k=k, j=j, c0=c0, n=n: T.matmul(
                                banks[3][:, 0:n], lhsT=W[:, k, 384 + j * 128:384 + (j + 1) * 128],
                                rhs=hT[:, k, t0 * 128 + c0:t0 * 128 + c0 + n], start=(k == 0), stop=(k == 7)),
                                reads=[hT_d[t0 + c0 // 128 + q] for q in range(n // 128)] + [Wd],
                                writes=[bdeps[3]], signal=(k == 7))
                        kb.op(ACT, lambda j=j, c0=c0, n=n: S.activation(
                            out=sgT[:, j, c0:c0 + n], in_=banks[3][:, 0:n], func=AF.Silu),
                            reads=[bdeps[3]], writes=[sgT_d])
                if mid_hook is not None:
                    mid_hook()
                def kbs(ti):
                    if si < 0:
                        return [2 * (ti // 2), 2 * (ti // 2) + 1]
                    return list(range(nkb))

                def scores(ti, j, kbi, ps_):
                    bk = 4 + (j % 2)
                    kb.op(PE, lambda: T.matmul(
                        banks[bk][:, :], lhsT=kT[:, kbi * 128:(kbi + 1) * 128],
                        rhs=qT4[:, ti, :, :].rearrange("p h t -> p (h t)"),
                        start=True, stop=True), reads=[kT_d, qT4_d], writes=[bdeps[bk]])
                    kb.op(ACT, lambda: S.activation(
                        out=PT[ps_][:, j, :], in_=banks[bk][:, :], func=AF.Exp, scale=0.125,
                        bias=(bias_t[:, ti, kbi:kbi + 1] if is_s else 0.0)),
                        reads=[bdeps[bk], cd], writes=[PT_d[ps_]])

                qbase = qtcount
                for j, kbi in enumerate(kbs(0)):
                    scores(0, j, kbi, (qbase + 1) % 2)
                for ti in range(nt):
                    tile = t0 + ti
                    if tile_hook is not None:
                        tile_hook(ti)
                    qtcount += 1
                    ps_ = qtcount % 2
                    bo = 6 + (qtcount % 2)
                    kl = kbs(ti)
                    kn = kbs(ti + 1) if ti + 1 < nt else []
                    for j, kbi in enumerate(kl):
                        if j < len(kn):
                            scores(ti + 1, j, kn[j], (qtcount + 1) % 2)
                        for h in range(4):
                            kb.op(PE, lambda: T.matmul(
                                banks[bo][:, h * 65:(h + 1) * 65], lhsT=PT[ps_][:, j, h * 128:(h + 1) * 128],
                                rhs=vaug[:, kbi, :], start=(j == 0 and h == 0), stop=(j == len(kl) - 1),
                                skip_group_check=True),
                                reads=[PT_d[ps_], vaug_d], writes=[bdeps[bo]],
                                signal=(h == 3 and j == len(kl) - 1))
                    ov = banks[bo][:, 0:260].rearrange("p (h d) -> p h d", d=65)
                    kb.op(DVE, lambda: V.reciprocal(out=rinv4[:], in_=ov[:, :, 64]),
                          reads=[bdeps[bo]], writes=[rinv_d])
                    kb.op(DVE, lambda: V.tensor_tensor(
                        out=on_bf[:].rearrange("p (h d) -> p h d", d=64), in0=ov[:, :, 0:64],
                        in1=rinv4[:, :].unsqueeze(2).to_broadcast([128, 4, 64]), op=ALU.mult),
                        reads=[bdeps[bo], rinv_d], writes=[on_d])
                    bkb7 = banks[2].bitcast(BF16)
                    for j in range(2):
                        kb.op(PE, lambda: T.transpose(out=bkb7[:, 512 + j * 128:512 + (j + 1) * 128],
                                                      in_=on_bf[:, j * 128:(j + 1) * 128],
                                                      identity=ident_b[:]),
                              reads=[on_d, cd2], writes=[bdeps[2]], signal=(j == 1))
                    kb.op(DVE, lambda: V.tensor_tensor(
                        out=gT[:, 2 * g:2 * g + 2, tile * 128:(tile + 1) * 128],
                        in0=bkb7[:, 512:768].rearrange("p (j t) -> p j t", j=2),
                        in1=sgT[:, :, ti * 128:(ti + 1) * 128], op=ALU.mult),
                        reads=[bdeps[2], sgT_d], writes=[gT_d[tile]])

            load_w(1, w0_pieces(1))
            for i in range(NT):
                xs = i % 2
                kb.dma(SP, xt_s[xs], xt[xs][:], x_all[i * 128:(i + 1) * 128, :], writes=[xt_d[xs]])
                make_hT(xt[xs], xt_d[xs], i, 0 if i < 4 else 1, 0, (4, 5) if i % 2 == 0 else (6, 7))
                if i >= 1:
                    j = i - 1
                    if j < 4:
                        p1_tile(0, -1, 0, j)
                    else:
                        p1_tile(0, SI_S, 2, j - 4)
            p1_tile(0, SI_S, 2, 7)
            p23(0, -1, 0)
            p23(0, SI_S, 2)
            for g in range(4):
                rest(g, -1, 0)

                def hook(g=g):
                    if g + 1 < 4:
                        p123(g + 1, -1, 0)
                        p123(g + 1, SI_S, 2)
                rest(g, SI_S, 2, hook)
                if g + 2 < 4:
                    load_w(g % 2, w0_pieces(g + 2))
            kb.barrier()

        if stg >= 3:
            epilogue(0, attn_w_out, x_all, last=(n_layers == 1))

        if n_layers == 2 and stg >= 4:
            layer1()
            epilogue(1, rec_w_out, x1_dram, last=True)

        kb.finish()
    return nc


def _consts(is_sample_slot):
    ident = np.eye(128, dtype=np.float32)
    t = np.arange(1024)
    rows = (t // 64).astype(np.float32)
    cols = (t % 64).astype(np.float32)
    inv_freq = (1.0 / (np.float32(10000.0) ** (np.arange(0, 32, 2, dtype=np.float32) / np.float32(32)))).astype(np.float32)
    ang_r = rows[:, None] * inv_freq[None, :]
    ang_c = cols[:, None] * inv_freq[None, :]
    ang = np.concatenate([ang_r, ang_r, ang_c, ang_c], axis=-1).astype(np.float32)
    if is_sample_slot:
        cos = np.cos(ang).astype(np.float32)
        sin = np.sin(ang).astype(np.float32)
    else:
        cos = np.ones_like(ang)
        sin = np.zeros_like(ang)
    sgn = np.concatenate([-np.ones(16), np.ones(16), -np.ones(16), np.ones(16)]).astype(np.float32)
    sins = sin * sgn[None, :]
    rope = np.stack([cos, sins], 0).reshape(2, 8, 128, 64).transpose(2, 0, 1, 3).copy()
    mid = 63
    wc = np.zeros((2, 128, 130), np.float32)
    mask = np.zeros((2, 128, 128), np.float32)
    s = np.arange(128)[:, None]
    tt = np.arange(128)[None, :]
    fw = np.where((s > mid) & (s <= tt), 1.0, 0.0) - np.where((s <= mid) & (s > tt), 1.0, 0.0)
    wc[0, :, :128] = fw
    mask[0] = (s <= tt)
    mb = 64
    bw = np.where((s < mb) & (s >= tt), 1.0, 0.0) - np.where((s >= mb) & (s < tt), 1.0, 0.0)
    wc[1, :, :128] = bw
    mask[1] = (s >= tt)
    sel = np.zeros((2, 2, 128), np.float32)
    wc = np.ascontiguousarray(wc.transpose(1, 0, 2))
    mask = np.ascontiguousarray(mask.transpose(1, 0, 2))
    bias = np.zeros((128, 8, 10), np.float32)
    rf = np.ones((128, 8, 2), np.float32)
    if not is_sample_slot:
        NEG = -30000.0
        bias[:, :, 0:2] = NEG
        for ti in range(8):
            for t_ in range(8):
                if t_ // 2 != ti // 2:
                    bias[:, ti, 2 + t_] = NEG
        for lt in range(8):
            if lt % 2 == 0:
                rf[:, lt, 0] = 0.0
            else:
                rf[:, lt, 1] = 0.0
    return dict(c_ident=ident, c_rope=rope.astype(np.float32), c_wc=wc, c_mask=mask, c_sel=sel,
                c_bias=bias, c_rf=rf)


_NC_CACHE = {}


def kernel(x_prompt, x_sample, cache_k, cache_v, state_rec, c, c_ctx,
           ada_w, ada_b, attn_w_in, attn_q_gain, attn_k_gain, attn_w_out,
           rec_w_in, rec_lower_bounds, rec_norm_gain, rec_w_out, ln_gain, ln_bias, _n_layers=2, _stage=99):
    f = lambda a: np.ascontiguousarray(np.asarray(a, dtype=np.float32))
    x_prompt, x_sample, cache_k, cache_v, state_rec = map(f, (x_prompt, x_sample, cache_k, cache_v, state_rec))
    c, c_ctx = f(c), f(c_ctx)
    consts = [_consts(True), _consts(False)]
    shared = dict(ada_w=f(ada_w), ada_b=f(ada_b), attn_w_in=f(attn_w_in)[0], q_gain=f(attn_q_gain),
                  k_gain=f(attn_k_gain), attn_w_out=f(attn_w_out)[0], rec_w_in=f(rec_w_in)[0],
                  lbr=np.ascontiguousarray(f(rec_lower_bounds).reshape(2, 2, 8, 128).transpose(3, 0, 1, 2)),
                  adabT=np.ascontiguousarray(f(ada_b).reshape(2, 24, 128).transpose(2, 0, 1)), rec_gain=f(rec_norm_gain), rec_w_out=f(rec_w_out)[0],
                  ln_gain=f(ln_gain), ln_bias=f(ln_bias))
    in_maps = []
    for core in range(8):
        m = dict(shared)
        if core < 4:
            s = core
            p0 = 2 * core
            xa = np.concatenate([x_prompt[p0:p0 + 2].reshape(512, D), x_sample[s]], axis=0)
            m.update(consts[0])
            m.update(cache_k=np.ascontiguousarray(cache_k[s, 0]), cache_v=np.ascontiguousarray(cache_v[s, 0]),
                     state0=np.ascontiguousarray(state_rec[s, 0]), cond_rows=np.stack([c_ctx, c[s]], 0))
        else:
            k = core - 4
            p0 = 8 + 2 * k
            p1 = 16 + 4 * k
            xa = np.concatenate([x_prompt[p0:p0 + 2].reshape(512, D), x_prompt[p1:p1 + 4].reshape(1024, D)], axis=0)
            m.update(consts[1])
            m.update(cache_k=np.zeros((256, 4, 64), np.float32), cache_v=np.zeros((256, 4, 64), np.float32),
                     state0=np.zeros((2, 8, 128, 128), np.float32), cond_rows=np.stack([c_ctx, c_ctx], 0))
        cr = m.pop("cond_rows")
        m.update(x_all=np.ascontiguousarray(xa),
                 condT=np.ascontiguousarray(cr.reshape(2, 8, 128).transpose(2, 1, 0)))
        in_maps.append(m)
    key = (_n_layers, _stage)
    if key not in _NC_CACHE:
        _NC_CACHE[key] = build_program(_n_layers, _stage)
    nc = _NC_CACHE[key]
    res = run_bass_kernel_spmd(nc, in_maps, core_ids=list(range(8)))
    R = res.results
    y_prompt = np.zeros((32, 256, D), np.float32)
    nk = np.zeros((32, 1, 256, 4, 64), np.float32)
    nv = np.zeros((32, 1, 256, 4, 64), np.float32)
    ns = np.zeros((32, 1, 2, 8, 128, 128), np.float32)
    y_sample = np.zeros((4, 1024, D), np.float32)
    for core in range(8):
        r = R[core]
        p0 = 2 * core if core < 4 else 8 + 2 * (core - 4)
        y_prompt[p0:p0 + 2] = r["y_out"][:512].reshape(2, 256, D)
        nk[p0:p0 + 2, 0] = r["nk_out"]
        nv[p0:p0 + 2, 0] = r["nv_out"]
        ns[p0:p0 + 2, 0] = r["ns_out"]
        if core < 4:
            y_sample[core] = r["y_out"][512:]
        else:
            p1 = 16 + 4 * (core - 4)
            y_prompt[p1:p1 + 4] = r["y_out"][512:].reshape(4, 256, D)
            nk[p1:p1 + 4, 0] = r["nk1_out"].reshape(4, 256, 4, 64)
            nv[p1:p1 + 4, 0] = r["nv1_out"].reshape(4, 256, 4, 64)
            ns[p1:p1 + 4, 0] = r["ns1_out"]
    return (y_prompt, y_sample, nk, nv, ns)
```

```python
import contextlib
import numpy as np
import concourse.bass as bass
import concourse.mybir as mybir
from concourse.bass_utils import run_bass_kernel_spmd

F32 = mybir.dt.float32
BF16 = mybir.dt.bfloat16
AF = mybir.ActivationFunctionType
ALU = mybir.AluOpType
AX = mybir.AxisListType

D = 1024
NT = 12
NTOK = NT * 128
ALPHA = 4.0 ** 0.25
NORM_EPS = 1e-6
LN_EPS = 1e-5
SEQS = [(0, 2, False), (2, 2, False), (4, 8, True)]
SI_S = 2
PB = (0, 4, False)
DEBUG = False
import os
SKIP = os.environ.get('KSKIP', '')


class Dep:
    __slots__ = ("w", "r", "name", "excl")

    def __init__(self, name="", excl=False):
        self.excl = excl
        self.w = {}
        self.r = {}
        self.name = name


class Sig:
    def __init__(self, name, sem, h=None):
        self.name = name
        self.sem = sem
        self.h = h
        self.n = 0
        self.seen = {}


class KB:
    def __init__(self, nc, es):
        self.nc = nc
        self.es = es
        self.root = es
        self.nsem = 0
        self.pe = self.eng("pe", nc.tensor)
        self.act = self.eng("act", nc.scalar)
        self.dve = self.eng("dve", nc.vector)
        self.pool = self.eng("pool", nc.gpsimd)
        self.sp = self.eng("sp", nc.sync)
        self.dsems = []
        self.uid = 0

    def newsem(self, name):
        self.nsem += 1
        return self.root.enter_context(self.nc.semaphore(f"{name}_{self.nsem}"))

    def eng(self, name, h):
        return Sig(name, self.newsem(name), h)

    def dsem(self, name):
        s = Sig(name, self.newsem(name))
        self.dsems.append(s)
        return s

    def sb(self, name, shape, dt):
        self.uid += 1
        return self.es.enter_context(self.nc.sbuf_tensor(f"{name}_{self.uid}", list(shape), dt))

    def _waits(self, e, reads, writes):
        need = {}
        for d in reads:
            for s, v in d.w.items():
                need[s] = max(need.get(s, 0), v)
        for d in writes:
            for s, v in d.w.items():
                need[s] = max(need.get(s, 0), v)
            for s, v in d.r.items():
                if s is e:
                    continue
                need[s] = max(need.get(s, 0), v)
        if e is self.pe:
            need.pop(e, None)
        for s, v in need.items():
            if e.seen.get(s, 0) >= v:
                continue
            e.h.wait_ge(s.sem, v)
            e.seen[s] = v

    def op(self, e, fn, reads=(), writes=(), signal=True):
        ex = [d for d in reads if d.excl]
        if ex:
            writes = list(writes) + ex
        self._waits(e, reads, writes)
        inst = fn()
        if signal:
            e.n += 1
            inst.then_inc(e.sem, 1)
            val = e.n
        else:
            val = e.n + 1
        for d in reads:
            d.r[e] = max(d.r.get(e, 0), val)
        for d in writes:
            d.w[e] = max(d.w.get(e, 0), val)
        return inst

    def dma(self, q, ds, out, in_, reads=(), writes=(), **kw):
        self._waits_dma(q, ds, reads, writes)
        inst = q.h.dma_start(out=out, in_=in_, **kw)
        ds.n += 16
        inst.then_inc(ds.sem, 16)
        for d in reads:
            d.r[ds] = ds.n
        for d in writes:
            d.w[ds] = ds.n
        return inst

    def _waits_dma(self, q, ds, reads, writes):
        need = {}
        for d in reads:
            for s, v in d.w.items():
                need[s] = max(need.get(s, 0), v)
        for d in writes:
            for s, v in d.w.items():
                if s is not ds:
                    need[s] = max(need.get(s, 0), v)
            for s, v in d.r.items():
                need[s] = max(need.get(s, 0), v)
        for s, v in need.items():
            if q.seen.get(s, 0) >= v:
                continue
            q.h.wait_ge(s.sem, v)
            q.seen[s] = v

    def barrier(self):
        sigs = [self.pe, self.act, self.dve, self.pool] + self.dsems
        for e in (self.pe, self.act, self.dve, self.pool, self.sp):
            for s_ in sigs:
                if s_ is e or s_.n == 0:
                    continue
                if e.seen.get(s_, 0) >= s_.n:
                    continue
                e.h.wait_ge(s_.sem, s_.n)
                e.seen[s_] = s_.n

    def finish(self):
        for s in self.dsems:
            if s.n > 0:
                self.sp.h.wait_ge(s.sem, s.n)
        for e in (self.pe, self.act, self.dve, self.pool):
            if e.n > 0:
                self.sp.h.wait_ge(e.sem, e.n)


def build_program(n_layers=2, stg=99):
    nc = bass.Bass("TRN2", target_bir_lowering=False, dynamic_dma_scratch_size=8192)

    def din(name, shape, dt=F32):
        return nc.dram_tensor(name, list(shape), dt, kind="ExternalInput").ap()

    def dout(name, shape, dt=F32):
        return nc.dram_tensor(name, list(shape), dt, kind="ExternalOutput").ap()

    x_all = din("x_all", [NTOK, D])
    cache_k = din("cache_k", [256, 4, 64])
    cache_v = din("cache_v", [256, 4, 64])
    state0 = din("state0", [2, 8, 128, 128])
    cond = din("condT", [128, 8, 2])
    ada_w = din("ada_w", [2, D, 3 * D])
    ada_b = din("ada_b", [2, 3 * D])
    adabT_in = din("adabT", [128, 2, 24])
    attn_w_in = din("attn_w_in", [D, 2560])
    q_gain = din("q_gain", [1, 64])
    k_gain = din("k_gain", [1, 64])
    attn_w_out = din("attn_w_out", [D, D])
    rec_w_in = din("rec_w_in", [D, 5120])
    rec_lb = din("lbr", [128, 2, 2, 8])
    rec_gain = din("rec_gain", [1, 128])
    rec_w_out = din("rec_w_out", [D, D])
    ln_gain = din("ln_gain", [2, D])
    ln_bias = din("ln_bias", [2, D])
    c_ident = din("c_ident", [128, 128])
    c_rope = din("c_rope", [128, 2, 8, 64])
    c_wc = din("c_wc", [128, 2, 130])
    c_mask = din("c_mask", [128, 2, 128])
    c_sel = din("c_sel", [2, 2, 128])
    c_bias = din("c_bias", [128, 8, 10])
    c_rf = din("c_rf", [128, 8, 2])

    y_out = dout("y_out", [NTOK, D])
    nk_out = dout("nk_out", [2, 256, 4, 64])
    nv_out = dout("nv_out", [2, 256, 4, 64])
    ns_out = dout("ns_out", [2, 2, 8, 128, 128])
    nk1_out = dout("nk1_out", [1024, 4, 64])
    nv1_out = dout("nv1_out", [1024, 4, 64])
    ns1_out = dout("ns1_out", [4, 2, 8, 128, 128])
    if DEBUG:
        x1_dram = dout("x1_dram", [NTOK, D])
    else:
        x1_dram = nc.dram_tensor("x1_dram", [NTOK, D], F32, kind="Internal").ap()

    es = contextlib.ExitStack()
    with es:
        kb = KB(nc, es)
        PE, ACT, DVE, POOL, SP = kb.pe, kb.act, kb.dve, kb.pool, kb.sp
        T, V, S, G = nc.tensor, nc.vector, nc.scalar, nc.gpsimd

        banks = []
        bdeps = []
        for i in range(8):
            banks.append(es.enter_context(nc.psum_tensor(f"bank{i}", [128, 512], F32)))
            bdeps.append(Dep(f"bank{i}", excl=True))

        hT = kb.sb("hT", [128, 8, NTOK], BF16)
        hT_d = [Dep(f"hT{i}") for i in range(NT)]
        gT = kb.sb("gT", [128, 8, NTOK], BF16)
        gT_d = [Dep(f"gT{i}") for i in range(NT)]
        Wsl = [kb.sb("W", [128, 8, 640], BF16) for _ in range(2)]
        Wsl_d = [Dep("W0"), Dep("W1")]
        Wsl_s = [kb.dsem("W0"), kb.dsem("W1")]
        wout = kb.sb("wout", [128, 8, D], BF16)
        wout_d = Dep("wout")
        wout_s = kb.dsem("wout")
        xt_d = [Dep(f"xt{i}") for i in range(4)]
        xt_s = [kb.dsem(f"xt{i}") for i in range(4)]
        vt_d = [Dep("vt0"), Dep("vt1")]
        yt_d = [Dep(f"yt{i}") for i in range(4)]
        yt_s = [kb.dsem(f"yt{i}") for i in range(4)]
        gate_rep_d = [Dep("gr0"), Dep("gr1")]
        lnrep_d = Dep("lnrep")
        lnrep_s = kb.dsem("lnrep")
        cs = kb.dsem("const")
        cd = Dep("const")
        ident_f = kb.sb("ident_f", [128, 128], F32)
        ident_b = kb.sb("ident_b", [128, 128], BF16)
        wc = kb.sb("wc", [128, 2, 130], F32)
        maskt = kb.sb("mask", [128, 2, 128], F32)
        gain5 = kb.sb("gain5", [128, 5, 64], F32)
        bias_t = kb.sb("bias_t", [128, 8, 10], F32)
        rf_t = kb.sb("rf_t", [128, 8, 2], F32)
        rgain = kb.sb("rgain", [128, 128], F32)
        condT = kb.sb("condT", [128, 8, 2], F32)
        scT = kb.sb("scT", [128, 8, 2], F32)
        adabT = kb.sb("adabT", [128, 2, 24], F32)
        modT = kb.sb("modT", [128, 2, 16, 2], F32)
        modT_d = Dep("modT")
        gate_dram = nc.dram_tensor("gate_dram", [2, 2, D], F32, kind="Internal").ap()
        gate_s = kb.dsem("gate")
        gate_rep_s = [kb.dsem("grep0"), kb.dsem("grep1")]
        gate_row_d = Dep("gate_row")
        lbr = kb.sb("lbr", [128, 2, 2, 8], F32)
        lbm1 = kb.sb("lbm1", [128, 2, 8], F32)
        oml = kb.sb("oml", [128, 2, 8], F32)
        lb_d = Dep("lb")

        def cload(dst, src, **kw):
            kb.dma(SP, cs, dst, src, writes=[cd], **kw)

        cload(ident_f[:], c_ident)
        cload(bias_t[:], c_bias)
        cload(rf_t[:], c_rf)
        cload(wc[:], c_wc)
        cload(maskt[:], c_mask)
        for h in range(4):
            cload(gain5[:, h, :], q_gain[0:1, :].to_broadcast([128, 64]))
        cload(gain5[:, 4, :], k_gain[0:1, :].to_broadcast([128, 64]))
        cload(rgain[:], rec_gain[0:1, :].to_broadcast([128, 128]))
        cload(condT[:], cond)
        cload(adabT[:], adabT_in)
        cload(lbr[:], rec_lb)

        cd2 = Dep("const2")
        kb.op(ACT, lambda: S.copy(out=ident_b[:], in_=ident_f[:]), reads=[cd], writes=[cd2])
        kb.op(ACT, lambda: S.mul(out=gain5[:], in_=gain5[:], mul=8.0), reads=[cd], writes=[cd2])
        kb.op(ACT, lambda: S.activation(out=scT[:], in_=condT[:], func=AF.Silu), reads=[cd], writes=[cd2])
        kb.op(DVE, lambda: V.tensor_tensor(out=lbm1[:], in0=lbr[:, 1, :, :], in1=lbr[:, 0, :, :], op=ALU.subtract),
              reads=[cd], writes=[lb_d])
        kb.op(ACT, lambda: S.activation(out=oml[:], in_=lbm1[:], func=AF.Sigmoid, scale=-1.0),
              reads=[lb_d], writes=[lb_d])
        kb.op(DVE, lambda: V.tensor_scalar(out=lbm1[:], in0=oml[:], scalar1=-1.0, scalar2=None, op0=ALU.mult),
              reads=[lb_d], writes=[lb_d])

        with contextlib.ExitStack() as es_p:
            kb.es = es_p
            stage = [kb.sb("adastage", [128, 8, 512], F32) for _ in range(2)]
            stage_d = [Dep("st0"), Dep("st1")]
            stage_s = [kb.dsem("st0"), kb.dsem("st1")]
            rowtmp = [kb.sb("rowtmp", [2, 512], F32) for _ in range(2)]
            adab_row = kb.sb("adab_row", [2, 2, D], F32)
            grow = kb.sb("grow", [2, 2, D], F32)
            grow_d = Dep("grow")
            adab_d = Dep("adab_row")
            adab_s = kb.dsem("adab_row")
            rowtmp_d = [Dep("rowtmp0"), Dep("rowtmp1")]
            kb.es = es
            for r in range(2):
                kb.dma(SP, adab_s, adab_row[r:r + 1, :, :], ada_b[:, 2 * D:3 * D].rearrange("(o l) n -> o l n", o=1),
                       writes=[adab_d])
            it = 0
            for l in range(2):
                aw = ada_w[l].rearrange("(k p) n -> p k n", p=128)
                for j in range(6):
                    sl = it % 2
                    it += 1
                    kb.dma(SP, stage_s[sl], stage[sl][:], aw[:, :, j * 512:(j + 1) * 512], writes=[stage_d[sl]])
                    if j < 4:
                        bk = 0
                        rsl = it % 2
                        for k in range(8):
                            kb.op(PE, lambda k=k, sl=sl: T.matmul(
                                banks[2][0:2, :], lhsT=scT[:, k, :], rhs=stage[sl][:, k, :],
                                start=(k == 0), stop=(k == 7)),
                                reads=[stage_d[sl], cd2], writes=[bdeps[2]], signal=(k == 7))
                        kb.op(ACT, lambda rsl=rsl: S.copy(out=rowtmp[rsl][:, :], in_=banks[2][0:2, :]),
                              reads=[bdeps[2]], writes=[rowtmp_d[rsl]])
                        for sub in range(4):
                            cch = j * 4 + sub
                            kb.op(PE, lambda sub=sub, cch=cch, rsl=rsl: T.transpose(
                                out=banks[bk][:, cch * 2:cch * 2 + 2], in_=rowtmp[rsl][:, sub * 128:(sub + 1) * 128],
                                identity=ident_f[0:2, 0:2]), reads=[rowtmp_d[rsl], cd], writes=[bdeps[bk]],
                                signal=(sub == 3))
                        if j == 3:
                            kb.op(DVE, lambda l=l: V.tensor_tensor(
                                out=modT[:, l, :, :], in0=banks[0][:, 0:32].rearrange("p (c o) -> p c o", o=2),
                                in1=adabT[:, l, 0:16].unsqueeze(2).to_broadcast([128, 16, 2]), op=ALU.add),
                                reads=[bdeps[0], cd], writes=[modT_d])
                            kb.op(DVE, lambda l=l: V.tensor_scalar(
                                out=modT[:, l, 8:16, :], in0=modT[:, l, 8:16, :], scalar1=1.0, scalar2=None,
                                op0=ALU.add), reads=[modT_d], writes=[modT_d])
                    else:
                        bk = 1
                        nb = j - 4
                        for k in range(8):
                            kb.op(PE, lambda k=k, sl=sl: T.matmul(
                                banks[bk][0:2, :], lhsT=scT[:, k, :], rhs=stage[sl][:, k, :],
                                start=(k == 0), stop=(k == 7)),
                                reads=[stage_d[sl], cd2], writes=[bdeps[bk]], signal=(k == 7))
                        kb.op(DVE, lambda l=l, nb=nb: V.tensor_tensor(
                            out=grow[:, l, nb * 512:(nb + 1) * 512], in0=banks[1][0:2, :],
                            in1=adab_row[:, l, nb * 512:(nb + 1) * 512], op=ALU.add),
                            reads=[bdeps[1], adab_d], writes=[grow_d])
                        if l == 1 and nb == 1:
                            kb.dma(SP, gate_s, gate_dram, grow[:], reads=[grow_d], writes=[gate_row_d])
            kb.barrier()

        ev_rr = [0]

        def make_hT(src_ap, src_dep, i, c, l, pb, all_act=False):
            for half in range(2):
                bk = pb[half]
                for q in range(4):
                    k = half * 4 + q
                    kb.op(PE, lambda k=k, q=q, bk=bk: T.transpose(
                        out=banks[bk][:, q * 128:(q + 1) * 128], in_=src_ap[:, k * 128:(k + 1) * 128],
                        identity=ident_f[:]), reads=[src_dep, cd], writes=[bdeps[bk]], signal=(q == 3))
                for q in range(4):
                    k = half * 4 + q
                    if half == 0 or all_act:
                        kb.op(ACT, lambda k=k, q=q, bk=bk: S.activation(
                            out=hT[:, k, i * 128:(i + 1) * 128], in_=banks[bk][:, q * 128:(q + 1) * 128],
                            func=AF.Identity, scale=modT[:, l, 8 + k, c:c + 1], bias=modT[:, l, k, c:c + 1]),
                            reads=[bdeps[bk], modT_d], writes=[hT_d[i]])
                    else:
                        kb.op(DVE, lambda k=k, q=q, bk=bk: V.tensor_scalar(
                            out=hT[:, k, i * 128:(i + 1) * 128], in0=banks[bk][:, q * 128:(q + 1) * 128],
                            scalar1=modT[:, l, 8 + k, c:c + 1], scalar2=modT[:, l, k, c:c + 1],
                            op0=ALU.mult, op1=ALU.add),
                            reads=[bdeps[bk], modT_d], writes=[hT_d[i]])

        def load_w(slot, pieces):
            for (c0, ncol, src) in pieces:
                kb.dma(POOL, Wsl_s[slot], Wsl[slot][:, :, c0:c0 + ncol],
                       src.rearrange("(k p) n -> p k n", p=128), writes=[Wsl_d[slot]])

        def prefetch_wout(w_out_ap):
            kb.dma(POOL, wout_s, wout[:], w_out_ap.rearrange("(k p) n -> p k n", p=128), writes=[wout_d])

        def epilogue(l, w_out_ap, res_src, last):
            with contextlib.ExitStack() as es_e:
                kb.es = es_e
                xt = [kb.sb("xt", [128, D], F32) for _ in range(4)]
                vt = [kb.sb("vt", [128, D], F32) for _ in range(2)]
                yt = [kb.sb("yt", [128, D], F32) for _ in range(4)]
                gate_rep = [kb.sb("gate_rep", [128, D], F32) for _ in range(2)]
                lnrep = kb.sb("lnrep", [128, 2, D], F32)
                _epilogue(l, w_out_ap, res_src, last, xt, vt, yt, gate_rep, lnrep, wout)
                kb.barrier()
            kb.es = es

        def _epilogue(l, w_out_ap, res_src, last, xt, vt, yt, gate_rep, lnrep, wout):
            kb.dma(SP, lnrep_s, lnrep[:, 0, :], ln_gain[l:l + 1, :].to_broadcast([128, D]), writes=[lnrep_d])
            kb.dma(SP, lnrep_s, lnrep[:, 1, :], ln_bias[l:l + 1, :].to_broadcast([128, D]), writes=[lnrep_d])
            for c in range(2):
                kb.dma(SP, gate_rep_s[c], gate_rep[c][:], gate_dram[c:c + 1, l, :].to_broadcast([128, D]),
                       reads=[gate_row_d], writes=[gate_rep_d[c]])
            stats = [kb.sb("stats", [128, 2, 6], F32) for _ in range(2)]
            mv = [kb.sb("mv", [128, 2], F32) for _ in range(2)]
            rstd = [kb.sb("rstd", [128, 1], F32) for _ in range(2)]
            nmr = [kb.sb("nmr", [128, 1], F32) for _ in range(2)]
            st_d = [Dep("stats0"), Dep("stats1")]
            def xload(i):
                if i < NT:
                    kb.dma(SP, xt_s[i % 4], xt[i % 4][:], res_src[i * 128:(i + 1) * 128, :], writes=[xt_d[i % 4]])

            def stage1(i):
                c = 0 if i < 4 else 1
                xs = i % 4
                p = i % 2
                xload(i + 3)
                for nb in range(2):
                    bk = (i % 2) * 2 + nb
                    for k in range(8):
                        kb.op(PE, lambda: T.matmul(
                            banks[bk][:, :], lhsT=gT[:, k, i * 128:(i + 1) * 128],
                            rhs=wout[:, k, nb * 512:(nb + 1) * 512], start=(k == 0), stop=(k == 7)),
                            reads=[gT_d[i], wout_d], writes=[bdeps[bk]], signal=(k == 7))
                    kb.op(DVE, lambda: V.tensor_tensor(
                        out=vt[p][:, nb * 512:(nb + 1) * 512], in0=banks[bk][:, :],
                        in1=gate_rep[c][:, nb * 512:(nb + 1) * 512], op=ALU.mult),
                        reads=[bdeps[bk], gate_rep_d[c]], writes=[vt_d[p]])
                kb.op(DVE, lambda: V.scalar_tensor_tensor(
                    out=vt[p][:], in0=xt[xs][:], scalar=ALPHA, in1=vt[p][:], op0=ALU.mult, op1=ALU.add),
                    reads=[xt_d[xs], vt_d[p]], writes=[vt_d[p]])
                kb.op(DVE, lambda: V.bn_stats(out=stats[p][:, 0, :], in_=vt[p][:, 0:512]), reads=[vt_d[p]], writes=[st_d[p]])
                kb.op(DVE, lambda: V.bn_stats(out=stats[p][:, 1, :], in_=vt[p][:, 512:1024]), reads=[vt_d[p]], writes=[st_d[p]])
                kb.op(DVE, lambda: V.bn_aggr(out=mv[p][:], in_=stats[p][:]), reads=[st_d[p]], writes=[st_d[p]])
                kb.op(DVE, lambda: V.tensor_scalar(out=rstd[p][:], in0=mv[p][:, 1:2], scalar1=LN_EPS, scalar2=None,
                                                   op0=ALU.add), reads=[st_d[p]], writes=[st_d[p]])
                kb.op(ACT, lambda: S.sqrt(out=rstd[p][:], in_=rstd[p][:]), reads=[st_d[p]], writes=[st_d[p]])

            def stage2(i):
                c = 0 if i < 4 else 1
                ys = i % 4
                p = i % 2
                kb.op(DVE, lambda: V.reciprocal(out=rstd[p][:], in_=rstd[p][:]), reads=[st_d[p]], writes=[st_d[p]])
                kb.op(DVE, lambda: V.scalar_tensor_tensor(out=nmr[p][:], in0=mv[p][:, 0:1], scalar=-1.0, in1=rstd[p][:],
                                                          op0=ALU.mult, op1=ALU.mult), reads=[st_d[p]], writes=[st_d[p]])
                kb.op(ACT, lambda: S.activation(out=yt[ys][:], in_=vt[p][:], func=AF.Identity,
                                                scale=rstd[p][:, 0:1], bias=nmr[p][:, 0:1]),
                      reads=[vt_d[p], st_d[p]], writes=[yt_d[ys]])
                kb.op(POOL, lambda: G.tensor_tensor(out=yt[ys][:], in0=yt[ys][:], in1=lnrep[:, 0, :], op=ALU.mult),
                      reads=[yt_d[ys], lnrep_d], writes=[yt_d[ys]])
                kb.op(POOL, lambda: G.tensor_tensor(out=yt[ys][:], in0=yt[ys][:], in1=lnrep[:, 1, :], op=ALU.add),
                      reads=[yt_d[ys], lnrep_d], writes=[yt_d[ys]])
                if last:
                    kb.dma(SP, yt_s[ys], y_out[i * 128:(i + 1) * 128, :], yt[ys][:], reads=[yt_d[ys]])
                else:
                    kb.dma(SP, yt_s[ys], x1_dram[i * 128:(i + 1) * 128, :], yt[ys][:], reads=[yt_d[ys]])

            def stage3(i):
                if not last and 0 <= i < NT:
                    make_hT(yt[i % 4], yt_d[i % 4], i, 0 if i < 4 else 1, l + 1, (6, 7) if i % 2 == 0 else (4, 5), all_act=True)

            xload(0)
            xload(1)
            xload(2)
            stage1(0)
            for i in range(NT):
                if i + 1 < NT:
                    stage1(i + 1)
                stage2(i)
                stage3(i - 1)
            stage3(NT - 1)

        def w1_pieces(h):
            return [(j * 128, 128, rec_w_in[:, j * 1024 + h * 128:j * 1024 + (h + 1) * 128]) for j in range(5)]

        def layer1():
            with contextlib.ExitStack() as es1:
                kb.es = es1
                sqT = [kb.sb("sqT", [128, 512], F32) for _ in range(2)]
                sqT_d = [Dep("sqT0"), Dep("sqT1")]
                snT = [kb.sb("snT", [128, 2, 512], F32) for _ in range(2)]
                snT_d = [Dep("snT0"), Dep("snT1")]
                lfT = [kb.sb("lfT", [128, 2, 512], F32) for _ in range(2)]
                lfT_d = [Dep("lfT0"), Dep("lfT1")]
                lf_tok = kb.sb("lf_tok", [128, 2, 4, 128], F32); lftok_d = [Dep("lftok0"), Dep("lftok1")]
                epos = kb.sb("epos", [128, 2, 512], F32); eneg = kb.sb("eneg", [128, 2, 512], F32)
                e_d = [Dep("e0"), Dep("e1")]
                ftmp = kb.sb("ftmp", [128, 2, 4], F32); ftmp_d = [Dep("ftmp0"), Dep("ftmp1")]
                sgT1 = [kb.sb("sgT1", [128, NTOK], BF16) for _ in range(2)]
                qtT = [kb.sb("qtT", [128, 2, NTOK], BF16) for _ in range(2)]
                ktT = [kb.sb("ktT", [128, 2, NTOK], BF16) for _ in range(2)]
                kt_tok = [kb.sb("kt_tok", [128, NT, 2, 128], BF16) for _ in range(2)]
                vtok = [kb.sb("vtok", [128, NT, 128], BF16) for _ in range(2)]
                esc = [kb.sb("esc", [128, NT, 2, 2], F32) for _ in range(2)]
                elast = [kb.sb("elast", [128, NT, 2], F32) for _ in range(2)]
                sgT1_d = [Dep("sgT1a"), Dep("sgT1b")]
                qtT_d = [Dep("qtTa"), Dep("qtTb")]
                ktT_d = [Dep("ktTa"), Dep("ktTb")]
                kttok_d = [[Dep("kttoka0"), Dep("kttoka1")], [Dep("kttokb0"), Dep("kttokb1")]]
                vtok_d = [[Dep("vtoka0"), Dep("vtoka1")], [Dep("vtokb0"), Dep("vtokb1")]]
                esc_d = [Dep("esca"), Dep("escb")]
                NCH = 10
                AT_bf = kb.sb("AT_bf", [128, NCH, 128], BF16); AT_d = [Dep(f"AT{i}") for i in range(NCH)]
                St = kb.sb("St", [128, NCH, 128], F32); St_d = [Dep(f"S{i}") for i in range(NCH)]
                St_s = [kb.dsem(f"S{i}") for i in range(NCH)]
                Sbf = kb.sb("Sbf", [128, NCH, 128], BF16); Sbf_d = [Dep(f"Sbf{i}") for i in range(NCH)]
                sstage = kb.sb("sstage", [128, 4, 128], F32); sstage_d = [Dep(f"sst{i}") for i in range(4)]
                sstage_s = [kb.dsem(f"sst{i}") for i in range(4)]
                o_acc = kb.sb("o_acc", [128, NT, 128], F32); oacc_d = [Dep(f"oacc{i}") for i in range(NT)]
                oss = kb.sb("oss", [128, NT], F32); oss_d = [Dep("oss0"), Dep("oss1")]
                kb.es = es

                pcount = [0]

                def stageA(h, c):
                    hp = h % 2
                    par = c % 2
                    W = Wsl[hp]
                    Wd = Wsl_d[hp]
                    tok0 = c * 512
                    hds = [hT_d[c * 4 + q] for q in range(4)]

                    def proj_fm(col0):
                        pcount[0] += 1
                        bk = pcount[0] % 2
                        for k in range(8):
                            kb.op(PE, lambda: T.matmul(
                                banks[bk][:, :], lhsT=W[:, k, col0:col0 + 128],
                                rhs=hT[:, k, tok0:tok0 + 512], start=(k == 0), stop=(k == 7)),
                                reads=hds + [Wd], writes=[bdeps[bk]], signal=(k == 7))
                        return bk
                    bk = proj_fm(0)
                    kb.op(ACT, lambda: S.activation(out=sqT[par][:], in_=banks[bk][:, :], func=AF.Silu),
                          reads=[bdeps[bk]], writes=[sqT_d[par]])
                    yield
                    bk = proj_fm(512)
                    kb.op(ACT, lambda: S.activation(out=sgT1[hp][:, tok0:tok0 + 512], in_=banks[bk][:, :], func=AF.Silu),
                          reads=[bdeps[bk]], writes=[sgT1_d[hp]])
                    yield
                    for d in range(2):
                        bk = proj_fm(128 + d * 128)
                        kb.op(ACT, lambda: S.activation(out=snT[par][:, d, :], in_=banks[bk][:, :],
                                                        func=AF.Sigmoid, scale=-1.0),
                              reads=[bdeps[bk]], writes=[snT_d[par]])
                        yield
                    for d in range(2):
                        kb.op(ACT, lambda: S.activation(out=lfT[par][:, d, :], in_=snT[par][:, d, :],
                                                        func=AF.Ln, scale=lbm1[:, d, h:h + 1], bias=1.0),
                              reads=[snT_d[par], lb_d], writes=[lfT_d[par]])
                    pcount[0] += 1
                    bk = pcount[0] % 2
                    for j in range(4):
                        for k in range(8):
                            kb.op(PE, lambda: T.matmul(
                                banks[bk][:, j * 128:(j + 1) * 128],
                                lhsT=hT[:, k, tok0 + j * 128:tok0 + (j + 1) * 128],
                                rhs=W[:, k, 384:512], start=(k == 0), stop=(k == 7)),
                                reads=hds + [Wd], writes=[bdeps[bk]], signal=(k == 7))
                    kb.op(DVE, lambda: V.tensor_copy(
                        out=vtok[hp][:, c * 4:c * 4 + 4, :], in_=banks[bk][:, :].rearrange("p (j d) -> p j d", d=128)),
                        reads=[bdeps[bk]], writes=[vtok_d[hp][min(c, 1)]])
                    yield

                def stageB1(h, c):
                    hp = h % 2
                    par = c % 2
                    tok0 = c * 512
                    for d in range(2):
                        for j in range(4):
                            kb.op(PE, lambda: T.transpose(
                                out=banks[3 + d][:, j * 128:(j + 1) * 128], in_=lfT[par][:, d, j * 128:(j + 1) * 128],
                                identity=ident_f[:]), reads=[lfT_d[par], cd], writes=[bdeps[3 + d]], signal=(j == 3))
                    for d in range(2):
                        kb.op(DVE, lambda: V.tensor_copy(
                            out=lf_tok[:, d, :, :], in_=banks[3 + d][:, :].rearrange("p (j d) -> p j d", d=128)),
                            reads=[bdeps[3 + d]], writes=[lftok_d[d]])
                    yield
                    for d in range(2):
                        for j in range(4):
                            kb.op(PE, lambda: T.matmul(
                                banks[3 + d][:, j * 128:(j + 1) * 128], lhsT=lf_tok[:, d, j, :],
                                rhs=wc[:, d, 0:128], start=True, stop=True),
                                reads=[lftok_d[d], cd], writes=[bdeps[3 + d]], signal=(j == 3))
                    for d in range(2):
                        kb.op(ACT, lambda: S.activation(out=epos[:, d, :], in_=banks[3 + d][:, :], func=AF.Exp),
                              reads=[bdeps[3 + d]], writes=[e_d[d]])
                        kb.op(ACT, lambda: S.activation(out=eneg[:, d, :], in_=banks[3 + d][:, :], func=AF.Exp,
                                                        scale=-1.0),
                              reads=[bdeps[3 + d]], writes=[e_d[d]])
                    yield
                    for d in range(2):
                        c0_, c1_ = (0, 127) if d == 0 else (127, 0)
                        snc = snT[par][:, d, :].rearrange("p (j t) -> p j t", t=128)
                        kb.op(DVE, lambda: V.tensor_scalar(out=ftmp[:, d, :], in0=snc[:, :, c0_],
                                                           scalar1=lbm1[:, d, h:h + 1],
                                                           scalar2=1.0, op0=ALU.mult, op1=ALU.add),
                              reads=[snT_d[par], lb_d], writes=[ftmp_d[d]])
                        env = eneg[:, d, :].rearrange("p (j t) -> p j t", t=128)
                        epv = epos[:, d, :].rearrange("p (j t) -> p j t", t=128)
                        kb.op(DVE, lambda: V.tensor_tensor(out=esc[hp][:, c * 4:c * 4 + 4, d, 0], in0=env[:, :, c0_],
                                                           in1=ftmp[:, d, :], op=ALU.mult),
                              reads=[e_d[d], ftmp_d[d]], writes=[esc_d[hp]])
                        kb.op(DVE, lambda: V.tensor_copy(out=esc[hp][:, c * 4:c * 4 + 4, d, 1], in_=epv[:, :, c1_]),
                              reads=[e_d[d]], writes=[esc_d[hp]])
                        if c >= 1:
                            kb.op(DVE, lambda: V.tensor_tensor(
                                out=esc[hp][:, c * 4:c * 4 + 4, d, 0], in0=esc[hp][:, c * 4:c * 4 + 4, d, 0],
                                in1=rf_t[:, (c - 1) * 4:(c - 1) * 4 + 4, d], op=ALU.mult),
                                reads=[esc_d[hp], cd], writes=[esc_d[hp]])
                        kb.op(DVE, lambda: V.tensor_tensor(
                            out=qtT[hp][:, d, tok0:tok0 + 512], in0=sqT[par][:], in1=epos[:, d, :], op=ALU.mult),
                            reads=[sqT_d[par], e_d[d]], writes=[qtT_d[hp]])
                        kb.op(DVE, lambda: V.scalar_tensor_tensor(
                            out=ktT[hp][:, d, tok0:tok0 + 512], in0=snT[par][:, d, :], scalar=oml[:, d, h:h + 1],
                            in1=eneg[:, d, :], op0=ALU.mult, op1=ALU.mult),
                            reads=[snT_d[par], e_d[d], lb_d], writes=[ktT_d[hp]])
                    if c == 2:
                        kb.op(DVE, lambda: V.tensor_tensor(out=elast[hp][:], in0=esc[hp][:, :, :, 0],
                                                           in1=esc[hp][:, :, :, 1], op=ALU.mult),
                              reads=[esc_d[hp]], writes=[esc_d[hp]])
                    yield

                def stageB2(h, c):
                    hp = h % 2
                    tok0 = c * 512
                    for d in range(2):
                        bb = banks[3 + d].bitcast(BF16)
                        for j in range(4):
                            kb.op(PE, lambda: T.transpose(
                                out=bb[:, j * 128:(j + 1) * 128],
                                in_=ktT[hp][:, d, tok0 + j * 128:tok0 + (j + 1) * 128], identity=ident_b[:]),
                                reads=[ktT_d[hp], cd2], writes=[bdeps[3 + d]], signal=(j == 3))
                    for d in range(2):
                        bb = banks[3 + d].bitcast(BF16)
                        kb.op(DVE, lambda: V.tensor_copy(
                            out=kt_tok[hp][:, c * 4:c * 4 + 4, d, :],
                            in_=bb[:, 0:512].rearrange("p (j d) -> p j d", d=128)),
                            reads=[bdeps[3 + d]], writes=[kttok_d[hp][min(c, 1)]])
                    yield

                import itertools

                def m1_stream(h):
                    return itertools.chain(stageA(h, 0), stageA(h, 1), stageB1(h, 0), stageA(h, 2), stageB2(h, 0),
                                           stageB1(h, 1), stageB2(h, 1), stageB1(h, 2), stageB2(h, 2))

                def pull(st, n):
                    for _ in range(n):
                        try:
                            next(st)
                        except StopIteration:
                            return

                def m2_setup(h):
                    chains = []
                    for si, (t0, nt, is_s) in enumerate(SEQS):
                        for d in range(2):
                            ci = si * 2 + d
                            if is_s:
                                kb.dma(SP, St_s[ci], St[:, ci, :], state0[d, h, :, :], writes=[St_d[ci]])
                            else:
                                kb.op(POOL, lambda: G.memset(St[:, ci, :], 0.0), writes=[St_d[ci]])
                            order = list(range(nt)) if d == 0 else list(range(nt - 1, -1, -1))
                            chains.append((ci, si, d, [t0 + x for x in order]))
                    return chains

                wcount = [0]
                sgcount = [0]

                def m2_round(h, chains, step, touched, mid_hook):
                    hp = h % 2
                    act = [ch for ch in chains if step < len(ch[3])]
                    waves = [act[i:i + 4] for i in range(0, len(act), 4)]
                    for wi, wave in enumerate(waves):
                        wcount[0] += 1
                        bP = 7 if wcount[0] % 2 == 0 else 2
                        for sl, (ci, si, d, tiles) in enumerate(wave):
                            tl = tiles[step]
                            ts_ = slice(tl * 128, (tl + 1) * 128)
                            cs_ = slice(sl * 128, (sl + 1) * 128)
                            kb.op(PE, lambda: T.matmul(banks[5][:, cs_], lhsT=ktT[hp][:, d, ts_], rhs=qtT[hp][:, d, ts_],
                                                       start=True, stop=True),
                                  reads=[ktT_d[hp], qtT_d[hp]], writes=[bdeps[5]], signal=(sl == len(wave) - 1))
                        for sl, (ci, si, d, tiles) in enumerate(wave):
                            tl = tiles[step]
                            cs_ = slice(sl * 128, (sl + 1) * 128)
                            kb.op(PE, lambda: T.matmul(banks[bP][:, cs_], lhsT=kt_tok[hp][:, tl, d, :],
                                                       rhs=vtok[hp][:, tl, :], start=True, stop=True),
                                  reads=[kttok_d[hp][0 if tl < 4 else 1], vtok_d[hp][0 if tl < 4 else 1]], writes=[bdeps[bP]],
                                  signal=(sl == len(wave) - 1))
                        for sl, (ci, si, d, tiles) in enumerate(wave):
                            tl = tiles[step]
                            cs_ = slice(sl * 128, (sl + 1) * 128)
                            kb.op(ACT, lambda: S.activation(out=Sbf[:, ci, :], in_=St[:, ci, :], func=AF.Copy,
                                                            scale=esc[hp][:, tl, d, 0:1]),
                                  reads=[St_d[ci], esc_d[hp]], writes=[Sbf_d[ci]])
                            kb.op(DVE, lambda: V.tensor_tensor(out=AT_bf[:, ci, :], in0=banks[5][:, cs_],
                                                               in1=maskt[:, d, :], op=ALU.mult),
                                  reads=[bdeps[5], cd], writes=[AT_d[ci]])
                        for sl, (ci, si, d, tiles) in enumerate(wave):
                            tl = tiles[step]
                            kb.op(DVE, lambda: V.tensor_scalar(out=St[:, ci, :], in0=St[:, ci, :],
                                                               scalar1=elast[hp][:, tl, d:d + 1], scalar2=None,
                                                               op0=ALU.mult),
                                  reads=[St_d[ci], esc_d[hp]], writes=[St_d[ci]])
                        if mid_hook is not None:
                            mid_hook(step, wi, False)
                        for sl, (ci, si, d, tiles) in enumerate(wave):
                            tl = tiles[step]
                            ts_ = slice(tl * 128, (tl + 1) * 128)
                            cs_ = slice(sl * 128, (sl + 1) * 128)
                            kb.op(PE, lambda: T.matmul(banks[6][:, cs_], lhsT=AT_bf[:, ci, :], rhs=vtok[hp][:, tl, :],
                                                       start=True, stop=False, skip_group_check=True),
                                  reads=[AT_d[ci], vtok_d[hp][0 if tl < 4 else 1]], writes=[bdeps[6]], signal=False)
                            kb.op(PE, lambda: T.matmul(banks[6][:, cs_], lhsT=qtT[hp][:, d, ts_], rhs=Sbf[:, ci, :],
                                                       start=False, stop=True, skip_group_check=True),
                                  reads=[qtT_d[hp], Sbf_d[ci]], writes=[bdeps[6]], signal=(sl == len(wave) - 1))
                        for sl, (ci, si, d, tiles) in enumerate(wave):
                            tl = tiles[step]
                            cs_ = slice(sl * 128, (sl + 1) * 128)
                            if not touched[tl]:
                                touched[tl] = True
                                kb.op(DVE, lambda: V.tensor_copy(out=o_acc[:, tl, :], in_=banks[6][:, cs_]),
                                      reads=[bdeps[6]], writes=[oacc_d[tl]])
                            else:
                                kb.op(DVE, lambda: V.tensor_tensor(out=o_acc[:, tl, :], in0=banks[6][:, cs_],
                                                                   in1=o_acc[:, tl, :], op=ALU.add),
                                      reads=[bdeps[6], oacc_d[tl]], writes=[oacc_d[tl]])
                            kb.op(DVE, lambda: V.scalar_tensor_tensor(
                                out=St[:, ci, :], in0=banks[bP][:, cs_], scalar=esc[hp][:, tl, d, 1:2],
                                in1=St[:, ci, :], op0=ALU.mult, op1=ALU.add),
                                reads=[bdeps[bP], St_d[ci], esc_d[hp]], writes=[St_d[ci]])
                            if SEQS[si][2]:
                                lt = tl - SEQS[si][0]
                                if (d == 0 and lt % 2 == 1) or (d == 1 and lt % 2 == 0):
                                    sgcount[0] += 1
                                    q_ = sgcount[0] % 4
                                    kb.op(ACT, lambda: S.copy(out=sstage[:, q_, :], in_=St[:, ci, :]),
                                          reads=[St_d[ci]], writes=[sstage_d[q_]])
                                    kb.dma(SP, sstage_s[q_], ns1_out[lt // 2, d, h, :, :], sstage[:, q_, :],
                                           reads=[sstage_d[q_]])
                        if mid_hook is not None:
                            mid_hook(step, wi, True)

                def m3(h, half):
                    hp = h % 2
                    lo, n_ = (0, 4) if half == 0 else (4, 8)
                    t8 = slice(lo, lo + n_)
                    osq = kt_tok[hp][:].rearrange("p t d k -> p (t d k)").bitcast(F32).rearrange(
                        "p (t k) -> p t k", k=128)[:, t8, :]
                    osq_d = kttok_d[hp][half]
                    on_bf = vtok[hp][:, t8, :]
                    on_d = vtok_d[hp][half]
                    oa = o_acc[:, t8, :]
                    oad = oacc_d[lo:lo + n_]
                    ossv = oss[:, t8]
                    kb.op(DVE, lambda: V.tensor_tensor(out=osq, in0=oa, in1=oa, op=ALU.mult),
                          reads=oad, writes=[osq_d])
                    kb.op(DVE, lambda: V.tensor_reduce(out=ossv, in_=osq, axis=AX.X, op=ALU.add),
                          reads=[osq_d], writes=[oss_d[half]])
                    kb.op(DVE, lambda: V.tensor_scalar(out=ossv, in0=ossv, scalar1=1.0 / 128.0,
                                                       scalar2=NORM_EPS, op0=ALU.mult, op1=ALU.add),
                          reads=[oss_d[half]], writes=[oss_d[half]])
                    kb.op(ACT, lambda: S.sqrt(out=ossv, in_=ossv), reads=[oss_d[half]], writes=[oss_d[half]])
                    kb.op(DVE, lambda: V.reciprocal(out=ossv, in_=ossv), reads=[oss_d[half]], writes=[oss_d[half]])
                    kb.op(DVE, lambda: V.tensor_tensor(
                        out=osq, in0=oa, in1=ossv.unsqueeze(2).to_broadcast([128, n_, 128]), op=ALU.mult),
                        reads=oad + [oss_d[half]], writes=[osq_d])
                    kb.op(DVE, lambda: V.tensor_tensor(
                        out=on_bf, in0=osq, in1=rgain[:, :].unsqueeze(1).to_broadcast([128, n_, 128]),
                        op=ALU.mult), reads=[osq_d, cd], writes=[on_d])

                def m3b(h, half):
                    hp = h % 2
                    lo, n_ = (0, 4) if half == 0 else (4, 8)
                    on_bf = vtok[hp][:, lo:lo + n_, :]
                    on_d = vtok_d[hp][half]
                    bkm = 5 if half == 0 else 6
                    bb = banks[bkm].bitcast(BF16)
                    for q in range(n_):
                        kb.op(PE, lambda: T.transpose(out=bb[:, q * 128:(q + 1) * 128], in_=on_bf[:, q, :],
                                                      identity=ident_b[:]),
                              reads=[on_d, cd2], writes=[bdeps[bkm]], signal=(q == n_ - 1))
                    kb.op(DVE, lambda: V.tensor_tensor(
                        out=gT[:, h, lo * 128:(lo + n_) * 128], in0=bb[:, 0:n_ * 128],
                        in1=sgT1[hp][:, lo * 128:(lo + n_) * 128], op=ALU.mult),
                        reads=[bdeps[bkm], sgT1_d[hp]], writes=[gT_d[lo + q] for q in range(n_)])

                load_w(0, w1_pieces(0))
                load_w(1, w1_pieces(1))
                pull(m1_stream(0), 1000)
                for h in range(8):
                    chains = m2_setup(h)
                    touched = [False] * NT
                    stream = m1_stream(h + 1) if h + 1 < 8 else iter(())
                    hp_count = [0]

                    def hook(step, wi, end_, h=h, stream=stream, hp_count=hp_count):
                        hp_count[0] += 1
                        pull(stream, 1 if end_ else 2)
                        if end_:
                            return
                        if step == 0 and wi == 0 and h >= 1:
                            m3b(h - 1, 1)
                        if step == 2 and wi == 0:
                            for (ci, si, d, tiles) in chains:
                                if not SEQS[si][2]:
                                    kb.dma(SP, St_s[ci], ns_out[si, d, h, :, :], St[:, ci, :], reads=[St_d[ci]])
                            m3(h, 0)
                        if step == 4 and wi == 0:
                            m3b(h, 0)
                    for r in range(8):
                        m2_round(h, chains, r, touched, hook)
                    pull(stream, 1000)
                    m3(h, 1)
                    if h + 2 < 8:
                        load_w(h % 2, w1_pieces(h + 2))
                    if h == 6:
                        prefetch_wout(rec_w_out)
                m3b(7, 1)
                kb.barrier()
            kb.es = es

        def w0_pieces(g):
            return [(0, 256, attn_w_in[:, g * 256:(g + 1) * 256]),
                    (256, 64, attn_w_in[:, 1024 + g * 64:1024 + (g + 1) * 64]),
                    (320, 64, attn_w_in[:, 1280 + g * 64:1280 + (g + 1) * 64]),
                    (384, 256, attn_w_in[:, 1536 + g * 256:1536 + (g + 1) * 256])]

        with contextlib.ExitStack() as es0:
            kb.es = es0
            qkv_all = [kb.sb("qkv_all", [128, 8, 384], F32), None, kb.sb("qkv_allS", [128, 8, 384], F32)]
            qkvA_d = [[Dep(f"qkvA{p}{i}") for i in range(8)] for p in range(3)]
            qkvA_s = [kb.dsem("qkvA0"), kb.dsem("qkvA1"), kb.dsem("qkvA2")]
            sq = [kb.sb("sq", [128, 320], F32) for _ in range(2)]
            sq_d = [Dep("sq0"), Dep("sq1")]
            ss_all = [kb.sb("ss_all", [128, 8, 5], F32) for _ in range(3)]
            ss_d = [Dep("ss0"), Dep("ss1"), Dep("ss2")]
            kout = [kb.sb("kout", [128, 8, 64], F32), None, kb.sb("koutS", [128, 8, 64], F32)]
            kout_d = [Dep("kout0"), None, Dep("kout2")]
            kout_s = [kb.dsem("kout0"), None, kb.dsem("kout2")]
            rt1 = kb.sb("rt1", [128, 8, 320], F32)
            rt2 = kb.sb("rt2", [128, 8, 320], F32)
            rt_d = Dep("rt")
            q_bf_all = [kb.sb("q_bf_all", [128, 8, 384], BF16), None, kb.sb("q_bf_allS", [128, 8, 384], BF16)]
            rope = kb.sb("rope", [128, 2, 8, 64], F32)
            rope_d = Dep("rope")
            rope_s = kb.dsem("rope")
            qbf_d = [Dep("qbf0"), Dep("qbf1"), Dep("qbf2")]
            qT4 = kb.sb("qT4", [128, 8, 4, 128], BF16)
            qT4_d = Dep("qT4")
            kT = kb.sb("kT", [128, 1280], BF16)
            kT_d = Dep("kT")
            vaug = kb.sb("vaug", [128, 10, 65], BF16)
            vaug_d = Dep("vaug")
            sgT = kb.sb("sgT", [128, 2, 1024], BF16)
            sgT_d = Dep("sgT")
            PT = [kb.sb("PT", [128, 10, 512], BF16) for _ in range(2)]
            PT_d = [Dep("PT0"), Dep("PT1")]
            ckv = kb.sb("ckv", [128, 2, 2, 64], F32)
            ckv_d = Dep("ckv")
            ckv_s = kb.dsem("ckv")
            ck_bf = kb.sb("ck_bf", [128, 2, 128], BF16)
            ckbf_d = Dep("ckbf")
            rinv4 = kb.sb("rinv4", [128, 4], F32)
            rinv_d = Dep("rinv4")
            on_bf = kb.sb("on_bf", [128, 256], BF16)
            on_d = Dep("on")
            xt = [kb.sb("xt", [128, D], F32) for _ in range(2)]
            kb.es = es

            kb.dma(SP, rope_s, rope[:], c_rope, writes=[rope_d])
            kb.op(POOL, lambda: G.memset(vaug[:], 1.0), writes=[vaug_d])
            kb.op(POOL, lambda: G.memset(qT4[:], 0.0), writes=[qT4_d])
            load_w(0, w0_pieces(0))
            tcount = 0
            qtcount = 0
            tpcount = 0
            def p123(g, si, par):
                for ti in range((PB if si < 0 else SEQS[si])[1]):
                    p1_tile(g, si, par, ti)
                p23(g, si, par)

            def p1_tile(g, si, par, ti):
                nonlocal tcount
                t0, nt, is_s = PB if si < 0 else SEQS[si]
                W = Wsl[g % 2]
                Wd = Wsl_d[g % 2]
                if True:
                    tile = t0 + ti
                    tcount += 1
                    bk = tcount % 2
                    p_ = tcount % 2
                    for k in range(8):
                        kb.op(PE, lambda: T.matmul(
                            banks[bk][:, 0:384], lhsT=hT[:, k, tile * 128:(tile + 1) * 128],
                            rhs=W[:, k, 0:384], start=(k == 0), stop=(k == 7)),
                            reads=[hT_d[tile], Wd], writes=[bdeps[bk]], signal=(k == 7))
                    kb.op(DVE, lambda: V.tensor_copy(out=qkv_all[par][:, ti, :], in_=banks[bk][:, 0:384]),
                          reads=[bdeps[bk]], writes=[qkvA_d[par][ti]])
                    kb.op(POOL, lambda: G.tensor_tensor(out=sq[p_][:], in0=qkv_all[par][:, ti, 0:320],
                                                        in1=qkv_all[par][:, ti, 0:320], op=ALU.mult),
                          reads=[qkvA_d[par][ti]], writes=[sq_d[p_]])
                    kb.op(DVE, lambda: V.tensor_reduce(out=ss_all[par][:, ti, :],
                                                       in_=sq[p_][:].rearrange("p (h d) -> p h d", d=64),
                                                       axis=AX.X, op=ALU.add), reads=[sq_d[p_]], writes=[ss_d[par]])

            def p23(g, si, par):
                t0, nt, is_s = PB if si < 0 else SEQS[si]
                koff = 2 if is_s else 0
                qds = qkvA_d[par][0:nt]
                if is_s:
                    kb.dma(SP, qkvA_s[par], nv1_out.rearrange("(t p) g d -> p t g d", p=128)[:, :, g, :],
                           qkv_all[par][:, 0:nt, 320:384], reads=qds)
                if not is_s:
                    kb.dma(SP, qkvA_s[par], nv_out.rearrange("s (t p) g d -> p (s t) g d", p=128)[:, :, g, :],
                           qkv_all[par][:, 0:nt, 320:384], reads=qds)
                ssv = ss_all[par][:, 0:nt, :]
                kb.op(DVE, lambda: V.tensor_scalar(out=ssv, in0=ssv, scalar1=64.0 * NORM_EPS,
                                                   scalar2=None, op0=ALU.add), reads=[ss_d[par]], writes=[ss_d[par]])
                kb.op(ACT, lambda: S.sqrt(out=ssv, in_=ssv), reads=[ss_d[par]], writes=[ss_d[par]])
                kb.op(DVE, lambda: V.reciprocal(out=ssv, in_=ssv), reads=[ss_d[par]], writes=[ss_d[par]])
                qk4 = qkv_all[par][:, 0:nt, 0:320].rearrange("p t (h d) -> p t h d", d=64)
                kb.op(DVE, lambda: V.tensor_tensor(
                    out=qk4, in0=qk4, in1=ssv.unsqueeze(3).to_broadcast([128, nt, 5, 64]), op=ALU.mult),
                    reads=qds + [ss_d[par]], writes=qds)
                g5 = gain5[:].unsqueeze(1).to_broadcast([128, nt, 5, 64])
                qb4 = q_bf_all[par][:, 0:nt, 0:320].rearrange("p t (h d) -> p t h d", d=64)
                if not is_s:
                    kb.op(DVE, lambda: V.tensor_tensor(out=qb4, in0=qk4, in1=g5, op=ALU.mult),
                          reads=qds + [cd2], writes=[qbf_d[par]])
                    kb.op(POOL, lambda: G.tensor_tensor(
                        out=kout[par][:, 0:nt, :], in0=qkv_all[par][:, 0:nt, 256:320],
                        in1=gain5[:, 4, :].unsqueeze(1).to_broadcast([128, nt, 64]), op=ALU.mult),
                        reads=qds + [cd2], writes=[kout_d[par]])
                    kb.dma(SP, kout_s[par], nk_out.rearrange("s (t p) g d -> p (s t) g d", p=128)[:, :, g, :],
                           kout[par][:, 0:nt, :], reads=[kout_d[par]])
                else:
                    kb.op(DVE, lambda: V.tensor_tensor(out=qk4, in0=qk4, in1=g5, op=ALU.mult),
                          reads=qds + [cd2], writes=qds)
                    r14 = rt1[:, 0:nt, :].rearrange("p t (h d) -> p t h d", d=64)
                    kb.op(DVE, lambda: V.tensor_tensor(
                        out=r14, in0=qk4, in1=rope[:, 0, 0:nt, :].unsqueeze(2).to_broadcast([128, nt, 5, 64]),
                        op=ALU.mult), reads=qds + [rope_d], writes=[rt_d])
                    qv = qkv_all[par][:, 0:nt, 0:320].rearrange("p t (h a b c) -> p t h a b c", a=2, b=2, c=16)
                    r2v = rt2[:, 0:nt, :].rearrange("p t (h a b c) -> p t h a b c", a=2, b=2, c=16)
                    snv = rope[:, 1, 0:nt, :].rearrange("p t (a b c) -> p t a b c", a=2, b=2)
                    for a in range(2):
                        for b in range(2):
                            eng_, h_ = (POOL, G) if a == 0 else (DVE, V)
                            kb.op(eng_, lambda: h_.tensor_tensor(
                                out=r2v[:, :, :, a, b, :], in0=qv[:, :, :, a, 1 - b, :],
                                in1=snv[:, :, a, b, :].unsqueeze(2).to_broadcast([128, nt, 5, 16]), op=ALU.mult),
                                reads=qds + [rope_d], writes=[rt_d])
                    kb.op(DVE, lambda: V.tensor_tensor(out=q_bf_all[par][:, 0:nt, 0:320], in0=rt1[:, 0:nt, :],
                                                       in1=rt2[:, 0:nt, :], op=ALU.add),
                          reads=[rt_d], writes=[qbf_d[par]])
                    kb.op(POOL, lambda: G.tensor_tensor(out=kout[par][:, 0:nt, :], in0=rt1[:, 0:nt, 256:320],
                                                        in1=rt2[:, 0:nt, 256:320], op=ALU.add),
                          reads=[rt_d], writes=[kout_d[par]])
                    kb.dma(SP, kout_s[par], nk1_out.rearrange("(t p) g d -> p t g d", p=128)[:, :, g, :],
                           kout[par][:, 0:nt, :], reads=[kout_d[par]])
                kb.op(POOL, lambda: G.tensor_copy(out=q_bf_all[par][:, 0:nt, 320:384], in_=q_bf_all[par][:, 0:nt, 256:320]),
                      reads=[qbf_d[par]], writes=[qbf_d[par]])

            def rest(g, si, par, mid_hook=None, tile_hook=None):
                nonlocal qtcount, tpcount
                t0, nt, is_s = PB if si < 0 else SEQS[si]
                W = Wsl[g % 2]
                Wd = Wsl_d[g % 2]
                koff = 2 if is_s else 0
                nkb = nt + koff
                if is_s:
                    kb.dma(SP, ckv_s, ckv[:, 0, :, :], cache_k[:, g, :].rearrange("(b p) d -> p b d", p=128),
                           writes=[ckv_d])
                    kb.dma(SP, ckv_s, ckv[:, 1, :, :], cache_v[:, g, :].rearrange("(b p) d -> p b d", p=128),
                           writes=[ckv_d])
                    kb.op(POOL, lambda: G.tensor_copy(out=ck_bf[:, :, 0:64], in_=ckv[:, 0, :, :]),
                          reads=[ckv_d], writes=[ckbf_d])
                    kb.op(POOL, lambda: G.tensor_copy(out=ck_bf[:, :, 64:128], in_=ckv[:, 0, :, :]),
                          reads=[ckv_d], writes=[ckbf_d])
                    kb.op(POOL, lambda: G.tensor_copy(out=vaug[:, 0:2, 0:64], in_=ckv[:, 1, :, :]),
                          reads=[ckv_d], writes=[vaug_d])
                    bkb = banks[3].bitcast(BF16)
                    for b2 in range(2):
                        kb.op(PE, lambda b2=b2: T.transpose(out=bkb[:, b2 * 128:(b2 + 1) * 128],
                                                            in_=ck_bf[:, b2, :], identity=ident_b[:]),
                              reads=[ckbf_d, cd2], writes=[bdeps[3]], signal=(b2 == 1))
                    kb.op(DVE, lambda: V.tensor_copy(out=kT[:, 0:256], in_=bkb[:, 0:256]),
                          reads=[bdeps[3]], writes=[kT_d])
                kb.op(POOL, lambda: G.tensor_copy(out=vaug[:, koff:koff + nt, 0:64], in_=qkv_all[par][:, 0:nt, 320:384]),
                      reads=qkvA_d[par][0:nt], writes=[vaug_d])
                for tp in range(nt // 2):
                    tpcount += 1
                    bkn = 2
                    bkb = banks[bkn].bitcast(BF16)
                    for u in range(2):
                        for j in range(3):
                            kb.op(PE, lambda: T.transpose(
                                out=bkb[:, j * 256 + u * 128:j * 256 + (u + 1) * 128],
                                in_=q_bf_all[par][:, 2 * tp + u, j * 128:(j + 1) * 128], identity=ident_b[:]),
                                reads=[qbf_d[par], cd2], writes=[bdeps[bkn]], signal=(u == 1 and j == 2))
                    kb.op(DVE, lambda: V.tensor_copy(
                        out=qT4[0:64, 2 * tp:2 * tp + 2, 0:4:2, :],
                        in_=bkb[0:64, 0:512].rearrange("p (h u t) -> p u h t", h=2, u=2)),
                        reads=[bdeps[bkn]], writes=[qT4_d])
                    kb.op(DVE, lambda: V.tensor_copy(
                        out=qT4[64:128, 2 * tp:2 * tp + 2, 1:4:2, :],
                        in_=bkb[64:128, 0:512].rearrange("p (h u t) -> p u h t", h=2, u=2)),
                        reads=[bdeps[bkn]], writes=[qT4_d])
                    kb.op(DVE, lambda: V.tensor_copy(
                        out=kT[:, (koff + 2 * tp) * 128:(koff + 2 * tp + 2) * 128], in_=bkb[:, 512:768]),
                        reads=[bdeps[bkn]], writes=[kT_d])
                ntok = nt * 128
                for c0 in range(0, ntok, 512):
                    n = min(512, ntok - c0)
                    for j in range(2):
                        bk = 3
                        for k in range(8):
                            kb.op(PE, lambda k=k, j=j, c0=c0, n=n: T.matmul(
                                banks[3][:, 0:n], lhsT=W[:, k, 384 + j * 128:384 + (j + 1) * 128],
                                rhs=hT[:, k, t0 * 128 + c0:t0 * 128 + c0 + n], start=(k == 0), stop=(k == 7)),
                                reads=[hT_d[t0 + c0 // 128 + q] for q in range(n // 128)] + [Wd],
                                writes=[bdeps[3]], signal=(k == 7))
                        kb.op(ACT, lambda j=j, c0=c0, n=n: S.activation(
                            out=sgT[:, j, c0:c0 + n], in_=banks[3][:, 0:n], func=AF.Silu),
                            reads=[bdeps[3]], writes=[sgT_d])
                if mid_hook is not None:
                    mid_hook()
                def kbs(ti):
                    if si < 0:
                        return [2 * (ti // 2), 2 * (ti // 2) + 1]
                    return list(range(nkb))

                def scores(ti, j, kbi, ps_):
                    bk = 4 + (j % 2)
                    kb.op(PE, lambda: T.matmul(
                        banks[bk][:, :], lhsT=kT[:, kbi * 128:(kbi + 1) * 128],
                        rhs=qT4[:, ti, :, :].rearrange("p h t -> p (h t)"),
                        start=True, stop=True), reads=[kT_d, qT4_d], writes=[bdeps[bk]])
                    kb.op(ACT, lambda: S.activation(
                        out=PT[ps_][:, j, :], in_=banks[bk][:, :], func=AF.Exp, scale=0.125,
                        bias=(bias_t[:, ti, kbi:kbi + 1] if is_s else 0.0)),
                        reads=[bdeps[bk], cd], writes=[PT_d[ps_]])

                qbase = qtcount
                for j, kbi in enumerate(kbs(0)):
                    scores(0, j, kbi, (qbase + 1) % 2)
                for ti in range(nt):
                    tile = t0 + ti
                    if tile_hook is not None:
                        tile_hook(ti)
                    qtcount += 1
                    ps_ = qtcount % 2
                    bo = 6 + (qtcount % 2)
                    kl = kbs(ti)
                    kn = kbs(ti + 1) if ti + 1 < nt else []
                    for j, kbi in enumerate(kl):
                        if j < len(kn):
                            scores(ti + 1, j, kn[j], (qtcount + 1) % 2)
                        for h in range(4):
                            kb.op(PE, lambda: T.matmul(
                                banks[bo][:, h * 65:(h + 1) * 65], lhsT=PT[ps_][:, j, h * 128:(h + 1) * 128],
                                rhs=vaug[:, kbi, :], start=(j == 0 and h == 0), stop=(j == len(kl) - 1),
                                skip_group_check=True),
                                reads=[PT_d[ps_], vaug_d], writes=[bdeps[bo]],
                                signal=(h == 3 and j == len(kl) - 1))
                    ov = banks[bo][:, 0:260].rearrange("p (h d) -> p h d", d=65)
                    kb.op(DVE, lambda: V.reciprocal(out=rinv4[:], in_=ov[:, :, 64]),
                          reads=[bdeps[bo]], writes=[rinv_d])
                    kb.op(DVE, lambda: V.tensor_tensor(
                        out=on_bf[:].rearrange("p (h d) -> p h d", d=64), in0=ov[:, :, 0:64],
                        in1=rinv4[:, :].unsqueeze(2).to_broadcast([128, 4, 64]), op=ALU.mult),
                        reads=[bdeps[bo], rinv_d], writes=[on_d])
                    bkb7 = banks[2].bitcast(BF16)
                    for j in range(2):
                        kb.op(PE, lambda: T.transpose(out=bkb7[:, 512 + j * 128:512 + (j + 1) * 128],
                                                      in_=on_bf[:, j * 128:(j + 1) * 128],
                                                      identity=ident_b[:]),
                              reads=[on_d, cd2], writes=[bdeps[2]], signal=(j == 1))
                    kb.op(DVE, lambda: V.tensor_tensor(
                        out=gT[:, 2 * g:2 * g + 2, tile * 128:(tile + 1) * 128],
                        in0=bkb7[:, 512:768].rearrange("p (j t) -> p j t", j=2),
                        in1=sgT[:, :, ti * 128:(ti + 1) * 128], op=ALU.mult),
                        reads=[bdeps[2], sgT_d], writes=[gT_d[tile]])

            load_w(1, w0_pieces(1))
            for i in range(NT):
                xs = i % 2
                kb.dma(SP, xt_s[xs], xt[xs][:], x_all[i * 128:(i + 1) * 128, :], writes=[xt_d[xs]])
                make_hT(xt[xs], xt_d[xs], i, 0 if i < 4 else 1, 0, (4, 5) if i % 2 == 0 else (6, 7))
                if i >= 1:
                    j = i - 1
                    if j < 4:
                        p1_tile(0, -1, 0, j)
                    else:
                        p1_tile(0, SI_S, 2, j - 4)
            p1_tile(0, SI_S, 2, 7)
            p23(0, -1, 0)
            p23(0, SI_S, 2)
            for g in range(4):
                rest(g, -1, 0)

                def hook(g=g):
                    if g + 1 < 4:
                        p123(g + 1, -1, 0)
                        p123(g + 1, SI_S, 2)
                if g == 3:
                    prefetch_wout(attn_w_out)
                rest(g, SI_S, 2, hook)
                if g + 2 < 4:
                    load_w(g % 2, w0_pieces(g + 2))
            kb.barrier()

        if stg >= 3:
            epilogue(0, attn_w_out, x_all, last=(n_layers == 1))

        if n_layers == 2 and stg >= 4:
            layer1()
            epilogue(1, rec_w_out, x1_dram, last=True)

        kb.finish()
    return nc


def _consts(is_sample_slot):
    ident = np.eye(128, dtype=np.float32)
    t = np.arange(1024)
    rows = (t // 64).astype(np.float32)
    cols = (t % 64).astype(np.float32)
    inv_freq = (1.0 / (np.float32(10000.0) ** (np.arange(0, 32, 2, dtype=np.float32) / np.float32(32)))).astype(np.float32)
    ang_r = rows[:, None] * inv_freq[None, :]
    ang_c = cols[:, None] * inv_freq[None, :]
    ang = np.concatenate([ang_r, ang_r, ang_c, ang_c], axis=-1).astype(np.float32)
    if is_sample_slot:
        cos = np.cos(ang).astype(np.float32)
        sin = np.sin(ang).astype(np.float32)
    else:
        cos = np.ones_like(ang)
        sin = np.zeros_like(ang)
    sgn = np.concatenate([-np.ones(16), np.ones(16), -np.ones(16), np.ones(16)]).astype(np.float32)
    sins = sin * sgn[None, :]
    rope = np.stack([cos, sins], 0).reshape(2, 8, 128, 64).transpose(2, 0, 1, 3).copy()
    mid = 63
    wc = np.zeros((2, 128, 130), np.float32)
    mask = np.zeros((2, 128, 128), np.float32)
    s = np.arange(128)[:, None]
    tt = np.arange(128)[None, :]
    fw = np.where((s > mid) & (s <= tt), 1.0, 0.0) - np.where((s <= mid) & (s > tt), 1.0, 0.0)
    wc[0, :, :128] = fw
    mask[0] = (s <= tt)
    mb = 64
    bw = np.where((s < mb) & (s >= tt), 1.0, 0.0) - np.where((s >= mb) & (s < tt), 1.0, 0.0)
    wc[1, :, :128] = bw
    mask[1] = (s >= tt)
    sel = np.zeros((2, 2, 128), np.float32)
    wc = np.ascontiguousarray(wc.transpose(1, 0, 2))
    mask = np.ascontiguousarray(mask.transpose(1, 0, 2))
    bias = np.zeros((128, 8, 10), np.float32)
    rf = np.ones((128, 8, 2), np.float32)
    if not is_sample_slot:
        NEG = -30000.0
        bias[:, :, 0:2] = NEG
        for ti in range(8):
            for t_ in range(8):
                if t_ // 2 != ti // 2:
                    bias[:, ti, 2 + t_] = NEG
        for lt in range(8):
            if lt % 2 == 0:
                rf[:, lt, 0] = 0.0
            else:
                rf[:, lt, 1] = 0.0
    return dict(c_ident=ident, c_rope=rope.astype(np.float32), c_wc=wc, c_mask=mask, c_sel=sel,
                c_bias=bias, c_rf=rf)


_NC_CACHE = {}


def kernel(x_prompt, x_sample, cache_k, cache_v, state_rec, c, c_ctx,
           ada_w, ada_b, attn_w_in, attn_q_gain, attn_k_gain, attn_w_out,
           rec_w_in, rec_lower_bounds, rec_norm_gain, rec_w_out, ln_gain, ln_bias, _n_layers=2, _stage=99):
    f = lambda a: np.ascontiguousarray(np.asarray(a, dtype=np.float32))
    x_prompt, x_sample, cache_k, cache_v, state_rec = map(f, (x_prompt, x_sample, cache_k, cache_v, state_rec))
    c, c_ctx = f(c), f(c_ctx)
    consts = [_consts(True), _consts(False)]
    shared = dict(ada_w=f(ada_w), ada_b=f(ada_b), attn_w_in=f(attn_w_in)[0], q_gain=f(attn_q_gain),
                  k_gain=f(attn_k_gain), attn_w_out=f(attn_w_out)[0], rec_w_in=f(rec_w_in)[0],
                  lbr=np.ascontiguousarray(f(rec_lower_bounds).reshape(2, 2, 8, 128).transpose(3, 0, 1, 2)),
                  adabT=np.ascontiguousarray(f(ada_b).reshape(2, 24, 128).transpose(2, 0, 1)), rec_gain=f(rec_norm_gain), rec_w_out=f(rec_w_out)[0],
                  ln_gain=f(ln_gain), ln_bias=f(ln_bias))
    in_maps = []
    for core in range(8):
        m = dict(shared)
        if core < 4:
            s = core
            p0 = 2 * core
            xa = np.concatenate([x_prompt[p0:p0 + 2].reshape(512, D), x_sample[s]], axis=0)
            m.update(consts[0])
            m.update(cache_k=np.ascontiguousarray(cache_k[s, 0]), cache_v=np.ascontiguousarray(cache_v[s, 0]),
                     state0=np.ascontiguousarray(state_rec[s, 0]), cond_rows=np.stack([c_ctx, c[s]], 0))
        else:
            k = core - 4
            p0 = 8 + 2 * k
            p1 = 16 + 4 * k
            xa = np.concatenate([x_prompt[p0:p0 + 2].reshape(512, D), x_prompt[p1:p1 + 4].reshape(1024, D)], axis=0)
            m.update(consts[1])
            m.update(cache_k=np.zeros((256, 4, 64), np.float32), cache_v=np.zeros((256, 4, 64), np.float32),
                     state0=np.zeros((2, 8, 128, 128), np.float32), cond_rows=np.stack([c_ctx, c_ctx], 0))
        cr = m.pop("cond_rows")
        m.update(x_all=np.ascontiguousarray(xa),
                 condT=np.ascontiguousarray(cr.reshape(2, 8, 128).transpose(2, 1, 0)))
        in_maps.append(m)
    key = (_n_layers, _stage)
    if key not in _NC_CACHE:
        _NC_CACHE[key] = build_program(_n_layers, _stage)
    nc = _NC_CACHE[key]
    res = run_bass_kernel_spmd(nc, in_maps, core_ids=list(range(8)))
    R = res.results
    y_prompt = np.zeros((32, 256, D), np.float32)
    nk = np.zeros((32, 1, 256, 4, 64), np.float32)
    nv = np.zeros((32, 1, 256, 4, 64), np.float32)
    ns = np.zeros((32, 1, 2, 8, 128, 128), np.float32)
    y_sample = np.zeros((4, 1024, D), np.float32)
    for core in range(8):
        r = R[core]
        p0 = 2 * core if core < 4 else 8 + 2 * (core - 4)
        y_prompt[p0:p0 + 2] = r["y_out"][:512].reshape(2, 256, D)
        nk[p0:p0 + 2, 0] = r["nk_out"]
        nv[p0:p0 + 2, 0] = r["nv_out"]
        ns[p0:p0 + 2, 0] = r["ns_out"]
        if core < 4:
            y_sample[core] = r["y_out"][512:]
        else:
            p1 = 16 + 4 * (core - 4)
            y_prompt[p1:p1 + 4] = r["y_out"][512:].reshape(4, 256, D)
            nk[p1:p1 + 4, 0] = r["nk1_out"].reshape(4, 256, 4, 64)
            nv[p1:p1 + 4, 0] = r["nv1_out"].reshape(4, 256, 4, 64)
            ns[p1:p1 + 4, 0] = r["ns1_out"]
    return (y_prompt, y_sample, nk, nv, ns)
```

```python
import contextlib
import numpy as np
import concourse.bass as bass
import concourse.mybir as mybir
from concourse.bass_utils import run_bass_kernel_spmd

F32 = mybir.dt.float32
BF16 = mybir.dt.bfloat16
AF = mybir.ActivationFunctionType
ALU = mybir.AluOpType
AX = mybir.AxisListType

D = 1024
NT = 12
NTOK = NT * 128
ALPHA = 4.0 ** 0.25
NORM_EPS = 1e-6
LN_EPS = 1e-5
SEQS = [(0, 2, False), (2, 2, False), (4, 8, True)]
SI_S = 2
PB = (0, 4, False)
DEBUG = False
import os
SKIP = os.environ.get('KSKIP', '')


class Dep:
    __slots__ = ("w", "r", "name", "excl")

    def __init__(self, name="", excl=False):
        self.excl = excl
        self.w = {}
        self.r = {}
        self.name = name


class Sig:
    def __init__(self, name, sem, h=None):
        self.name = name
        self.sem = sem
        self.h = h
        self.n = 0
        self.seen = {}


class KB:
    def __init__(self, nc, es):
        self.nc = nc
        self.es = es
        self.root = es
        self.nsem = 0
        self.pe = self.eng("pe", nc.tensor)
        self.act = self.eng("act", nc.scalar)
        self.dve = self.eng("dve", nc.vector)
        self.pool = self.eng("pool", nc.gpsimd)
        self.sp = self.eng("sp", nc.sync)
        self.dsems = []
        self.uid = 0

    def newsem(self, name):
        self.nsem += 1
        return self.root.enter_context(self.nc.semaphore(f"{name}_{self.nsem}"))

    def eng(self, name, h):
        return Sig(name, self.newsem(name), h)

    def dsem(self, name):
        s = Sig(name, self.newsem(name))
        self.dsems.append(s)
        return s

    def sb(self, name, shape, dt):
        self.uid += 1
        return self.es.enter_context(self.nc.sbuf_tensor(f"{name}_{self.uid}", list(shape), dt))

    def _waits(self, e, reads, writes):
        need = {}
        for d in reads:
            for s, v in d.w.items():
                need[s] = max(need.get(s, 0), v)
        for d in writes:
            for s, v in d.w.items():
                need[s] = max(need.get(s, 0), v)
            for s, v in d.r.items():
                if s is e:
                    continue
                need[s] = max(need.get(s, 0), v)
        if e is self.pe:
            need.pop(e, None)
        for s, v in need.items():
            if e.seen.get(s, 0) >= v:
                continue
            e.h.wait_ge(s.sem, v)
            e.seen[s] = v

    def op(self, e, fn, reads=(), writes=(), signal=True):
        ex = [d for d in reads if d.excl]
        if ex:
            writes = list(writes) + ex
        self._waits(e, reads, writes)
        inst = fn()
        if signal:
            e.n += 1
            inst.then_inc(e.sem, 1)
            val = e.n
        else:
            val = e.n + 1
        for d in reads:
            d.r[e] = max(d.r.get(e, 0), val)
        for d in writes:
            d.w[e] = max(d.w.get(e, 0), val)
        return inst

    def dma(self, q, ds, out, in_, reads=(), writes=(), **kw):
        self._waits_dma(q, ds, reads, writes)
        inst = q.h.dma_start(out=out, in_=in_, **kw)
        ds.n += 16
        inst.then_inc(ds.sem, 16)
        for d in reads:
            d.r[ds] = ds.n
        for d in writes:
            d.w[ds] = ds.n
        return inst

    def _waits_dma(self, q, ds, reads, writes):
        need = {}
        for d in reads:
            for s, v in d.w.items():
                need[s] = max(need.get(s, 0), v)
        for d in writes:
            for s, v in d.w.items():
                if s is not ds:
                    need[s] = max(need.get(s, 0), v)
            for s, v in d.r.items():
                need[s] = max(need.get(s, 0), v)
        for s, v in need.items():
            if q.seen.get(s, 0) >= v:
                continue
            q.h.wait_ge(s.sem, v)
            q.seen[s] = v

    def barrier(self):
        sigs = [self.pe, self.act, self.dve, self.pool] + self.dsems
        for e in (self.pe, self.act, self.dve, self.pool, self.sp):
            for s_ in sigs:
                if s_ is e or s_.n == 0:
                    continue
                if e.seen.get(s_, 0) >= s_.n:
                    continue
                e.h.wait_ge(s_.sem, s_.n)
                e.seen[s_] = s_.n

    def finish(self):
        for s in self.dsems:
            if s.n > 0:
                self.sp.h.wait_ge(s.sem, s.n)
        for e in (self.pe, self.act, self.dve, self.pool):
            if e.n > 0:
                self.sp.h.wait_ge(e.sem, e.n)


def build_program(n_layers=2, stg=99):
    nc = bass.Bass("TRN2", target_bir_lowering=False, dynamic_dma_scratch_size=8192)

    def din(name, shape, dt=F32):
        return nc.dram_tensor(name, list(shape), dt, kind="ExternalInput").ap()

    def dout(name, shape, dt=F32):
        return nc.dram_tensor(name, list(shape), dt, kind="ExternalOutput").ap()

    x_all = din("x_all", [NTOK, D])
    cache_k = din("cache_k", [256, 4, 64])
    cache_v = din("cache_v", [256, 4, 64])
    state0 = din("state0", [2, 8, 128, 128])
    cond = din("condT", [128, 8, 2])
    ada_w = din("ada_w", [2, D, 3 * D])
    ada_b = din("ada_b", [2, 3 * D])
    adabT_in = din("adabT", [128, 2, 24])
    attn_w_in = din("attn_w_in", [D, 2560])
    q_gain = din("q_gain", [1, 64])
    k_gain = din("k_gain", [1, 64])
    attn_w_out = din("attn_w_out", [D, D])
    rec_w_in = din("rec_w_in", [D, 5120])
    rec_lb = din("lbr", [128, 2, 2, 8])
    rec_gain = din("rec_gain", [1, 128])
    rec_w_out = din("rec_w_out", [D, D])
    ln_gain = din("ln_gain", [2, D])
    ln_bias = din("ln_bias", [2, D])
    c_ident = din("c_ident", [128, 128])
    c_rope = din("c_rope", [128, 2, 8, 64])
    c_wc = din("c_wc", [128, 2, 130])
    c_mask = din("c_mask", [128, 2, 128])
    c_sel = din("c_sel", [2, 2, 128])
    c_bias = din("c_bias", [128, 8, 10])
    c_rf = din("c_rf", [128, 8, 2])

    y_out = dout("y_out", [NTOK, D])
    nk_out = dout("nk_out", [2, 256, 4, 64])
    nv_out = dout("nv_out", [2, 256, 4, 64])
    ns_out = dout("ns_out", [2, 2, 8, 128, 128])
    nk1_out = dout("nk1_out", [1024, 4, 64])
    nv1_out = dout("nv1_out", [1024, 4, 64])
    ns1_out = dout("ns1_out", [4, 2, 8, 128, 128])
    if DEBUG:
        x1_dram = dout("x1_dram", [NTOK, D])
    else:
        x1_dram = nc.dram_tensor("x1_dram", [NTOK, D], F32, kind="Internal").ap()

    es = contextlib.ExitStack()
    with es:
        kb = KB(nc, es)
        PE, ACT, DVE, POOL, SP = kb.pe, kb.act, kb.dve, kb.pool, kb.sp
        T, V, S, G = nc.tensor, nc.vector, nc.scalar, nc.gpsimd

        banks = []
        bdeps = []
        for i in range(8):
            banks.append(es.enter_context(nc.psum_tensor(f"bank{i}", [128, 512], F32)))
            bdeps.append(Dep(f"bank{i}", excl=True))

        hT = kb.sb("hT", [128, 8, NTOK], BF16)
        hT_d = [Dep(f"hT{i}") for i in range(NT)]
        gT = kb.sb("gT", [128, 8, NTOK], BF16)
        gT_d = [Dep(f"gT{i}") for i in range(NT)]
        Wsl = [kb.sb("W", [128, 8, 640], BF16) for _ in range(2)]
        Wsl_d = [Dep("W0"), Dep("W1")]
        Wsl_s = [kb.dsem("W0"), kb.dsem("W1")]
        wout = kb.sb("wout", [128, 8, D], BF16)
        wout_d = Dep("wout")
        wout_s = kb.dsem("wout")
        xt_d = [Dep(f"xt{i}") for i in range(4)]
        xt_s = [kb.dsem(f"xt{i}") for i in range(4)]
        vt_d = [Dep("vt0"), Dep("vt1")]
        yt_d = [Dep(f"yt{i}") for i in range(4)]
        yt_s = [kb.dsem(f"yt{i}") for i in range(4)]
        gate_rep_d = [Dep("gr0"), Dep("gr1")]
        lnrep_d = Dep("lnrep")
        lnrep_s = kb.dsem("lnrep")
        cs = kb.dsem("const")
        cd = Dep("const")
        ident_f = kb.sb("ident_f", [128, 128], F32)
        ident_b = kb.sb("ident_b", [128, 128], BF16)
        wc = kb.sb("wc", [128, 2, 130], F32)
        maskt = kb.sb("mask", [128, 2, 128], F32)
        gain5 = kb.sb("gain5", [128, 5, 64], F32)
        bias_t = kb.sb("bias_t", [128, 8, 10], F32)
        rf_t = kb.sb("rf_t", [128, 8, 2], F32)
        rgain = kb.sb("rgain", [128, 128], F32)
        condT = kb.sb("condT", [128, 8, 2], F32)
        scT = kb.sb("scT", [128, 8, 2], F32)
        adabT = kb.sb("adabT", [128, 2, 24], F32)
        modT = kb.sb("modT", [128, 2, 16, 2], F32)
        modT_d = Dep("modT")
        gate_dram = nc.dram_tensor("gate_dram", [2, 2, D], F32, kind="Internal").ap()
        gate_s = kb.dsem("gate")
        gate_rep_s = [kb.dsem("grep0"), kb.dsem("grep1")]
        gate_row_d = Dep("gate_row")
        lbr = kb.sb("lbr", [128, 2, 2, 8], F32)
        lbm1 = kb.sb("lbm1", [128, 2, 8], F32)
        oml = kb.sb("oml", [128, 2, 8], F32)
        lb_d = Dep("lb")

        def cload(dst, src, **kw):
            kb.dma(SP, cs, dst, src, writes=[cd], **kw)

        cload(ident_f[:], c_ident)
        cload(bias_t[:], c_bias)
        cload(rf_t[:], c_rf)
        cload(wc[:], c_wc)
        cload(maskt[:], c_mask)
        for h in range(4):
            cload(gain5[:, h, :], q_gain[0:1, :].to_broadcast([128, 64]))
        cload(gain5[:, 4, :], k_gain[0:1, :].to_broadcast([128, 64]))
        cload(rgain[:], rec_gain[0:1, :].to_broadcast([128, 128]))
        cload(condT[:], cond)
        cload(adabT[:], adabT_in)
        cload(lbr[:], rec_lb)

        cd2 = Dep("const2")
        kb.op(ACT, lambda: S.copy(out=ident_b[:], in_=ident_f[:]), reads=[cd], writes=[cd2])
        kb.op(ACT, lambda: S.mul(out=gain5[:], in_=gain5[:], mul=8.0), reads=[cd], writes=[cd2])
        kb.op(ACT, lambda: S.activation(out=scT[:], in_=condT[:], func=AF.Silu), reads=[cd], writes=[cd2])
        kb.op(DVE, lambda: V.tensor_tensor(out=lbm1[:], in0=lbr[:, 1, :, :], in1=lbr[:, 0, :, :], op=ALU.subtract),
              reads=[cd], writes=[lb_d])
        kb.op(ACT, lambda: S.activation(out=oml[:], in_=lbm1[:], func=AF.Sigmoid, scale=-1.0),
              reads=[lb_d], writes=[lb_d])
        kb.op(DVE, lambda: V.tensor_scalar(out=lbm1[:], in0=oml[:], scalar1=-1.0, scalar2=None, op0=ALU.mult),
              reads=[lb_d], writes=[lb_d])

        with contextlib.ExitStack() as es_p:
            kb.es = es_p
            stage = [kb.sb("adastage", [128, 8, 512], F32) for _ in range(2)]
            stage_d = [Dep("st0"), Dep("st1")]
            stage_s = [kb.dsem("st0"), kb.dsem("st1")]
            rowtmp = [kb.sb("rowtmp", [2, 512], F32) for _ in range(2)]
            adab_row = kb.sb("adab_row", [2, 2, D], F32)
            grow = kb.sb("grow", [2, 2, D], F32)
            grow_d = Dep("grow")
            adab_d = Dep("adab_row")
            adab_s = kb.dsem("adab_row")
            rowtmp_d = [Dep("rowtmp0"), Dep("rowtmp1")]
            kb.es = es
            for r in range(2):
                kb.dma(SP, adab_s, adab_row[r:r + 1, :, :], ada_b[:, 2 * D:3 * D].rearrange("(o l) n -> o l n", o=1),
                       writes=[adab_d])
            it = 0
            for l in range(2):
                aw = ada_w[l].rearrange("(k p) n -> p k n", p=128)
                for j in range(6):
                    sl = it % 2
                    it += 1
                    kb.dma(SP, stage_s[sl], stage[sl][:], aw[:, :, j * 512:(j + 1) * 512], writes=[stage_d[sl]])
                    if j < 4:
                        bk = 0
                        rsl = it % 2
                        for k in range(8):
                            kb.op(PE, lambda k=k, sl=sl: T.matmul(
                                banks[2][0:2, :], lhsT=scT[:, k, :], rhs=stage[sl][:, k, :],
                                start=(k == 0), stop=(k == 7)),
                                reads=[stage_d[sl], cd2], writes=[bdeps[2]], signal=(k == 7))
                        kb.op(ACT, lambda rsl=rsl: S.copy(out=rowtmp[rsl][:, :], in_=banks[2][0:2, :]),
                              reads=[bdeps[2]], writes=[rowtmp_d[rsl]])
                        for sub in range(4):
                            cch = j * 4 + sub
                            kb.op(PE, lambda sub=sub, cch=cch, rsl=rsl: T.transpose(
                                out=banks[bk][:, cch * 2:cch * 2 + 2], in_=rowtmp[rsl][:, sub * 128:(sub + 1) * 128],
                                identity=ident_f[0:2, 0:2]), reads=[rowtmp_d[rsl], cd], writes=[bdeps[bk]],
                                signal=(sub == 3))
                        if j == 3:
                            kb.op(DVE, lambda l=l: V.tensor_tensor(
                                out=modT[:, l, :, :], in0=banks[0][:, 0:32].rearrange("p (c o) -> p c o", o=2),
                                in1=adabT[:, l, 0:16].unsqueeze(2).to_broadcast([128, 16, 2]), op=ALU.add),
                                reads=[bdeps[0], cd], writes=[modT_d])
                            kb.op(DVE, lambda l=l: V.tensor_scalar(
                                out=modT[:, l, 8:16, :], in0=modT[:, l, 8:16, :], scalar1=1.0, scalar2=None,
                                op0=ALU.add), reads=[modT_d], writes=[modT_d])
                    else:
                        bk = 1
                        nb = j - 4
                        for k in range(8):
                            kb.op(PE, lambda k=k, sl=sl: T.matmul(
                                banks[bk][0:2, :], lhsT=scT[:, k, :], rhs=stage[sl][:, k, :],
                                start=(k == 0), stop=(k == 7)),
                                reads=[stage_d[sl], cd2], writes=[bdeps[bk]], signal=(k == 7))
                        kb.op(DVE, lambda l=l, nb=nb: V.tensor_tensor(
                            out=grow[:, l, nb * 512:(nb + 1) * 512], in0=banks[1][0:2, :],
                            in1=adab_row[:, l, nb * 512:(nb + 1) * 512], op=ALU.add),
                            reads=[bdeps[1], adab_d], writes=[grow_d])
                        if l == 1 and nb == 1:
                            kb.dma(SP, gate_s, gate_dram, grow[:], reads=[grow_d], writes=[gate_row_d])
            kb.barrier()

        ev_rr = [0]

        def make_hT(src_ap, src_dep, i, c, l, pb, all_act=False):
            for half in range(2):
                bk = pb[half]
                for q in range(4):
                    k = half * 4 + q
                    kb.op(PE, lambda k=k, q=q, bk=bk: T.transpose(
                        out=banks[bk][:, q * 128:(q + 1) * 128], in_=src_ap[:, k * 128:(k + 1) * 128],
                        identity=ident_f[:]), reads=[src_dep, cd], writes=[bdeps[bk]], signal=(q == 3))
                for q in range(4):
                    k = half * 4 + q
                    if half == 0 or all_act:
                        kb.op(ACT, lambda k=k, q=q, bk=bk: S.activation(
                            out=hT[:, k, i * 128:(i + 1) * 128], in_=banks[bk][:, q * 128:(q + 1) * 128],
                            func=AF.Identity, scale=modT[:, l, 8 + k, c:c + 1], bias=modT[:, l, k, c:c + 1]),
                            reads=[bdeps[bk], modT_d], writes=[hT_d[i]])
                    else:
                        kb.op(DVE, lambda k=k, q=q, bk=bk: V.tensor_scalar(
                            out=hT[:, k, i * 128:(i + 1) * 128], in0=banks[bk][:, q * 128:(q + 1) * 128],
                            scalar1=modT[:, l, 8 + k, c:c + 1], scalar2=modT[:, l, k, c:c + 1],
                            op0=ALU.mult, op1=ALU.add),
                            reads=[bdeps[bk], modT_d], writes=[hT_d[i]])

        def load_w(slot, pieces):
            for (c0, ncol, src) in pieces:
                kb.dma(POOL, Wsl_s[slot], Wsl[slot][:, :, c0:c0 + ncol],
                       src.rearrange("(k p) n -> p k n", p=128), writes=[Wsl_d[slot]])

        def prefetch_wout(w_out_ap):
            kb.dma(POOL, wout_s, wout[:], w_out_ap.rearrange("(k p) n -> p k n", p=128), writes=[wout_d])

        def epilogue(l, w_out_ap, res_src, last):
            with contextlib.ExitStack() as es_e:
                kb.es = es_e
                xt = [kb.sb("xt", [128, D], F32) for _ in range(4)]
                vt = [kb.sb("vt", [128, D], F32) for _ in range(2)]
                yt = [kb.sb("yt", [128, D], F32) for _ in range(4)]
                gate_rep = [kb.sb("gate_rep", [128, D], F32) for _ in range(2)]
                lnrep = kb.sb("lnrep", [128, 2, D], F32)
                _epilogue(l, w_out_ap, res_src, last, xt, vt, yt, gate_rep, lnrep, wout)
                kb.barrier()
            kb.es = es

        def _epilogue(l, w_out_ap, res_src, last, xt, vt, yt, gate_rep, lnrep, wout):
            kb.dma(SP, lnrep_s, lnrep[:, 0, :], ln_gain[l:l + 1, :].to_broadcast([128, D]), writes=[lnrep_d])
            kb.dma(SP, lnrep_s, lnrep[:, 1, :], ln_bias[l:l + 1, :].to_broadcast([128, D]), writes=[lnrep_d])
            for c in range(2):
                kb.dma(SP, gate_rep_s[c], gate_rep[c][:], gate_dram[c:c + 1, l, :].to_broadcast([128, D]),
                       reads=[gate_row_d], writes=[gate_rep_d[c]])
            stats = [kb.sb("stats", [128, 2, 6], F32) for _ in range(2)]
            mv = [kb.sb("mv", [128, 2], F32) for _ in range(2)]
            rstd = [kb.sb("rstd", [128, 1], F32) for _ in range(2)]
            nmr = [kb.sb("nmr", [128, 1], F32) for _ in range(2)]
            st_d = [Dep("stats0"), Dep("stats1")]
            def xload(i):
                if i < NT:
                    kb.dma(SP, xt_s[i % 4], xt[i % 4][:], res_src[i * 128:(i + 1) * 128, :], writes=[xt_d[i % 4]])

            def stage1(i):
                c = 0 if i < 4 else 1
                xs = i % 4
                p = i % 2
                xload(i + 3)
                for nb in range(2):
                    bk = (i % 2) * 2 + nb
                    for k in range(8):
                        kb.op(PE, lambda: T.matmul(
                            banks[bk][:, :], lhsT=gT[:, k, i * 128:(i + 1) * 128],
                            rhs=wout[:, k, nb * 512:(nb + 1) * 512], start=(k == 0), stop=(k == 7)),
                            reads=[gT_d[i], wout_d], writes=[bdeps[bk]], signal=(k == 7))
                    kb.op(DVE, lambda: V.tensor_tensor(
                        out=vt[p][:, nb * 512:(nb + 1) * 512], in0=banks[bk][:, :],
                        in1=gate_rep[c][:, nb * 512:(nb + 1) * 512], op=ALU.mult),
                        reads=[bdeps[bk], gate_rep_d[c]], writes=[vt_d[p]])
                kb.op(DVE, lambda: V.scalar_tensor_tensor(
                    out=vt[p][:], in0=xt[xs][:], scalar=ALPHA, in1=vt[p][:], op0=ALU.mult, op1=ALU.add),
                    reads=[xt_d[xs], vt_d[p]], writes=[vt_d[p]])
                kb.op(DVE, lambda: V.bn_stats(out=stats[p][:, 0, :], in_=vt[p][:, 0:512]), reads=[vt_d[p]], writes=[st_d[p]])
                kb.op(DVE, lambda: V.bn_stats(out=stats[p][:, 1, :], in_=vt[p][:, 512:1024]), reads=[vt_d[p]], writes=[st_d[p]])
                kb.op(DVE, lambda: V.bn_aggr(out=mv[p][:], in_=stats[p][:]), reads=[st_d[p]], writes=[st_d[p]])
                kb.op(DVE, lambda: V.tensor_scalar(out=rstd[p][:], in0=mv[p][:, 1:2], scalar1=LN_EPS, scalar2=None,
                                                   op0=ALU.add), reads=[st_d[p]], writes=[st_d[p]])
                kb.op(ACT, lambda: S.sqrt(out=rstd[p][:], in_=rstd[p][:]), reads=[st_d[p]], writes=[st_d[p]])

            def stage2(i):
                c = 0 if i < 4 else 1
                ys = i % 4
                p = i % 2
                kb.op(DVE, lambda: V.reciprocal(out=rstd[p][:], in_=rstd[p][:]), reads=[st_d[p]], writes=[st_d[p]])
                kb.op(DVE, lambda: V.scalar_tensor_tensor(out=nmr[p][:], in0=mv[p][:, 0:1], scalar=-1.0, in1=rstd[p][:],
                                                          op0=ALU.mult, op1=ALU.mult), reads=[st_d[p]], writes=[st_d[p]])
                kb.op(ACT, lambda: S.activation(out=yt[ys][:], in_=vt[p][:], func=AF.Identity,
                                                scale=rstd[p][:, 0:1], bias=nmr[p][:, 0:1]),
                      reads=[vt_d[p], st_d[p]], writes=[yt_d[ys]])
                kb.op(POOL, lambda: G.tensor_tensor(out=yt[ys][:], in0=yt[ys][:], in1=lnrep[:, 0, :], op=ALU.mult),
                      reads=[yt_d[ys], lnrep_d], writes=[yt_d[ys]])
                kb.op(POOL, lambda: G.tensor_tensor(out=yt[ys][:], in0=yt[ys][:], in1=lnrep[:, 1, :], op=ALU.add),
                      reads=[yt_d[ys], lnrep_d], writes=[yt_d[ys]])
                if last:
                    kb.dma(SP, yt_s[ys], y_out[i * 128:(i + 1) * 128, :], yt[ys][:], reads=[yt_d[ys]])
                else:
                    kb.dma(SP, yt_s[ys], x1_dram[i * 128:(i + 1) * 128, :], yt[ys][:], reads=[yt_d[ys]])

            def stage3(i):
                if not last and 0 <= i < NT:
                    make_hT(yt[i % 4], yt_d[i % 4], i, 0 if i < 4 else 1, l + 1, (6, 7) if i % 2 == 0 else (4, 5), all_act=True)

            xload(0)
            xload(1)
            xload(2)
            stage1(0)
            for i in range(NT):
                if i + 1 < NT:
                    stage1(i + 1)
                stage2(i)
                stage3(i - 1)
            stage3(NT - 1)

        def w1_pieces(h):
            return [(j * 128, 128, rec_w_in[:, j * 1024 + h * 128:j * 1024 + (h + 1) * 128]) for j in range(5)]

        def layer1():
            with contextlib.ExitStack() as es1:
                kb.es = es1
                sqT = [kb.sb("sqT", [128, 512], F32) for _ in range(2)]
                sqT_d = [Dep("sqT0"), Dep("sqT1")]
                snT = [kb.sb("snT", [128, 2, 512], F32) for _ in range(2)]
                snT_d = [Dep("snT0"), Dep("snT1")]
                lfT = [kb.sb("lfT", [128, 2, 512], F32) for _ in range(2)]
                lfT_d = [Dep("lfT0"), Dep("lfT1")]
                lf_tok = kb.sb("lf_tok", [128, 2, 4, 128], F32); lftok_d = [Dep("lftok0"), Dep("lftok1")]
                epos = kb.sb("epos", [128, 2, 512], F32); eneg = kb.sb("eneg", [128, 2, 512], F32)
                e_d = [Dep("e0"), Dep("e1")]
                ftmp = kb.sb("ftmp", [128, 2, 4], F32); ftmp_d = [Dep("ftmp0"), Dep("ftmp1")]
                sgT1 = [kb.sb("sgT1", [128, NTOK], BF16) for _ in range(2)]
                qtT = [kb.sb("qtT", [128, 2, NTOK], BF16) for _ in range(2)]
                ktT = [kb.sb("ktT", [128, 2, NTOK], BF16) for _ in range(2)]
                kt_tok = [kb.sb("kt_tok", [128, NT, 2, 128], BF16) for _ in range(2)]
                vtok = [kb.sb("vtok", [128, NT, 128], BF16) for _ in range(2)]
                esc = [kb.sb("esc", [128, NT, 2, 2], F32) for _ in range(2)]
                elast = [kb.sb("elast", [128, NT, 2], F32) for _ in range(2)]
                sgT1_d = [Dep("sgT1a"), Dep("sgT1b")]
                qtT_d = [Dep("qtTa"), Dep("qtTb")]
                ktT_d = [Dep("ktTa"), Dep("ktTb")]
                kttok_d = [[Dep("kttoka0"), Dep("kttoka1")], [Dep("kttokb0"), Dep("kttokb1")]]
                vtok_d = [[Dep("vtoka0"), Dep("vtoka1")], [Dep("vtokb0"), Dep("vtokb1")]]
                esc_d = [Dep("esca"), Dep("escb")]
                NCH = 10
                AT_bf = kb.sb("AT_bf", [128, NCH, 128], BF16); AT_d = [Dep(f"AT{i}") for i in range(NCH)]
                St = kb.sb("St", [128, NCH, 128], F32); St_d = [Dep(f"S{i}") for i in range(NCH)]
                St_s = [kb.dsem(f"S{i}") for i in range(NCH)]
                Sbf = kb.sb("Sbf", [128, NCH, 128], BF16); Sbf_d = [Dep(f"Sbf{i}") for i in range(NCH)]
                sstage = kb.sb("sstage", [128, 4, 128], F32); sstage_d = [Dep(f"sst{i}") for i in range(4)]
                sstage_s = [kb.dsem(f"sst{i}") for i in range(4)]
                o_acc = kb.sb("o_acc", [128, NT, 128], F32); oacc_d = [Dep(f"oacc{i}") for i in range(NT)]
                oss = kb.sb("oss", [128, NT], F32); oss_d = [Dep("oss0"), Dep("oss1")]
                kb.es = es

                pcount = [0]

                def stageA(h, c):
                    hp = h % 2
                    par = c % 2
                    W = Wsl[hp]
                    Wd = Wsl_d[hp]
                    tok0 = c * 512
                    hds = [hT_d[c * 4 + q] for q in range(4)]

                    def proj_fm(col0):
                        pcount[0] += 1
                        bk = pcount[0] % 2
                        for k in range(8):
                            kb.op(PE, lambda: T.matmul(
                                banks[bk][:, :], lhsT=W[:, k, col0:col0 + 128],
                                rhs=hT[:, k, tok0:tok0 + 512], start=(k == 0), stop=(k == 7)),
                                reads=hds + [Wd], writes=[bdeps[bk]], signal=(k == 7))
                        return bk
                    bk = proj_fm(0)
                    kb.op(ACT, lambda: S.activation(out=sqT[par][:], in_=banks[bk][:, :], func=AF.Silu),
                          reads=[bdeps[bk]], writes=[sqT_d[par]])
                    yield
                    bk = proj_fm(512)
                    kb.op(ACT, lambda: S.activation(out=sgT1[hp][:, tok0:tok0 + 512], in_=banks[bk][:, :], func=AF.Silu),
                          reads=[bdeps[bk]], writes=[sgT1_d[hp]])
                    yield
                    for d in range(2):
                        bk = proj_fm(128 + d * 128)
                        kb.op(ACT, lambda: S.activation(out=snT[par][:, d, :], in_=banks[bk][:, :],
                                                        func=AF.Sigmoid, scale=-1.0),
                              reads=[bdeps[bk]], writes=[snT_d[par]])
                        yield
                    for d in range(2):
                        kb.op(ACT, lambda: S.activation(out=lfT[par][:, d, :], in_=snT[par][:, d, :],
                                                        func=AF.Ln, scale=lbm1[:, d, h:h + 1], bias=1.0),
                              reads=[snT_d[par], lb_d], writes=[lfT_d[par]])
                    pcount[0] += 1
                    bk = pcount[0] % 2
                    for j in range(4):
                        for k in range(8):
                            kb.op(PE, lambda: T.matmul(
                                banks[bk][:, j * 128:(j + 1) * 128],
                                lhsT=hT[:, k, tok0 + j * 128:tok0 + (j + 1) * 128],
                                rhs=W[:, k, 384:512], start=(k == 0), stop=(k == 7)),
                                reads=hds + [Wd], writes=[bdeps[bk]], signal=(k == 7))
                    kb.op(DVE, lambda: V.tensor_copy(
                        out=vtok[hp][:, c * 4:c * 4 + 4, :], in_=banks[bk][:, :].rearrange("p (j d) -> p j d", d=128)),
                        reads=[bdeps[bk]], writes=[vtok_d[hp][min(c, 1)]])
                    yield

                def stageB1(h, c):
                    hp = h % 2
                    par = c % 2
                    tok0 = c * 512
                    for d in range(2):
                        for j in range(4):
                            kb.op(PE, lambda: T.transpose(
                                out=banks[3 + d][:, j * 128:(j + 1) * 128], in_=lfT[par][:, d, j * 128:(j + 1) * 128],
                                identity=ident_f[:]), reads=[lfT_d[par], cd], writes=[bdeps[3 + d]], signal=(j == 3))
                    for d in range(2):
                        kb.op(DVE, lambda: V.tensor_copy(
                            out=lf_tok[:, d, :, :], in_=banks[3 + d][:, :].rearrange("p (j d) -> p j d", d=128)),
                            reads=[bdeps[3 + d]], writes=[lftok_d[d]])
                    yield
                    for d in range(2):
                        for j in range(4):
                            kb.op(PE, lambda: T.matmul(
                                banks[3 + d][:, j * 128:(j + 1) * 128], lhsT=lf_tok[:, d, j, :],
                                rhs=wc[:, d, 0:128], start=True, stop=True),
                                reads=[lftok_d[d], cd], writes=[bdeps[3 + d]], signal=(j == 3))
                    for d in range(2):
                        kb.op(ACT, lambda: S.activation(out=epos[:, d, :], in_=banks[3 + d][:, :], func=AF.Exp),
                              reads=[bdeps[3 + d]], writes=[e_d[d]])
                        kb.op(ACT, lambda: S.activation(out=eneg[:, d, :], in_=banks[3 + d][:, :], func=AF.Exp,
                                                        scale=-1.0),
                              reads=[bdeps[3 + d]], writes=[e_d[d]])
                    yield
                    for d in range(2):
                        c0_, c1_ = (0, 127) if d == 0 else (127, 0)
                        snc = snT[par][:, d, :].rearrange("p (j t) -> p j t", t=128)
                        kb.op(DVE, lambda: V.tensor_scalar(out=ftmp[:, d, :], in0=snc[:, :, c0_],
                                                           scalar1=lbm1[:, d, h:h + 1],
                                                           scalar2=1.0, op0=ALU.mult, op1=ALU.add),
                              reads=[snT_d[par], lb_d], writes=[ftmp_d[d]])
                        env = eneg[:, d, :].rearrange("p (j t) -> p j t", t=128)
                        epv = epos[:, d, :].rearrange("p (j t) -> p j t", t=128)
                        kb.op(DVE, lambda: V.tensor_tensor(out=esc[hp][:, c * 4:c * 4 + 4, d, 0], in0=env[:, :, c0_],
                                                           in1=ftmp[:, d, :], op=ALU.mult),
                              reads=[e_d[d], ftmp_d[d]], writes=[esc_d[hp]])
                        kb.op(DVE, lambda: V.tensor_copy(out=esc[hp][:, c * 4:c * 4 + 4, d, 1], in_=epv[:, :, c1_]),
                              reads=[e_d[d]], writes=[esc_d[hp]])
                        if c >= 1:
                            kb.op(DVE, lambda: V.tensor_tensor(
                                out=esc[hp][:, c * 4:c * 4 + 4, d, 0], in0=esc[hp][:, c * 4:c * 4 + 4, d, 0],
                                in1=rf_t[:, (c - 1) * 4:(c - 1) * 4 + 4, d], op=ALU.mult),
                                reads=[esc_d[hp], cd], writes=[esc_d[hp]])
                        kb.op(DVE, lambda: V.tensor_tensor(
                            out=qtT[hp][:, d, tok0:tok0 + 512], in0=sqT[par][:], in1=epos[:, d, :], op=ALU.mult),
                            reads=[sqT_d[par], e_d[d]], writes=[qtT_d[hp]])
                        kb.op(DVE, lambda: V.scalar_tensor_tensor(
                            out=ktT[hp][:, d, tok0:tok0 + 512], in0=snT[par][:, d, :], scalar=oml[:, d, h:h + 1],
                            in1=eneg[:, d, :], op0=ALU.mult, op1=ALU.mult),
                            reads=[snT_d[par], e_d[d], lb_d], writes=[ktT_d[hp]])
                    if c == 2:
                        kb.op(DVE, lambda: V.tensor_tensor(out=elast[hp][:], in0=esc[hp][:, :, :, 0],
                                                           in1=esc[hp][:, :, :, 1], op=ALU.mult),
                              reads=[esc_d[hp]], writes=[esc_d[hp]])
                    yield

                def stageB2(h, c):
                    hp = h % 2
                    tok0 = c * 512
                    for d in range(2):
                        bb = banks[3 + d].bitcast(BF16)
                        for j in range(4):
                            kb.op(PE, lambda: T.transpose(
                                out=bb[:, j * 128:(j + 1) * 128],
                                in_=ktT[hp][:, d, tok0 + j * 128:tok0 + (j + 1) * 128], identity=ident_b[:]),
                                reads=[ktT_d[hp], cd2], writes=[bdeps[3 + d]], signal=(j == 3))
                    for d in range(2):
                        bb = banks[3 + d].bitcast(BF16)
                        kb.op(DVE, lambda: V.tensor_copy(
                            out=kt_tok[hp][:, c * 4:c * 4 + 4, d, :],
                            in_=bb[:, 0:512].rearrange("p (j d) -> p j d", d=128)),
                            reads=[bdeps[3 + d]], writes=[kttok_d[hp][min(c, 1)]])
                    yield

                import itertools

                def m1_stream(h):
                    return itertools.chain(stageA(h, 0), stageA(h, 1), stageB1(h, 0), stageA(h, 2), stageB2(h, 0),
                                           stageB1(h, 1), stageB2(h, 1), stageB1(h, 2), stageB2(h, 2))

                def pull(st, n):
                    for _ in range(n):
                        try:
                            next(st)
                        except StopIteration:
                            return

                def m2_setup(h):
                    chains = []
                    for si, (t0, nt, is_s) in enumerate(SEQS):
                        for d in range(2):
                            ci = si * 2 + d
                            if is_s:
                                kb.dma(SP, St_s[ci], St[:, ci, :], state0[d, h, :, :], writes=[St_d[ci]])
                            else:
                                kb.op(POOL, lambda: G.memset(St[:, ci, :], 0.0), writes=[St_d[ci]])
                            order = list(range(nt)) if d == 0 else list(range(nt - 1, -1, -1))
                            chains.append((ci, si, d, [t0 + x for x in order]))
                    return chains

                wcount = [0]
                sgcount = [0]

                def m2_round(h, chains, step, touched, mid_hook):
                    hp = h % 2
                    act = [ch for ch in chains if step < len(ch[3])]
                    waves = [act[i:i + 4] for i in range(0, len(act), 4)]
                    for wi, wave in enumerate(waves):
                        wcount[0] += 1
                        bP = 7 if wcount[0] % 2 == 0 else 2
                        for sl, (ci, si, d, tiles) in enumerate(wave):
                            tl = tiles[step]
                            ts_ = slice(tl * 128, (tl + 1) * 128)
                            cs_ = slice(sl * 128, (sl + 1) * 128)
                            kb.op(PE, lambda: T.matmul(banks[5][:, cs_], lhsT=ktT[hp][:, d, ts_], rhs=qtT[hp][:, d, ts_],
                                                       start=True, stop=True),
                                  reads=[ktT_d[hp], qtT_d[hp]], writes=[bdeps[5]], signal=(sl == len(wave) - 1))
                        for sl, (ci, si, d, tiles) in enumerate(wave):
                            tl = tiles[step]
                            cs_ = slice(sl * 128, (sl + 1) * 128)
                            kb.op(PE, lambda: T.matmul(banks[bP][:, cs_], lhsT=kt_tok[hp][:, tl, d, :],
                                                       rhs=vtok[hp][:, tl, :], start=True, stop=True),
                                  reads=[kttok_d[hp][0 if tl < 4 else 1], vtok_d[hp][0 if tl < 4 else 1]], writes=[bdeps[bP]],
                                  signal=(sl == len(wave) - 1))
                        for sl, (ci, si, d, tiles) in enumerate(wave):
                            tl = tiles[step]
                            cs_ = slice(sl * 128, (sl + 1) * 128)
                            kb.op(ACT, lambda: S.activation(out=Sbf[:, ci, :], in_=St[:, ci, :], func=AF.Copy,
                                                            scale=esc[hp][:, tl, d, 0:1]),
                                  reads=[St_d[ci], esc_d[hp]], writes=[Sbf_d[ci]])
                            kb.op(DVE, lambda: V.tensor_tensor(out=AT_bf[:, ci, :], in0=banks[5][:, cs_],
                                                               in1=maskt[:, d, :], op=ALU.mult),
                                  reads=[bdeps[5], cd], writes=[AT_d[ci]])
                        for sl, (ci, si, d, tiles) in enumerate(wave):
                            tl = tiles[step]
                            kb.op(DVE, lambda: V.tensor_scalar(out=St[:, ci, :], in0=St[:, ci, :],
                                                               scalar1=elast[hp][:, tl, d:d + 1], scalar2=None,
                                                               op0=ALU.mult),
                                  reads=[St_d[ci], esc_d[hp]], writes=[St_d[ci]])
                        if mid_hook is not None:
                            mid_hook(step, wi, False)
                        for sl, (ci, si, d, tiles) in enumerate(wave):
                            tl = tiles[step]
                            ts_ = slice(tl * 128, (tl + 1) * 128)
                            cs_ = slice(sl * 128, (sl + 1) * 128)
                            kb.op(PE, lambda: T.matmul(banks[6][:, cs_], lhsT=AT_bf[:, ci, :], rhs=vtok[hp][:, tl, :],
                                                       start=True, stop=False, skip_group_check=True),
                                  reads=[AT_d[ci], vtok_d[hp][0 if tl < 4 else 1]], writes=[bdeps[6]], signal=False)
                            kb.op(PE, lambda: T.matmul(banks[6][:, cs_], lhsT=qtT[hp][:, d, ts_], rhs=Sbf[:, ci, :],
                                                       start=False, stop=True, skip_group_check=True),
                                  reads=[qtT_d[hp], Sbf_d[ci]], writes=[bdeps[6]], signal=(sl == len(wave) - 1))
                        for sl, (ci, si, d, tiles) in enumerate(wave):
                            tl = tiles[step]
                            cs_ = slice(sl * 128, (sl + 1) * 128)
                            if not touched[tl]:
                                touched[tl] = True
                                kb.op(DVE, lambda: V.tensor_copy(out=o_acc[:, tl, :], in_=banks[6][:, cs_]),
                                      reads=[bdeps[6]], writes=[oacc_d[tl]])
                            else:
                                kb.op(DVE, lambda: V.tensor_tensor(out=o_acc[:, tl, :], in0=banks[6][:, cs_],
                                                                   in1=o_acc[:, tl, :], op=ALU.add),
                                      reads=[bdeps[6], oacc_d[tl]], writes=[oacc_d[tl]])
                            kb.op(DVE, lambda: V.scalar_tensor_tensor(
                                out=St[:, ci, :], in0=banks[bP][:, cs_], scalar=esc[hp][:, tl, d, 1:2],
                                in1=St[:, ci, :], op0=ALU.mult, op1=ALU.add),
                                reads=[bdeps[bP], St_d[ci], esc_d[hp]], writes=[St_d[ci]])
                            if SEQS[si][2]:
                                lt = tl - SEQS[si][0]
                                if (d == 0 and lt % 2 == 1) or (d == 1 and lt % 2 == 0):
                                    sgcount[0] += 1
                                    q_ = sgcount[0] % 4
                                    kb.op(ACT, lambda: S.copy(out=sstage[:, q_, :], in_=St[:, ci, :]),
                                          reads=[St_d[ci]], writes=[sstage_d[q_]])
                                    kb.dma(SP, sstage_s[q_], ns1_out[lt // 2, d, h, :, :], sstage[:, q_, :],
                                           reads=[sstage_d[q_]])
                        if mid_hook is not None:
                            mid_hook(step, wi, True)

                def m3(h, half):
                    hp = h % 2
                    lo, n_ = (0, 4) if half == 0 else (4, 8)
                    t8 = slice(lo, lo + n_)
                    osq = kt_tok[hp][:].rearrange("p t d k -> p (t d k)").bitcast(F32).rearrange(
                        "p (t k) -> p t k", k=128)[:, t8, :]
                    osq_d = kttok_d[hp][half]
                    on_bf = vtok[hp][:, t8, :]
                    on_d = vtok_d[hp][half]
                    oa = o_acc[:, t8, :]
                    oad = oacc_d[lo:lo + n_]
                    ossv = oss[:, t8]
                    kb.op(DVE, lambda: V.tensor_tensor(out=osq, in0=oa, in1=oa, op=ALU.mult),
                          reads=oad, writes=[osq_d])
                    kb.op(DVE, lambda: V.tensor_reduce(out=ossv, in_=osq, axis=AX.X, op=ALU.add),
                          reads=[osq_d], writes=[oss_d[half]])
                    kb.op(DVE, lambda: V.tensor_scalar(out=ossv, in0=ossv, scalar1=1.0 / 128.0,
                                                       scalar2=NORM_EPS, op0=ALU.mult, op1=ALU.add),
                          reads=[oss_d[half]], writes=[oss_d[half]])
                    kb.op(ACT, lambda: S.sqrt(out=ossv, in_=ossv), reads=[oss_d[half]], writes=[oss_d[half]])
                    kb.op(DVE, lambda: V.reciprocal(out=ossv, in_=ossv), reads=[oss_d[half]], writes=[oss_d[half]])
                    kb.op(DVE, lambda: V.tensor_tensor(
                        out=osq, in0=oa, in1=ossv.unsqueeze(2).to_broadcast([128, n_, 128]), op=ALU.mult),
                        reads=oad + [oss_d[half]], writes=[osq_d])
                    kb.op(DVE, lambda: V.tensor_tensor(
                        out=on_bf, in0=osq, in1=rgain[:, :].unsqueeze(1).to_broadcast([128, n_, 128]),
                        op=ALU.mult), reads=[osq_d, cd], writes=[on_d])

                def m3b(h, half):
                    hp = h % 2
                    lo, n_ = (0, 4) if half == 0 else (4, 8)
                    on_bf = vtok[hp][:, lo:lo + n_, :]
                    on_d = vtok_d[hp][half]
                    bkm = 5 if half == 0 else 6
                    bb = banks[bkm].bitcast(BF16)
                    for q in range(n_):
                        kb.op(PE, lambda: T.transpose(out=bb[:, q * 128:(q + 1) * 128], in_=on_bf[:, q, :],
                                                      identity=ident_b[:]),
                              reads=[on_d, cd2], writes=[bdeps[bkm]], signal=(q == n_ - 1))
                    kb.op(DVE, lambda: V.tensor_tensor(
                        out=gT[:, h, lo * 128:(lo + n_) * 128], in0=bb[:, 0:n_ * 128],
                        in1=sgT1[hp][:, lo * 128:(lo + n_) * 128], op=ALU.mult),
                        reads=[bdeps[bkm], sgT1_d[hp]], writes=[gT_d[lo + q] for q in range(n_)])

                pull(m1_stream(0), 1000)
                for h in range(8):
                    chains = m2_setup(h)
                    touched = [False] * NT
                    stream = m1_stream(h + 1) if h + 1 < 8 else iter(())
                    hp_count = [0]

                    def hook(step, wi, end_, h=h, stream=stream, hp_count=hp_count):
                        hp_count[0] += 1
                        pull(stream, 1 if end_ else 2)
                        if end_:
                            return
                        if step == 0 and wi == 0 and h >= 1:
                            m3b(h - 1, 1)
                        if step == 2 and wi == 0:
                            for (ci, si, d, tiles) in chains:
                                if not SEQS[si][2]:
                                    kb.dma(SP, St_s[ci], ns_out[si, d, h, :, :], St[:, ci, :], reads=[St_d[ci]])
                            m3(h, 0)
                        if step == 4 and wi == 0:
                            m3b(h, 0)
                    for r in range(8):
                        m2_round(h, chains, r, touched, hook)
                    pull(stream, 1000)
                    m3(h, 1)
                    if h + 2 < 8:
                        load_w(h % 2, w1_pieces(h + 2))
                    if h == 6:
                        prefetch_wout(rec_w_out)
                m3b(7, 1)
                kb.barrier()
            kb.es = es

        def w0_pieces(g):
            return [(0, 256, attn_w_in[:, g * 256:(g + 1) * 256]),
                    (256, 64, attn_w_in[:, 1024 + g * 64:1024 + (g + 1) * 64]),
                    (320, 64, attn_w_in[:, 1280 + g * 64:1280 + (g + 1) * 64]),
                    (384, 256, attn_w_in[:, 1536 + g * 256:1536 + (g + 1) * 256])]

        with contextlib.ExitStack() as es0:
            kb.es = es0
            qkv_all = [kb.sb("qkv_all", [128, 8, 384], F32), None, kb.sb("qkv_allS", [128, 8, 384], F32)]
            qkvA_d = [[Dep(f"qkvA{p}{i}") for i in range(8)] for p in range(3)]
            qkvA_s = [kb.dsem("qkvA0"), kb.dsem("qkvA1"), kb.dsem("qkvA2")]
            sq = [kb.sb("sq", [128, 320], F32) for _ in range(2)]
            sq_d = [Dep("sq0"), Dep("sq1")]
            ss_all = [kb.sb("ss_all", [128, 8, 5], F32) for _ in range(3)]
            ss_d = [Dep("ss0"), Dep("ss1"), Dep("ss2")]
            kout = [kb.sb("kout", [128, 8, 64], F32), None, kb.sb("koutS", [128, 8, 64], F32)]
            kout_d = [Dep("kout0"), None, Dep("kout2")]
            kout_s = [kb.dsem("kout0"), None, kb.dsem("kout2")]
            rt1 = kb.sb("rt1", [128, 8, 320], F32)
            rt2 = kb.sb("rt2", [128, 8, 320], F32)
            rt_d = Dep("rt")
            q_bf_all = [kb.sb("q_bf_all", [128, 8, 384], BF16), None, kb.sb("q_bf_allS", [128, 8, 384], BF16)]
            rope = kb.sb("rope", [128, 2, 8, 64], F32)
            rope_d = Dep("rope")
            rope_s = kb.dsem("rope")
            qbf_d = [Dep("qbf0"), Dep("qbf1"), Dep("qbf2")]
            qT4 = kb.sb("qT4", [128, 8, 4, 128], BF16)
            qT4_d = Dep("qT4")
            kT = kb.sb("kT", [128, 1280], BF16)
            kT_d = Dep("kT")
            vaug = kb.sb("vaug", [128, 10, 65], BF16)
            vaug_d = Dep("vaug")
            sgT = kb.sb("sgT", [128, 2, 1024], BF16)
            sgT_d = Dep("sgT")
            PT = [kb.sb("PT", [128, 10, 512], BF16) for _ in range(2)]
            PT_d = [Dep("PT0"), Dep("PT1")]
            ckv = kb.sb("ckv", [128, 2, 2, 64], F32)
            ckv_d = Dep("ckv")
            ckv_s = kb.dsem("ckv")
            ck_bf = kb.sb("ck_bf", [128, 2, 128], BF16)
            ckbf_d = Dep("ckbf")
            rinv4 = kb.sb("rinv4", [128, 4], F32)
            rinv_d = Dep("rinv4")
            on_bf = kb.sb("on_bf", [128, 256], BF16)
            on_d = Dep("on")
            xt = [kb.sb("xt", [128, D], F32) for _ in range(2)]
            kb.es = es

            kb.dma(SP, rope_s, rope[:], c_rope, writes=[rope_d])
            kb.op(POOL, lambda: G.memset(vaug[:], 1.0), writes=[vaug_d])
            kb.op(POOL, lambda: G.memset(qT4[:], 0.0), writes=[qT4_d])
            load_w(0, w0_pieces(0))
            tcount = 0
            qtcount = 0
            tpcount = 0
            def p123(g, si, par):
                for ti in range((PB if si < 0 else SEQS[si])[1]):
                    p1_tile(g, si, par, ti)
                p23(g, si, par)

            def p1_tile(g, si, par, ti):
                nonlocal tcount
                t0, nt, is_s = PB if si < 0 else SEQS[si]
                W = Wsl[g % 2]
                Wd = Wsl_d[g % 2]
                if True:
                    tile = t0 + ti
                    tcount += 1
                    bk = tcount % 2
                    p_ = tcount % 2
                    for k in range(8):
                        kb.op(PE, lambda: T.matmul(
                            banks[bk][:, 0:384], lhsT=hT[:, k, tile * 128:(tile + 1) * 128],
                            rhs=W[:, k, 0:384], start=(k == 0), stop=(k == 7)),
                            reads=[hT_d[tile], Wd], writes=[bdeps[bk]], signal=(k == 7))
                    kb.op(DVE, lambda: V.tensor_copy(out=qkv_all[par][:, ti, :], in_=banks[bk][:, 0:384]),
                          reads=[bdeps[bk]], writes=[qkvA_d[par][ti]])
                    kb.op(POOL, lambda: G.tensor_tensor(out=sq[p_][:], in0=qkv_all[par][:, ti, 0:320],
                                                        in1=qkv_all[par][:, ti, 0:320], op=ALU.mult),
                          reads=[qkvA_d[par][ti]], writes=[sq_d[p_]])
                    kb.op(DVE, lambda: V.tensor_reduce(out=ss_all[par][:, ti, :],
                                                       in_=sq[p_][:].rearrange("p (h d) -> p h d", d=64),
                                                       axis=AX.X, op=ALU.add), reads=[sq_d[p_]], writes=[ss_d[par]])

            def p23(g, si, par):
                t0, nt, is_s = PB if si < 0 else SEQS[si]
                koff = 2 if is_s else 0
                qds = qkvA_d[par][0:nt]
                if is_s:
                    kb.dma(SP, qkvA_s[par], nv1_out.rearrange("(t p) g d -> p t g d", p=128)[:, :, g, :],
                           qkv_all[par][:, 0:nt, 320:384], reads=qds)
                if not is_s:
                    kb.dma(SP, qkvA_s[par], nv_out.rearrange("s (t p) g d -> p (s t) g d", p=128)[:, :, g, :],
                           qkv_all[par][:, 0:nt, 320:384], reads=qds)
                ssv = ss_all[par][:, 0:nt, :]
                kb.op(DVE, lambda: V.tensor_scalar(out=ssv, in0=ssv, scalar1=64.0 * NORM_EPS,
                                                   scalar2=None, op0=ALU.add), reads=[ss_d[par]], writes=[ss_d[par]])
                kb.op(ACT, lambda: S.sqrt(out=ssv, in_=ssv), reads=[ss_d[par]], writes=[ss_d[par]])
                kb.op(DVE, lambda: V.reciprocal(out=ssv, in_=ssv), reads=[ss_d[par]], writes=[ss_d[par]])
                qk4 = qkv_all[par][:, 0:nt, 0:320].rearrange("p t (h d) -> p t h d", d=64)
                kb.op(DVE, lambda: V.tensor_tensor(
                    out=qk4, in0=qk4, in1=ssv.unsqueeze(3).to_broadcast([128, nt, 5, 64]), op=ALU.mult),
                    reads=qds + [ss_d[par]], writes=qds)
                g5 = gain5[:].unsqueeze(1).to_broadcast([128, nt, 5, 64])
                qb4 = q_bf_all[par][:, 0:nt, 0:320].rearrange("p t (h d) -> p t h d", d=64)
                if not is_s:
                    kb.op(DVE, lambda: V.tensor_tensor(out=qb4, in0=qk4, in1=g5, op=ALU.mult),
                          reads=qds + [cd2], writes=[qbf_d[par]])
                    kb.op(POOL, lambda: G.tensor_tensor(
                        out=kout[par][:, 0:nt, :], in0=qkv_all[par][:, 0:nt, 256:320],
                        in1=gain5[:, 4, :].unsqueeze(1).to_broadcast([128, nt, 64]), op=ALU.mult),
                        reads=qds + [cd2], writes=[kout_d[par]])
                    kb.dma(SP, kout_s[par], nk_out.rearrange("s (t p) g d -> p (s t) g d", p=128)[:, :, g, :],
                           kout[par][:, 0:nt, :], reads=[kout_d[par]])
                else:
                    kb.op(DVE, lambda: V.tensor_tensor(out=qk4, in0=qk4, in1=g5, op=ALU.mult),
                          reads=qds + [cd2], writes=qds)
                    r14 = rt1[:, 0:nt, :].rearrange("p t (h d) -> p t h d", d=64)
                    kb.op(DVE, lambda: V.tensor_tensor(
                        out=r14, in0=qk4, in1=rope[:, 0, 0:nt, :].unsqueeze(2).to_broadcast([128, nt, 5, 64]),
                        op=ALU.mult), reads=qds + [rope_d], writes=[rt_d])
                    qv = qkv_all[par][:, 0:nt, 0:320].rearrange("p t (h a b c) -> p t h a b c", a=2, b=2, c=16)
                    r2v = rt2[:, 0:nt, :].rearrange("p t (h a b c) -> p t h a b c", a=2, b=2, c=16)
                    snv = rope[:, 1, 0:nt, :].rearrange("p t (a b c) -> p t a b c", a=2, b=2)
                    for a in range(2):
                        for b in range(2):
                            eng_, h_ = (POOL, G) if a == 0 else (DVE, V)
                            kb.op(eng_, lambda: h_.tensor_tensor(
                                out=r2v[:, :, :, a, b, :], in0=qv[:, :, :, a, 1 - b, :],
                                in1=snv[:, :, a, b, :].unsqueeze(2).to_broadcast([128, nt, 5, 16]), op=ALU.mult),
                                reads=qds + [rope_d], writes=[rt_d])
                    kb.op(DVE, lambda: V.tensor_tensor(out=q_bf_all[par][:, 0:nt, 0:320], in0=rt1[:, 0:nt, :],
                                                       in1=rt2[:, 0:nt, :], op=ALU.add),
                          reads=[rt_d], writes=[qbf_d[par]])
                    kb.op(POOL, lambda: G.tensor_tensor(out=kout[par][:, 0:nt, :], in0=rt1[:, 0:nt, 256:320],
                                                        in1=rt2[:, 0:nt, 256:320], op=ALU.add),
                          reads=[rt_d], writes=[kout_d[par]])
                    kb.dma(SP, kout_s[par], nk1_out.rearrange("(t p) g d -> p t g d", p=128)[:, :, g, :],
                           kout[par][:, 0:nt, :], reads=[kout_d[par]])
                kb.op(POOL, lambda: G.tensor_copy(out=q_bf_all[par][:, 0:nt, 320:384], in_=q_bf_all[par][:, 0:nt, 256:320]),
                      reads=[qbf_d[par]], writes=[qbf_d[par]])

            def rest(g, si, par, mid_hook=None, tile_hook=None):
                nonlocal qtcount, tpcount
                t0, nt, is_s = PB if si < 0 else SEQS[si]
                W = Wsl[g % 2]
                Wd = Wsl_d[g % 2]
                koff = 2 if is_s else 0
                nkb = nt + koff
                if is_s:
                    kb.dma(SP, ckv_s, ckv[:, 0, :, :], cache_k[:, g, :].rearrange("(b p) d -> p b d", p=128),
                           writes=[ckv_d])
                    kb.dma(SP, ckv_s, ckv[:, 1, :, :], cache_v[:, g, :].rearrange("(b p) d -> p b d", p=128),
                           writes=[ckv_d])
                    kb.op(POOL, lambda: G.tensor_copy(out=ck_bf[:, :, 0:64], in_=ckv[:, 0, :, :]),
                          reads=[ckv_d], writes=[ckbf_d])
                    kb.op(POOL, lambda: G.tensor_copy(out=ck_bf[:, :, 64:128], in_=ckv[:, 0, :, :]),
                          reads=[ckv_d], writes=[ckbf_d])
                    kb.op(POOL, lambda: G.tensor_copy(out=vaug[:, 0:2, 0:64], in_=ckv[:, 1, :, :]),
                          reads=[ckv_d], writes=[vaug_d])
                    bkb = banks[3].bitcast(BF16)
                    for b2 in range(2):
                        kb.op(PE, lambda b2=b2: T.transpose(out=bkb[:, b2 * 128:(b2 + 1) * 128],
                                                            in_=ck_bf[:, b2, :], identity=ident_b[:]),
                              reads=[ckbf_d, cd2], writes=[bdeps[3]], signal=(b2 == 1))
                    kb.op(DVE, lambda: V.tensor_copy(out=kT[:, 0:256], in_=bkb[:, 0:256]),
                          reads=[bdeps[3]], writes=[kT_d])
                kb.op(POOL, lambda: G.tensor_copy(out=vaug[:, koff:koff + nt, 0:64], in_=qkv_all[par][:, 0:nt, 320:384]),
                      reads=qkvA_d[par][0:nt], writes=[vaug_d])
                for tp in range(nt // 2):
                    tpcount += 1
                    bkn = 2
                    bkb = banks[bkn].bitcast(BF16)
                    for u in range(2):
                        for j in range(3):
                            kb.op(PE, lambda: T.transpose(
                                out=bkb[:, j * 256 + u * 128:j * 256 + (u + 1) * 128],
                                in_=q_bf_all[par][:, 2 * tp + u, j * 128:(j + 1) * 128], identity=ident_b[:]),
                                reads=[qbf_d[par], cd2], writes=[bdeps[bkn]], signal=(u == 1 and j == 2))
                    kb.op(DVE, lambda: V.tensor_copy(
                        out=qT4[0:64, 2 * tp:2 * tp + 2, 0:4:2, :],
                        in_=bkb[0:64, 0:512].rearrange("p (h u t) -> p u h t", h=2, u=2)),
                        reads=[bdeps[bkn]], writes=[qT4_d])
                    kb.op(DVE, lambda: V.tensor_copy(
                        out=qT4[64:128, 2 * tp:2 * tp + 2, 1:4:2, :],
                        in_=bkb[64:128, 0:512].rearrange("p (h u t) -> p u h t", h=2, u=2)),
                        reads=[bdeps[bkn]], writes=[qT4_d])
                    kb.op(DVE, lambda: V.tensor_copy(
                        out=kT[:, (koff + 2 * tp) * 128:(koff + 2 * tp + 2) * 128], in_=bkb[:, 512:768]),
                        reads=[bdeps[bkn]], writes=[kT_d])
                ntok = nt * 128
                for c0 in range(0, ntok, 512):
                    n = min(512, ntok - c0)
                    for j in range(2):
                        bk = 3
                        for k in range(8):
                            kb.op(PE, lambda k=k, j=j, c0=c0, n=n: T.matmul(
                                banks[3][:, 0:n], lhsT=W[:, k, 384 + j * 128:384 + (j + 1) * 128],
                                rhs=hT[:, k, t0 * 128 + c0:t0 * 128 + c0 + n], start=(k == 0), stop=(k == 7)),
                                reads=[hT_d[t0 + c0 // 128 + q] for q in range(n // 128)] + [Wd],
                                writes=[bdeps[3]], signal=(k == 7))
                        kb.op(ACT, lambda j=j, c0=c0, n=n: S.activation(
                            out=sgT[:, j, c0:c0 + n], in_=banks[3][:, 0:n], func=AF.Silu),
                            reads=[bdeps[3]], writes=[sgT_d])
                if mid_hook is not None:
                    mid_hook()
                def kbs(ti):
                    if si < 0:
                        return [2 * (ti // 2), 2 * (ti // 2) + 1]
                    return list(range(nkb))

                def scores(ti, j, kbi, ps_):
                    bk = 4 + (j % 2)
                    kb.op(PE, lambda: T.matmul(
                        banks[bk][:, :], lhsT=kT[:, kbi * 128:(kbi + 1) * 128],
                        rhs=qT4[:, ti, :, :].rearrange("p h t -> p (h t)"),
                        start=True, stop=True), reads=[kT_d, qT4_d], writes=[bdeps[bk]])
                    kb.op(ACT, lambda: S.activation(
                        out=PT[ps_][:, j, :], in_=banks[bk][:, :], func=AF.Exp, scale=0.125,
                        bias=(bias_t[:, ti, kbi:kbi + 1] if is_s else 0.0)),
                        reads=[bdeps[bk], cd], writes=[PT_d[ps_]])

                qbase = qtcount
                for j, kbi in enumerate(kbs(0)):
                    scores(0, j, kbi, (qbase + 1) % 2)
                for ti in range(nt):
                    tile = t0 + ti
                    if tile_hook is not None:
                        tile_hook(ti)
                    qtcount += 1
                    ps_ = qtcount % 2
                    bo = 6 + (qtcount % 2)
                    kl = kbs(ti)
                    kn = kbs(ti + 1) if ti + 1 < nt else []
                    for j, kbi in enumerate(kl):
                        if j < len(kn):
                            scores(ti + 1, j, kn[j], (qtcount + 1) % 2)
                        for h in range(4):
                            kb.op(PE, lambda: T.matmul(
                                banks[bo][:, h * 65:(h + 1) * 65], lhsT=PT[ps_][:, j, h * 128:(h + 1) * 128],
                                rhs=vaug[:, kbi, :], start=(j == 0 and h == 0), stop=(j == len(kl) - 1),
                                skip_group_check=True),
                                reads=[PT_d[ps_], vaug_d], writes=[bdeps[bo]],
                                signal=(h == 3 and j == len(kl) - 1))
                    ov = banks[bo][:, 0:260].rearrange("p (h d) -> p h d", d=65)
                    kb.op(DVE, lambda: V.reciprocal(out=rinv4[:], in_=ov[:, :, 64]),
                          reads=[bdeps[bo]], writes=[rinv_d])
                    kb.op(DVE, lambda: V.tensor_tensor(
                        out=on_bf[:].rearrange("p (h d) -> p h d", d=64), in0=ov[:, :, 0:64],
                        in1=rinv4[:, :].unsqueeze(2).to_broadcast([128, 4, 64]), op=ALU.mult),
                        reads=[bdeps[bo], rinv_d], writes=[on_d])
                    bkb7 = banks[2].bitcast(BF16)
                    for j in range(2):
                        kb.op(PE, lambda: T.transpose(out=bkb7[:, 512 + j * 128:512 + (j + 1) * 128],
                                                      in_=on_bf[:, j * 128:(j + 1) * 128],
                                                      identity=ident_b[:]),
                              reads=[on_d, cd2], writes=[bdeps[2]], signal=(j == 1))
                    kb.op(DVE, lambda: V.tensor_tensor(
                        out=gT[:, 2 * g:2 * g + 2, tile * 128:(tile + 1) * 128],
                        in0=bkb7[:, 512:768].rearrange("p (j t) -> p j t", j=2),
                        in1=sgT[:, :, ti * 128:(ti + 1) * 128], op=ALU.mult),
                        reads=[bdeps[2], sgT_d], writes=[gT_d[tile]])

            load_w(1, w0_pieces(1))
            for i in range(NT):
                xs = i % 2
                kb.dma(SP, xt_s[xs], xt[xs][:], x_all[i * 128:(i + 1) * 128, :], writes=[xt_d[xs]])
                make_hT(xt[xs], xt_d[xs], i, 0 if i < 4 else 1, 0, (4, 5) if i % 2 == 0 else (6, 7))
                if i >= 1:
                    j = i - 1
                    if j < 4:
                        p1_tile(0, -1, 0, j)
                    else:
                        p1_tile(0, SI_S, 2, j - 4)
            p1_tile(0, SI_S, 2, 7)
            p23(0, -1, 0)
            p23(0, SI_S, 2)
            for g in range(4):
                rest(g, -1, 0)

                def hook(g=g):
                    if g + 1 < 4:
                        p123(g + 1, -1, 0)
                        p123(g + 1, SI_S, 2)
                if g == 3:
                    prefetch_wout(attn_w_out)
                rest(g, SI_S, 2, hook)
                if g + 2 < 4:
                    load_w(g % 2, w0_pieces(g + 2))
            kb.barrier()

        if stg >= 3:
            if n_layers == 2:
                load_w(0, w1_pieces(0))
                load_w(1, w1_pieces(1))
            epilogue(0, attn_w_out, x_all, last=(n_layers == 1))

        if n_layers == 2 and stg >= 4:
            layer1()
            epilogue(1, rec_w_out, x1_dram, last=True)

        kb.finish()
    return nc


def _consts(is_sample_slot):
    ident = np.eye(128, dtype=np.float32)
    t = np.arange(1024)
    rows = (t // 64).astype(np.float32)
    cols = (t % 64).astype(np.float32)
    inv_freq = (1.0 / (np.float32(10000.0) ** (np.arange(0, 32, 2, dtype=np.float32) / np.float32(32)))).astype(np.float32)
    ang_r = rows[:, None] * inv_freq[None, :]
    ang_c = cols[:, None] * inv_freq[None, :]
    ang = np.concatenate([ang_r, ang_r, ang_c, ang_c], axis=-1).astype(np.float32)
    if is_sample_slot:
        cos = np.cos(ang).astype(np.float32)
        sin = np.sin(ang).astype(np.float32)
    else:
        cos = np.ones_like(ang)
        sin = np.zeros_like(ang)
    sgn = np.concatenate([-np.ones(16), np.ones(16), -np.ones(16), np.ones(16)]).astype(np.float32)
    sins = sin * sgn[None, :]
    rope = np.stack([cos, sins], 0).reshape(2, 8, 128, 64).transpose(2, 0, 1, 3).copy()
    mid = 63
    wc = np.zeros((2, 128, 130), np.float32)
    mask = np.zeros((2, 128, 128), np.float32)
    s = np.arange(128)[:, None]
    tt = np.arange(128)[None, :]
    fw = np.where((s > mid) & (s <= tt), 1.0, 0.0) - np.where((s <= mid) & (s > tt), 1.0, 0.0)
    wc[0, :, :128] = fw
    mask[0] = (s <= tt)
    mb = 64
    bw = np.where((s < mb) & (s >= tt), 1.0, 0.0) - np.where((s >= mb) & (s < tt), 1.0, 0.0)
    wc[1, :, :128] = bw
    mask[1] = (s >= tt)
    sel = np.zeros((2, 2, 128), np.float32)
    wc = np.ascontiguousarray(wc.transpose(1, 0, 2))
    mask = np.ascontiguousarray(mask.transpose(1, 0, 2))
    bias = np.zeros((128, 8, 10), np.float32)
    rf = np.ones((128, 8, 2), np.float32)
    if not is_sample_slot:
        NEG = -30000.0
        bias[:, :, 0:2] = NEG
        for ti in range(8):
            for t_ in range(8):
                if t_ // 2 != ti // 2:
                    bias[:, ti, 2 + t_] = NEG
        for lt in range(8):
            if lt % 2 == 0:
                rf[:, lt, 0] = 0.0
            else:
                rf[:, lt, 1] = 0.0
    return dict(c_ident=ident, c_rope=rope.astype(np.float32), c_wc=wc, c_mask=mask, c_sel=sel,
                c_bias=bias, c_rf=rf)


_NC_CACHE = {}


def kernel(x_prompt, x_sample, cache_k, cache_v, state_rec, c, c_ctx,
           ada_w, ada_b, attn_w_in, attn_q_gain, attn_k_gain, attn_w_out,
           rec_w_in, rec_lower_bounds, rec_norm_gain, rec_w_out, ln_gain, ln_bias, _n_layers=2, _stage=99):
    f = lambda a: np.ascontiguousarray(np.asarray(a, dtype=np.float32))
    x_prompt, x_sample, cache_k, cache_v, state_rec = map(f, (x_prompt, x_sample, cache_k, cache_v, state_rec))
    c, c_ctx = f(c), f(c_ctx)
    consts = [_consts(True), _consts(False)]
    shared = dict(ada_w=f(ada_w), ada_b=f(ada_b), attn_w_in=f(attn_w_in)[0], q_gain=f(attn_q_gain),
                  k_gain=f(attn_k_gain), attn_w_out=f(attn_w_out)[0], rec_w_in=f(rec_w_in)[0],
                  lbr=np.ascontiguousarray(f(rec_lower_bounds).reshape(2, 2, 8, 128).transpose(3, 0, 1, 2)),
                  adabT=np.ascontiguousarray(f(ada_b).reshape(2, 24, 128).transpose(2, 0, 1)), rec_gain=f(rec_norm_gain), rec_w_out=f(rec_w_out)[0],
                  ln_gain=f(ln_gain), ln_bias=f(ln_bias))
    in_maps = []
    for core in range(8):
        m = dict(shared)
        if core < 4:
            s = core
            p0 = 2 * core
            xa = np.concatenate([x_prompt[p0:p0 + 2].reshape(512, D), x_sample[s]], axis=0)
            m.update(consts[0])
            m.update(cache_k=np.ascontiguousarray(cache_k[s, 0]), cache_v=np.ascontiguousarray(cache_v[s, 0]),
                     state0=np.ascontiguousarray(state_rec[s, 0]), cond_rows=np.stack([c_ctx, c[s]], 0))
        else:
            k = core - 4
            p0 = 8 + 2 * k
            p1 = 16 + 4 * k
            xa = np.concatenate([x_prompt[p0:p0 + 2].reshape(512, D), x_prompt[p1:p1 + 4].reshape(1024, D)], axis=0)
            m.update(consts[1])
            m.update(cache_k=np.zeros((256, 4, 64), np.float32), cache_v=np.zeros((256, 4, 64), np.float32),
                     state0=np.zeros((2, 8, 128, 128), np.float32), cond_rows=np.stack([c_ctx, c_ctx], 0))
        cr = m.pop("cond_rows")
        m.update(x_all=np.ascontiguousarray(xa),
                 condT=np.ascontiguousarray(cr.reshape(2, 8, 128).transpose(2, 1, 0)))
        in_maps.append(m)
    key = (_n_layers, _stage)
    if key not in _NC_CACHE:
        _NC_CACHE[key] = build_program(_n_layers, _stage)
    nc = _NC_CACHE[key]
    res = run_bass_kernel_spmd(nc, in_maps, core_ids=list(range(8)))
    R = res.results
    y_prompt = np.zeros((32, 256, D), np.float32)
    nk = np.zeros((32, 1, 256, 4, 64), np.float32)
    nv = np.zeros((32, 1, 256, 4, 64), np.float32)
    ns = np.zeros((32, 1, 2, 8, 128, 128), np.float32)
    y_sample = np.zeros((4, 1024, D), np.float32)
    for core in range(8):
        r = R[core]
        p0 = 2 * core if core < 4 else 8 + 2 * (core - 4)
        y_prompt[p0:p0 + 2] = r["y_out"][:512].reshape(2, 256, D)
        nk[p0:p0 + 2, 0] = r["nk_out"]
        nv[p0:p0 + 2, 0] = r["nv_out"]
        ns[p0:p0 + 2, 0] = r["ns_out"]
        if core < 4:
            y_sample[core] = r["y_out"][512:]
        else:
            p1 = 16 + 4 * (core - 4)
            y_prompt[p1:p1 + 4] = r["y_out"][512:].reshape(4, 256, D)
            nk[p1:p1 + 4, 0] = r["nk1_out"].reshape(4, 256, 4, 64)
            nv[p1:p1 + 4, 0] = r["nv1_out"].reshape(4, 256, 4, 64)
            ns[p1:p1 + 4, 0] = r["ns1_out"]
    return (y_prompt, y_sample, nk, nv, ns)
```
